# Optimizing a Trainium2 kernel written in Bass

```python
import math
import jax, jax.numpy as jnp
from jax import lax
import numpy as np

D_MODEL = 1024
BATCH = 8
SEQ = 4096
DEPTH = 2
DEC_BATCH = 16
DEC_SEQ = 4096
PAST_LEN = 128

HEAD_DIM = 64
MIX_WIDTH = D_MODEL
H_A = 4
H_B = 4
H_B_KV = 2
H_C = 4
H_D = 4
DILATED_PATTERNS = ((128, 1), (512, 4), (2048, 16))
T5_BUCKETS = 32
T5_MAX_DIST = 1024
GRID_W = 64
NA_ROWS = 8
NA_COLS = 16
D_Q_LORA = 256
D_KV_LORA = 128
D_NOPE = 64
D_ROPE = 32
D_V = 64
ROPE_THETA = 10000.0
QBLOCK = 128
D_FF = -(-8 * D_MODEL // (3 * 256)) * 256
IN_SIZES = (H_A * HEAD_DIM, H_A * HEAD_DIM, H_A * HEAD_DIM,
            H_B * HEAD_DIM, H_B_KV * HEAD_DIM, H_B_KV * HEAD_DIM,
            H_C * HEAD_DIM, H_C * HEAD_DIM, H_C * HEAD_DIM,
            D_Q_LORA, D_KV_LORA, D_ROPE)
D_IN = sum(IN_SIZES)
N_GROUPS = 4
GROUP_WIDTH = MIX_WIDTH // N_GROUPS
RMS_EPS = 1e-6
NEG_INF = -1e30

kernel_name = "hybrid_parallel_head_group_encoder"


def rms_norm(x, g):
    xf = x.astype(jnp.float32)
    y = xf * lax.rsqrt(jnp.mean(xf * xf, axis=-1, keepdims=True) + RMS_EPS)
    return (y * g.astype(jnp.float32)).astype(x.dtype)


def rope_angles(pos, dim):
    inv = 1.0 / (ROPE_THETA ** (jnp.arange(0, dim, 2, dtype=jnp.float32) / dim))
    return pos.astype(jnp.float32)[:, None] * inv[None, :]


def apply_rope(x, ang):
    x1, x2 = jnp.split(x, 2, axis=-1)
    cos = jnp.cos(ang)[:, None, :].astype(x.dtype)
    sin = jnp.sin(ang)[:, None, :].astype(x.dtype)
    return jnp.concatenate([x1 * cos - x2 * sin, x1 * sin + x2 * cos], axis=-1)


def t5_bucket(rel):
    nb = T5_BUCKETS // 2
    max_exact = nb // 2
    n = jnp.abs(rel)
    n_f = jnp.maximum(n, max_exact).astype(jnp.float32)
    large = max_exact + (jnp.log(n_f / max_exact) / math.log(T5_MAX_DIST / max_exact)
                         * (nb - max_exact)).astype(jnp.int32)
    large = jnp.minimum(large, nb - 1)
    return jnp.where(rel > 0, nb, 0) + jnp.where(n < max_exact, n, large)


def dilated_window_attention(q, k, v, t5_bias):
    B, S, H, Dh = q.shape
    nblk = S // QBLOCK
    scale = Dh ** -0.5
    outs, lses = [], []
    for window, dil in DILATED_PATTERNS:
        half = window // (2 * dil)
        off = jnp.arange(-half, half + 1, dtype=jnp.int32) * dil
        bias = t5_bias[t5_bucket(off)].T.astype(jnp.float32)

        def block(i, off=off, bias=bias):
            qpos = i * QBLOCK + jnp.arange(QBLOCK, dtype=jnp.int32)
            kidx = qpos[:, None] + off[None, :]
            valid = (kidx >= 0) & (kidx < S)
            kidx = jnp.clip(kidx, 0, S - 1)
            qb = lax.dynamic_slice_in_dim(q, i * QBLOCK, QBLOCK, axis=1)
            kg = k[:, kidx]
            vg = v[:, kidx]
            s = jnp.einsum('bqhd,bqkhd->bhqk', qb, kg).astype(jnp.float32) * scale + bias[:, None, :]
            s = jnp.where(valid[None, None], s, NEG_INF)
            m = jnp.max(s, axis=-1, keepdims=True)
            e = jnp.exp(s - m)
            l = jnp.sum(e, axis=-1, keepdims=True)
            o = jnp.einsum('bhqk,bqkhd->bqhd', e.astype(v.dtype), vg).astype(jnp.float32)
            o = o / jnp.transpose(l, (0, 2, 1, 3))
            lse = (m + jnp.log(l))[..., 0]
            return o, lse

        o, lse = lax.map(block, jnp.arange(nblk))
        outs.append(jnp.moveaxis(o, 0, 1).reshape(B, S, H, Dh))
        lses.append(jnp.transpose(lse, (1, 2, 0, 3)).reshape(B, H, S))
    w = jax.nn.softmax(jnp.stack(lses, axis=0), axis=0)
    w = jnp.transpose(w, (0, 1, 3, 2))[..., None]
    return jnp.sum(w * jnp.stack(outs, axis=0), axis=0)


def dense_block_attention(q, k, v):
    B, S, G, R, Dq = q.shape
    Dv = v.shape[-1]
    scale = Dq ** -0.5

    def block(i):
        qb = lax.dynamic_slice_in_dim(q, i * QBLOCK, QBLOCK, axis=1)
        s = jnp.einsum('bqgrd,bkgd->bgrqk', qb, k).astype(jnp.float32) * scale
        p = jax.nn.softmax(s, axis=-1).astype(v.dtype)
        return jnp.einsum('bgrqk,bkgd->bqgrd', p, v)

    o = lax.map(block, jnp.arange(S // QBLOCK))
    return jnp.moveaxis(o, 0, 1).reshape(B, S, G * R, Dv)


def neighborhood_attention(q, k, v, rpb):
    B, S, H, Dh = q.shape
    rows = S // GRID_W
    kr = min(NA_ROWS, rows)
    scale = Dh ** -0.5
    qg = q.reshape(B, rows, GRID_W, H, Dh)
    kg = k.reshape(B, rows, GRID_W, H, Dh)
    vg = v.reshape(B, rows, GRID_W, H, Dh)
    cols = jnp.arange(GRID_W, dtype=jnp.int32)
    cs = jnp.clip(cols - NA_COLS // 2, 0, GRID_W - NA_COLS)
    cidx = cs[:, None] + jnp.arange(NA_COLS, dtype=jnp.int32)[None, :]
    dc = cidx - cols[:, None]

    def row_block(r):
        rs = jnp.clip(r - kr // 2, 0, rows - kr)
        k_rows = lax.dynamic_slice_in_dim(kg, rs, kr, axis=1)
        v_rows = lax.dynamic_slice_in_dim(vg, rs, kr, axis=1)
        k_n = k_rows[:, :, cidx]
        v_n = v_rows[:, :, cidx]
        q_r = lax.dynamic_index_in_dim(qg, r, axis=1, keepdims=False)
        dr = rs + jnp.arange(kr, dtype=jnp.int32) - r
        bias = rpb[:, dr[None, :, None] + NA_ROWS - 1, dc[:, None, :] + NA_COLS - 1]
        s = jnp.einsum('bchd,brckhd->bhcrk', q_r, k_n).astype(jnp.float32) * scale
        s = s + bias.astype(jnp.float32)[None]
        p = jax.nn.softmax(s.reshape(B, H, GRID_W, kr * NA_COLS), axis=-1)
        p = p.reshape(B, H, GRID_W, kr, NA_COLS).astype(v.dtype)
        return jnp.einsum('bhcrk,brckhd->bchd', p, v_n)

    o = lax.map(row_block, jnp.arange(rows))
    return jnp.moveaxis(o, 0, 1).reshape(B, S, H, Dh)


def mixing_layer(h, t5_bias, w_in, b_q_gain, b_k_gain, c_rpb, d_q_gain, d_w_uq,
                 d_kv_gain, d_w_ukv, out_gain, w_out):
    B, S, _ = h.shape
    proj = h @ w_in
    splits = [int(c) for c in np.cumsum(IN_SIZES)[:-1]]
    aq, ak, av, bq, bk, bv, cq, ck, cv, dq, dkv, dkr = jnp.split(proj, splits, axis=-1)

    def heads(t, n):
        return t.reshape(B, S, n, t.shape[-1] // n)

    t = jnp.arange(S, dtype=jnp.int32)

    o_a = dilated_window_attention(heads(aq, H_A), heads(ak, H_A), heads(av, H_A), t5_bias)

    qb = rms_norm(heads(bq, H_B), b_q_gain)
    kb = rms_norm(heads(bk, H_B_KV), b_k_gain)
    ang_r = rope_angles(t // GRID_W, HEAD_DIM // 2)
    ang_c = rope_angles(t % GRID_W, HEAD_DIM // 2)
    hd2 = HEAD_DIM // 2
    qb = jnp.concatenate([apply_rope(qb[..., :hd2], ang_r), apply_rope(qb[..., hd2:], ang_c)], axis=-1)
    kb = jnp.concatenate([apply_rope(kb[..., :hd2], ang_r), apply_rope(kb[..., hd2:], ang_c)], axis=-1)
    qb = qb.reshape(B, S, H_B_KV, H_B // H_B_KV, HEAD_DIM)
    o_b = dense_block_attention(qb, kb, heads(bv, H_B_KV))

    o_c = neighborhood_attention(heads(cq, H_C), heads(ck, H_C), heads(cv, H_C), c_rpb)

    ang_t = rope_angles(t, D_ROPE)
    q_d = (rms_norm(dq, d_q_gain) @ d_w_uq).reshape(B, S, H_D, D_NOPE + D_ROPE)
    q_d = jnp.concatenate([q_d[..., :D_NOPE], apply_rope(q_d[..., D_NOPE:], ang_t)], axis=-1)
    kv_d = (rms_norm(dkv, d_kv_gain) @ d_w_ukv).reshape(B, S, H_D, D_NOPE + D_V)
    k_rope = jnp.broadcast_to(apply_rope(dkr[:, :, None, :], ang_t), (B, S, H_D, D_ROPE))
    k_d = jnp.concatenate([kv_d[..., :D_NOPE], k_rope], axis=-1)
    o_d = dense_block_attention(q_d[:, :, :, None, :], k_d, kv_d[..., D_NOPE:])

    o = jnp.stack([o_a.astype(h.dtype).reshape(B, S, GROUP_WIDTH),
                   o_b.reshape(B, S, GROUP_WIDTH),
                   o_c.reshape(B, S, GROUP_WIDTH),
                   o_d.reshape(B, S, GROUP_WIDTH)], axis=-2)
    o = rms_norm(o, out_gain.reshape(N_GROUPS, GROUP_WIDTH))
    return o.reshape(B, S, MIX_WIDTH) @ w_out


def swiglu(h, w_gate, w_up, w_down):
    return (jax.nn.silu(h @ w_gate) * (h @ w_up)) @ w_down


def trunk(x, t5_bias, norm_mix, w_in, b_q_gain, b_k_gain, c_rpb, d_q_gain, d_w_uq,
          d_kv_gain, d_w_ukv, out_gain, w_out, norm_ffn, w_gate, w_up, w_down, final_norm):
    for l in range(DEPTH):
        h = rms_norm(x, norm_mix[l])
        x = x + mixing_layer(h, t5_bias, w_in[l], b_q_gain[l], b_k_gain[l], c_rpb[l],
                             d_q_gain[l], d_w_uq[l], d_kv_gain[l], d_w_ukv[l],
                             out_gain[l], w_out[l])
        h = rms_norm(x, norm_ffn[l])
        x = x + swiglu(h, w_gate[l], w_up[l], w_down[l])
    return rms_norm(x, final_norm)


def setup_inputs(seed: int = 0) -> dict:
    key = jax.random.key(seed)
    ks = jax.random.split(key, 20)
    f32 = jnp.float32

    def nrm(k, shape, scale):
        return jax.random.normal(k, shape, f32) * scale

    def gain(k, shape):
        return 1.0 + 0.05 * jax.random.normal(k, shape, f32)

    return {
        "x_prompt": nrm(ks[0], (BATCH, SEQ, D_MODEL), 1.0),
        "x_sample": nrm(ks[1], (DEC_BATCH, DEC_SEQ, D_MODEL), 1.0),
        "t5_bias": nrm(ks[2], (T5_BUCKETS, H_A), 0.5),
        "norm_mix": gain(ks[3], (DEPTH, D_MODEL)),
        "w_in": nrm(ks[4], (DEPTH, D_MODEL, D_IN), D_MODEL ** -0.5),
        "b_q_gain": gain(ks[5], (DEPTH, HEAD_DIM)),
        "b_k_gain": gain(ks[6], (DEPTH, HEAD_DIM)),
        "c_rpb": nrm(ks[7], (DEPTH, H_C, 2 * NA_ROWS - 1, 2 * NA_COLS - 1), 0.5),
        "d_q_gain": gain(ks[8], (DEPTH, D_Q_LORA)),
        "d_w_uq": nrm(ks[9], (DEPTH, D_Q_LORA, H_D * (D_NOPE + D_ROPE)), D_Q_LORA ** -0.5),
        "d_kv_gain": gain(ks[10], (DEPTH, D_KV_LORA)),
        "d_w_ukv": nrm(ks[11], (DEPTH, D_KV_LORA, H_D * (D_NOPE + D_V)), D_KV_LORA ** -0.5),
        "out_gain": gain(ks[12], (DEPTH, MIX_WIDTH)),
        "w_out": nrm(ks[13], (DEPTH, MIX_WIDTH, D_MODEL), MIX_WIDTH ** -0.5),
        "norm_ffn": gain(ks[14], (DEPTH, D_MODEL)),
        "w_gate": nrm(ks[15], (DEPTH, D_MODEL, D_FF), D_MODEL ** -0.5),
        "w_up": nrm(ks[16], (DEPTH, D_MODEL, D_FF), D_MODEL ** -0.5),
        "w_down": nrm(ks[17], (DEPTH, D_FF, D_MODEL), D_FF ** -0.5),
        "final_norm": gain(ks[18], (D_MODEL,)),
    }


def reference(x_prompt, x_sample, t5_bias, norm_mix, w_in, b_q_gain, b_k_gain, c_rpb,
              d_q_gain, d_w_uq, d_kv_gain, d_w_ukv, out_gain, w_out, norm_ffn,
              w_gate, w_up, w_down, final_norm):
    y_prompt = trunk(x_prompt, t5_bias, norm_mix, w_in, b_q_gain, b_k_gain, c_rpb, d_q_gain,
                     d_w_uq, d_kv_gain, d_w_ukv, out_gain, w_out, norm_ffn, w_gate, w_up,
                     w_down, final_norm)
    y_sample = trunk(x_sample, t5_bias, norm_mix, w_in, b_q_gain, b_k_gain, c_rpb, d_q_gain,
                     d_w_uq, d_kv_gain, d_w_ukv, out_gain, w_out, norm_ffn, w_gate, w_up,
                     w_down, final_norm)
    return (y_prompt, y_sample)
```

```python
import os
from contextlib import ExitStack
import numpy as np
import ml_dtypes
import concourse.bass as bass
import concourse.mybir as mybir
from concourse.bass_utils import run_bass_kernel_spmd

F32, BF16 = mybir.dt.float32, mybir.dt.bfloat16
AF = mybir.ActivationFunctionType
ALU = mybir.AluOpType
AX = mybir.AxisListType

S = 4096
DM = 1024
DFF = 2816
NCOL = 3008
EPS = 1e-6
STOP = int(os.environ.get('KSTOP', '99'))
KV = int(os.environ.get('KV', '15'))
KDUP = int(os.environ.get('KDUP', '0'))
NCORES = 8
ENG = ('pe', 'act', 'dve', 'pool', 'sp')


class Sched:
    def __init__(self, nc):
        self.nc = nc
        self.streams = {e: [] for e in ENG}
        self.cnt = {e: 0 for e in ENG}
        self.dcnt = {}
        self.lastw = {}
        self.readers = {}
        self.known = {e: {} for e in ENG}
        self.vc = {e: {} for e in ENG}

    def _need(self, eng, tok):
        kind, key, val = tok
        if kind == 'eng':
            if key == 'pe' and eng == 'pe':
                return
            sem = 'E_' + key
        else:
            sem = key
            val = self.dcnt[key]
        if self.known[eng].get(sem, 0) >= val:
            return
        self.known[eng][sem] = val
        self.streams[eng].append(('wait', sem, val))
        if kind == 'eng':
            snap = self.vc[key].get(val)
            if snap is not None:
                kn = self.known[eng]
                for x, c in zip(ENG, snap):
                    if c > kn.get('E_' + x, 0):
                        kn['E_' + x] = c

    def _deps(self, eng, reads, writes):
        for r in reads:
            t = self.lastw.get(r)
            if t is not None:
                self._need(eng, t)
        for w in writes:
            t = self.lastw.get(w)
            if t is not None:
                self._need(eng, t)
            for k, v in self.readers.get(w, {}).items():
                self._need(eng, (k[0], k[1], v))

    def _commit(self, tok, reads, writes):
        for r in reads:
            d = self.readers.setdefault(r, {})
            k = (tok[0], tok[1])
            if d.get(k, 0) < tok[2]:
                d[k] = tok[2]
        for w in writes:
            self.lastw[w] = tok
            self.readers[w] = {}

    def op(self, eng, fn, reads=(), writes=()):
        self._deps(eng, reads, writes)
        self.cnt[eng] += 1
        tok = ('eng', eng, self.cnt[eng])
        self.streams[eng].append(('op', fn))
        kn = self.known[eng]
        self.vc[eng][self.cnt[eng]] = tuple(kn.get('E_' + x, 0) for x in ENG)
        self._commit(tok, reads, writes)

    def dma(self, q, sem, out, in_, reads=(), writes=()):
        self._deps(q, reads, writes)
        self.dcnt[sem] = self.dcnt.get(sem, 0) + 16
        tok = ('dma', sem, self.dcnt[sem])
        self.streams[q].append(('dma', out, in_, sem))
        self._commit(tok, reads, writes)

    def barrier(self):
        for e in ENG:
            for x in ENG:
                if self.cnt[x] > 0 and not (x == 'pe' and e == 'pe'):
                    self._need(e, ('eng', x, self.cnt[x]))
            for s in list(self.dcnt.keys()):
                self._need(e, ('dma', s, 0))

    def emit(self):
        nc = self.nc
        names = ['E_' + e for e in ENG] + list(self.dcnt.keys())
        with ExitStack() as es:
            semh = {n: es.enter_context(nc.semaphore(n)) for n in names}
            block = es.enter_context(nc.Block())

            def run(e, name):
                esem = semh['E_' + name]
                for it in self.streams[name]:
                    if it[0] == 'wait':
                        e.wait_ge(semh[it[1]], it[2])
                    elif it[0] == 'op':
                        it[1](e).then_inc(esem, 1)
                    else:
                        e.dma_start(out=it[1], in_=it[2]).then_inc(semh[it[3]], 16)

            @block.tensor
            def _(e):
                run(e, 'pe')

            @block.scalar
            def _(e):
                run(e, 'act')

            @block.vector
            def _(e):
                run(e, 'dve')

            @block.gpsimd
            def _(e):
                run(e, 'pool')

            @block.sync
            def _(e):
                run(e, 'sp')


OFF = dict(aq=0, ak=256, av=512, bq=768, bk=1024, bv=1152, cq=1280, ck=1536, cv=1792, dq=2048, dkv=2304,
           dkr=2432)


def _wext_cols():
    cols = []
    for nm in ('aq', 'ak', 'cq', 'ck'):
        cols += list(range(OFF[nm], OFF[nm] + 256))
    sw = [d ^ 16 for d in range(64)]
    for pair in ((0, 2), (1, 3)):
        for h in pair:
            cols += [OFF['bq'] + h * 64 + d for d in range(64)]
    for pair in ((0, 2), (1, 3)):
        for h in pair:
            cols += [OFF['bq'] + h * 64 + sw[d] for d in range(64)]
    for h in (0, 1):
        cols += [OFF['bk'] + h * 64 + d for d in range(64)]
    for h in (0, 1):
        cols += [OFF['bk'] + h * 64 + sw[d] for d in range(64)]
    cols += list(range(OFF['dq'], OFF['dq'] + 256))
    cols += list(range(OFF['dkv'], OFF['dkv'] + 128))
    cols += [-1] * 64 + [OFF['dkr'] + d for d in range(32)]
    cols += [-1] * 64 + [OFF['dkr'] + (d ^ 16) for d in range(32)]
    cols += list(range(OFF['av'], OFF['av'] + 256))
    cols += list(range(OFF['bv'], OFF['bv'] + 128))
    cols += list(range(OFF['cv'], OFF['cv'] + 256))
    assert len(cols) == NCOL
    return np.array(cols)


def _gather_cols(w, cols):
    out = np.zeros(w.shape[:-1] + (len(cols),), dtype=w.dtype)
    m = cols >= 0
    out[..., m] = w[..., cols[m]]
    return out


def _t5_bucket(rel):
    nb, max_exact = 16, 8
    n = np.abs(rel)
    n_f = np.maximum(n, max_exact).astype(np.float32)
    large = max_exact + (np.log(n_f / np.float32(max_exact)) / np.float32(np.log(1024 / max_exact))
                         * np.float32(nb - max_exact)).astype(np.int32)
    large = np.minimum(large, nb - 1)
    return np.where(rel > 0, nb, 0) + np.where(n < max_exact, n, large)


LG = 3072
UA = 2944


def _consts():
    c = {}
    c['ident'] = np.eye(128, dtype=np.float32).astype(ml_dtypes.bfloat16)
    bo = np.zeros((128, 128), np.float32)
    bo[:64, :64] = 1
    bo[64:, 64:] = 1
    c['blockones'] = bo
    c['allones'] = np.ones((128, 128), np.float32)
    inv = (1.0 / (10000.0 ** (np.arange(0, 32, 2, dtype=np.float32) / np.float32(32)))).astype(np.float32)
    t = np.arange(S)
    cb = np.zeros((128, S), np.float32)
    sb = np.zeros((128, S), np.float32)
    for p in range(128):
        d = p % 64
        i = d % 16
        pos = (t // 64) if d < 32 else (t % 64)
        ang = (pos.astype(np.float32) * inv[i]).astype(np.float32).astype(np.float64)
        sgn = -1.0 if (d % 32) < 16 else 1.0
        cb[p] = np.cos(ang)
        sb[p] = sgn * np.sin(ang)
    cd = np.zeros((128, S), np.float32)
    sd = np.zeros((128, S), np.float32)
    for p in range(64, 96):
        d = p - 64
        i = d % 16
        ang = (t.astype(np.float32) * inv[i]).astype(np.float32).astype(np.float64)
        sgn = -1.0 if d < 16 else 1.0
        cd[p] = np.cos(ang)
        sd[p] = sgn * np.sin(ang)
    c['rope'] = np.stack([cb, sb, cd, sd], 0)
    kk = np.arange(LG)
    off = np.where(kk <= 2943, 1408 - kk, 4480 - kk)
    bk = _t5_bucket(off)
    mult = ((np.abs(off) <= 64).astype(np.float32)
            + ((np.abs(off) <= 256) & (off % 4 == 0)).astype(np.float32)
            + ((np.abs(off) <= 1024) & (off % 16 == 0)).astype(np.float32))
    mult[2944] = 0.0
    oh = np.zeros((32, LG), np.float32)
    oh[bk, np.arange(LG)] = 1.0
    c['a_onehot'] = oh
    c['a_mult'] = np.tile(mult[None, :], (4, 1)).astype(np.float32)
    cols = np.arange(64)
    cs = np.clip(cols - 8, 0, 48)
    kc = np.arange(64)[:, None]
    cv = ((kc >= cs[None, :]) & (kc < cs[None, :] + 16)).astype(np.float32)
    c['c_cv'] = np.concatenate([cv, cv], 0)
    return c


def build(nseq, dbg=None):
    nc = bass.Bass("TRN2", target_bir_lowering=False)
    sc = Sched(nc)

    def din(name, shape, dt=F32):
        return nc.dram_tensor(name, list(shape), dt, kind="ExternalInput")

    def dscr(name, shape, dt=BF16):
        return nc.dram_tensor(name, list(shape), dt, kind="Internal")

    x_in = din("x", [nseq, S, DM]).ap()
    y_out = nc.dram_tensor("y", [nseq, S, DM], F32, kind="ExternalOutput").ap()
    wext_in = din("wext", [2, DM, NCOL]).ap()
    uq_in = din("uq", [2, 256, 768]).ap()
    ukv_in = din("ukv", [2, 128, 512]).ap()
    wout_in = din("wout", [2, DM, DM]).ap()
    wg_in = din("wg", [2, DM, DFF]).ap()
    wu_in = din("wu", [2, DM, DFF]).ap()
    wd_in = din("wd", [2, DFF, DM]).ap()
    rowg_in = din("rowg", [7, DM]).ap()
    colc_in = din("colc", [2, 128, 8]).ap()
    t5_in = din("t5", [32, 4]).ap()
    rpb_in = din("rpbf", [2, 60, 31]).ap()
    ident_in = din("ident", [128, 128], BF16).ap()
    bones_in = din("blockones", [128, 128]).ap()
    aones_in = din("allones", [128, 128]).ap()
    rope_in = din("rope", [4, 128, S]).ap()
    aoh_in = din("a_onehot", [32, LG]).ap()
    amult_in = din("a_mult", [4, LG]).ap()
    ccv_in = din("c_cv", [128, 64]).ap()

    wext_b = dscr("wext_b", [2, DM, NCOL])
    uq_b = dscr("uq_b", [2, 256, 768])
    ukv_b = dscr("ukv_b", [2, 128, 512])
    wout_b = dscr("wout_b", [2, DM, DM])
    wg_b = dscr("wg_b", [2, DM, DFF])
    wu_b = dscr("wu_b", [2, DM, DFF])
    wd_b = dscr("wd_b", [2, DFF, DM])
    fmAC = dscr("fmAC", [8, 128, S])
    fmB = dscr("fmB", [3, 128, S])
    fmDq = dscr("fmDq", [4, 96, S])
    fmDk = dscr("fmDk", [4, 96, S])
    vA = dscr("vA", [S, 260])
    vB = dscr("vB", [S, 130])
    vC = dscr("vC", [S, 260])
    vD = dscr("vD", [S, 260])
    oN = dscr("oN", [S, DM])
    toeA = dscr("toeA", [4, 128, LG], F32)
    toeC = dscr("toeC", [60, 64, 128], F32)
    growA = dscr("growA", [4, LG], F32)
    growC = dscr("growC", [60, 128], F32)

    dbg_out = {}
    if dbg:
        for nm, shp, dt in dbg:
            dbg_out[nm] = nc.dram_tensor("dbg_" + nm, list(shp), dt, kind="ExternalOutput")

    def sb(name, shape, dt):
        return nc.alloc_sbuf_tensor(name, list(shape), dt)

    ident = sb("ident_s", [128, 128], BF16)
    bones = sb("bones_s", [128, 128], F32)
    aones = sb("aones_s", [128, 128], F32)
    colc = sb("colc_s", [128, 2, 8], F32)
    epsb = sb("epsb_s", [128, 1], F32)
    ARENA_ELEMS = 105000
    arena = sb("arena", [128, ARENA_ELEMS // 2], F32)
    NBANK = 6
    psA = nc.alloc_psum_tensor("psA", [128, NBANK * 512], F32)
    psT = nc.alloc_psum_tensor("psT", [128, 1024], F32)

    class Carver:
        def __init__(self):
            self.off = 0

        def take(self, shape, dt):
            n = int(np.prod(shape[1:]))
            ne = n * (2 if dt == F32 else 1)
            if self.off % 2:
                self.off += 1
            if ne % 2:
                ne += 1
            v = arena[0:shape[0], self.off // 2:(self.off + ne) // 2]
            self.last_off32 = self.off // 2
            self.off += ne
            assert self.off <= ARENA_ELEMS, (self.off, ARENA_ELEMS)
            if dt == BF16:
                v = v.bitcast(BF16)[:, 0:n]
            if len(shape) == 3:
                v = v.rearrange("p (a b) -> p a b", a=shape[1])
            elif len(shape) == 4:
                v = v.rearrange("p (a b c) -> p a b c", a=shape[1], b=shape[2])
            return v

    bank_rr = [0]

    def bank():
        b = bank_rr[0]
        bank_rr[0] = (b + 1) % 8
        return b

    def PSfull(b):
        return psA[:, b * 512:(b + 1) * 512] if b < NBANK else psT[:, (b - NBANK) * 512:(b - NBANK + 1) * 512]

    def PS(b, rows=128, cols=512, r0=0):
        return PSfull(b)[r0:r0 + rows, 0:cols]

    sc.dma('sp', 'd_c0', ident[:], ident_in, writes=['ident'])
    sc.dma('sp', 'd_c0', bones[:], bones_in, writes=['bones'])
    sc.dma('sp', 'd_c0', aones[:], aones_in, writes=['aones'])
    sc.op('dve', (lambda e: e.memset(epsb[:], EPS)), writes=['epsb'])
    sc.dma('sp', 'd_c0', colc[:], colc_in.rearrange("l p c -> p l c"), writes=['colc'])
    cvw = Carver()
    wstage = [cvw.take([128, 24064], BF16) for _ in range(2)]
    wi = 0
    for l in range(2):
        for (dst, src_, nm, rows, ncol) in ((wext_b, wext_in, 'wext', DM, NCOL), (uq_b, uq_in, 'uq', 256, 768),
                                            (ukv_b, ukv_in, 'ukv', 128, 512), (wout_b, wout_in, 'wout', DM, DM),
                                            (wg_b, wg_in, 'wg', DM, DFF), (wu_b, wu_in, 'wu', DM, DFF),
                                            (wd_b, wd_in, 'wd', DFF, DM)):
            k = rows // 128
            st = wstage[wi % 2][:, 0:k * ncol].rearrange("p (k n) -> p k n", k=k)
            sc.dma('pool', 'd_wc%d' % (wi % 2), st, src_[l].rearrange("(k p) n -> p k n", p=128), writes=[('wstage', wi % 2)])
            sc.dma('sp', 'd_wo%d' % (wi % 2), dst.ap()[l].rearrange("(k p) n -> p k n", p=128), st,
                   reads=[('wstage', wi % 2)], writes=[(nm + '_b', l)])
            wi += 1
    sc.barrier()

    def phase_p1(s, l, xsrc):
        cv = Carver()
        wext = cv.take([128, 8, NCOL], BF16)
        uq = cv.take([128, 2, 768], BF16)
        ukv = cv.take([128, 512], BF16)
        xt = [cv.take([128, 4, DM], F32) for _ in range(2)]
        ropet = [cv.take([128, 4, 512], F32) for _ in range(2)]
        hbs = [cv.take([128, 4, DM], BF16) for _ in range(2)]
        hT = cv.take([128, 8, 512], BF16)
        junk = cv.take([128, DM], BF16)
        ss = cv.take([128, 8], F32)
        sq = [cv.take([128, 512], F32) for _ in range(2)]
        rs = [cv.take([128, 512], F32) for _ in range(2)]
        t1 = [cv.take([128, 512], F32) for _ in range(2)]
        t2 = [cv.take([128, 512], F32) for _ in range(2)]
        dqn = cv.take([128, 2, 512], BF16)
        dkvn = cv.take([128, 512], BF16)
        stAC = [cv.take([128, 8, 512], BF16) for _ in range(2)]
        stB = [cv.take([128, 3, 512], BF16) for _ in range(2)]
        stDq = [cv.take([128, 4, 512], BF16) for _ in range(2)]
        stDk = [cv.take([128, 4, 512], BF16) for _ in range(2)]
        stVA = [cv.take([128, 4, 260], BF16) for _ in range(2)]
        stVB = [cv.take([128, 4, 130], BF16) for _ in range(2)]
        stVC = [cv.take([128, 4, 260], BF16) for _ in range(2)]
        stVD = [cv.take([128, 4, 260], BF16) for _ in range(2)]
        gmix = cv.take([128, DM], F32)
        sc.dma('sp', 'd_w1', gmix, rowg_in[l:l + 1, :].partition_broadcast(128), writes=[('rowg', l)])

        sc.dma('sp', 'd_w1', wext, wext_b.ap()[l].rearrange("(k p) n -> p k n", p=128),
               reads=[('wext_b', l)], writes=['wext'])
        sc.dma('sp', 'd_w1', uq, uq_b.ap()[l].rearrange("(k p) n -> p k n", p=128), reads=[('uq_b', l)], writes=['uq'])
        sc.dma('sp', 'd_w1', ukv, ukv_b.ap()[l], reads=[('ukv_b', l)], writes=['ukv'])
        for i in range(2):
            for (st, H, nm) in ((stVA, 4, 'stVA'), (stVB, 2, 'stVB'), (stVC, 4, 'stVC'), (stVD, 4, 'stVD')):
                v = st[i].rearrange("p j (h e) -> p (j h) e", e=65)
                sc.op('pool', (lambda e, v=v: e.memset(v[:, :, 64:65], 1.0)), writes=[(nm, i, 'ones')])

        def load(tt):
            sl = tt % 2
            t0 = tt * 512
            sc.dma('sp', 'd_x%d' % sl, xt[sl], xsrc[t0:t0 + 512, :].rearrange("(j p) d -> p j d", p=128),
                   reads=[('xres', s, 4 * tt + j) for j in range(4)], writes=[('xt', sl)])
            sc.dma('sp', 'd_x%d' % sl, ropet[sl], rope_in[:, :, t0:t0 + 512].rearrange("a p t -> p a t"),
                   writes=[('ropet', sl)])

        load(0)

        def prep(tt):
            sl = tt % 2
            X = xt[sl]
            hb = hbs[sl]
            for j in range(4):
                sc.op('act', (lambda e, j=j: e.activation(junk, X[:, j, :], AF.Square, accum_out=ss[:, j:j + 1])),
                      reads=[('xt', sl)], writes=['junk', ('ssj', j)])
            sc.op('act', (lambda e: e.activation(ss[:, 4:8], ss[:, 0:4], AF.Sqrt, bias=epsb[:, 0:1], scale=1.0 / DM)),
                  reads=[('ssj', j) for j in range(4)] + ['epsb'], writes=['rstd0'])
            sc.op('dve', (lambda e: e.reciprocal(ss[:, 4:8], ss[:, 4:8])), reads=['rstd0'], writes=['rstd'])
            for j in range(4):
                sc.op('dve', (lambda e, j=j: e.scalar_tensor_tensor(hb[:, j, :], X[:, j, :], ss[:, 4 + j:5 + j], gmix,
                                                                   ALU.mult, ALU.mult)),
                      reads=[('xt', sl), 'rstd', ('rowg', l)], writes=[('hb', sl, j)])

        def do_tile(tt):
            sl = tt % 2
            t0 = tt * 512
            if tt + 1 < 8:
                load(tt + 1)
            R = ropet[sl]
            hb = hbs[sl]
            for kc in range(8):
                bT = bank()
                pT = PS(bT)

                def tr(e, kc=kc, pT=pT):
                    for j in range(4):
                        ins = e.matmul(pT[:, j * 128:(j + 1) * 128], hb[:, j, kc * 128:(kc + 1) * 128], ident[:], start=True,
                                       stop=True)
                    return ins
                sc.op('pe', tr, reads=[('hb', sl, j) for j in range(4)] + ['ident'], writes=[('ps', bT)])
                eng = 'act' if kc % 2 == 0 else 'dve'
                if eng == 'act':
                    sc.op('act', (lambda e, kc=kc, pT=pT: e.copy(hT[:, kc, :], pT)), reads=[('ps', bT)], writes=[('hT', kc)])
                else:
                    sc.op('dve', (lambda e, kc=kc, pT=pT: e.tensor_copy(hT[:, kc, :], pT)), reads=[('ps', bT)],
                          writes=[('hT', kc)])
            if tt + 1 < 8:
                prep(tt + 1)
            hT_r = [('hT', kc) for kc in range(8)]

            def fm_chunk(c0, width, b):
                def f(e):
                    for kc in range(8):
                        ins = e.matmul(PS(b, rows=width), wext[:, kc, c0:c0 + width], hT[:, kc, :], start=(kc == 0),
                                       stop=(kc == 7))
                    return ins
                sc.op('pe', f, reads=hT_r + ['wext'], writes=[('ps', b)])

            if STOP <= 2:
                return
            for c in range(8):
                b = bank()
                fm_chunk(c * 128, 128, b)
                sc.op('act', (lambda e, c=c, b=b: e.copy(stAC[sl][:, c, :], PS(b))), reads=[('ps', b)],
                      writes=[('stAC', sl, c)])
            sc.dma('sp', 'd_oAC%d' % sl, fmAC.ap()[:, :, t0:t0 + 512].rearrange("c p t -> p c t"), stAC[sl],
                   reads=[('stAC', sl, c) for c in range(8)], writes=[('fmAC', s)])

            if STOP <= 3:
                return
            for ci, (c_raw, c_sw, gcol) in enumerate(((8, 10, 0), (9, 11, 0), (12, 13, 2))):
                b1, b2, b3 = bank(), bank(), bank()
                u = ci % 2
                fm_chunk(1024 + (c_raw - 8) * 128, 128, b1)
                fm_chunk(1024 + (c_sw - 8) * 128, 128, b2)
                sc.op('act', (lambda e, b1=b1, u=u: e.activation(sq[u], PS(b1), AF.Square)), reads=[('ps', b1)],
                      writes=[('sq', u)])
                sc.op('pe', (lambda e, b3=b3, u=u: e.matmul(PS(b3), bones[:], sq[u], start=True, stop=True)),
                      reads=[('sq', u), 'bones'], writes=[('ps', b3)])
                sc.op('act', (lambda e, b3=b3, u=u: e.activation(rs[u], PS(b3), AF.Sqrt, bias=epsb[:, 0:1], scale=1.0 / 64)),
                      reads=[('ps', b3), 'epsb'], writes=[('rs', u)])
                sc.op('dve', (lambda e, u=u: e.reciprocal(rs[u], rs[u])), reads=[('rs', u)], writes=[('rs', u)])
                sc.op('dve', (lambda e, b1=b1, u=u, gcol=gcol: e.scalar_tensor_tensor(
                    t1[u], PS(b1), colc[:, l, gcol:gcol + 1], R[:, 0, :], ALU.mult, ALU.mult)),
                    reads=[('ps', b1), 'colc', ('ropet', sl)], writes=[('t1', u)])
                sc.op('dve', (lambda e, b2=b2, u=u, gcol=gcol: e.scalar_tensor_tensor(
                    t2[u], PS(b2), colc[:, l, gcol + 1:gcol + 2], R[:, 1, :], ALU.mult, ALU.mult)),
                    reads=[('ps', b2), 'colc', ('ropet', sl)], writes=[('t2', u)])
                sc.op('pool', (lambda e, u=u: e.tensor_tensor(t1[u], t1[u], t2[u], ALU.add)), reads=[('t1', u), ('t2', u)],
                      writes=[('t1', u)])
                sc.op('dve', (lambda e, u=u, ci=ci: e.tensor_tensor(stB[sl][:, ci, :], t1[u], rs[u], ALU.mult)),
                      reads=[('t1', u), ('rs', u)], writes=[('stB', sl, ci)])
            sc.dma('sp', 'd_oB%d' % sl, fmB.ap()[:, :, t0:t0 + 512].rearrange("c p t -> p c t"), stB[sl],
                   reads=[('stB', sl, c) for c in range(3)], writes=[('fmB', s)])

            if STOP <= 4:
                return
            bq0, bq1, bkv, bsq, bskv = bank(), bank(), bank(), bank(), bank()
            fm_chunk(1792, 128, bq0)
            fm_chunk(1920, 128, bq1)
            fm_chunk(2048, 128, bkv)
            sc.op('act', (lambda e: e.activation(sq[0], PS(bq0), AF.Square)), reads=[('ps', bq0)], writes=[('sq', 0)])
            sc.op('act', (lambda e: e.activation(sq[1], PS(bq1), AF.Square)), reads=[('ps', bq1)], writes=[('sq', 1)])

            def ssd(e):
                e.matmul(PS(bsq), aones[:], sq[0], start=True, stop=False)
                return e.matmul(PS(bsq), aones[:], sq[1], start=False, stop=True)
            sc.op('pe', ssd, reads=[('sq', 0), ('sq', 1), 'aones'], writes=[('ps', bsq)])
            sc.op('act', (lambda e: e.activation(rs[0], PS(bsq), AF.Sqrt, bias=epsb[:, 0:1], scale=1.0 / 256)),
                  reads=[('ps', bsq), 'epsb'], writes=[('rs', 0)])
            sc.op('dve', (lambda e: e.reciprocal(rs[0], rs[0])), reads=[('rs', 0)], writes=[('rs', 0)])
            sc.op('dve', (lambda e: e.scalar_tensor_tensor(dqn[:, 0, :], PS(bq0), colc[:, l, 4:5], rs[0], ALU.mult, ALU.mult)),
                  reads=[('ps', bq0), ('rs', 0), 'colc'], writes=[('dqn', 0)])
            sc.op('dve', (lambda e: e.scalar_tensor_tensor(dqn[:, 1, :], PS(bq1), colc[:, l, 5:6], rs[0], ALU.mult, ALU.mult)),
                  reads=[('ps', bq1), ('rs', 0), 'colc'], writes=[('dqn', 1)])
            sc.op('act', (lambda e: e.activation(t2[0], PS(bkv), AF.Square)), reads=[('ps', bkv)], writes=[('t2', 0)])
            sc.op('pe', (lambda e: e.matmul(PS(bskv), aones[:], t2[0], start=True, stop=True)), reads=[('t2', 0), 'aones'],
                  writes=[('ps', bskv)])
            sc.op('act', (lambda e: e.activation(rs[1], PS(bskv), AF.Sqrt, bias=epsb[:, 0:1], scale=1.0 / 128)),
                  reads=[('ps', bskv), 'epsb'], writes=[('rs', 1)])
            sc.op('dve', (lambda e: e.reciprocal(rs[1], rs[1])), reads=[('rs', 1)], writes=[('rs', 1)])
            sc.op('dve', (lambda e: e.scalar_tensor_tensor(dkvn, PS(bkv), colc[:, l, 6:7], rs[1], ALU.mult, ALU.mult)),
                  reads=[('ps', bkv), ('rs', 1), 'colc'], writes=['dkvn'])

            for h in range(4):
                b1, b2 = bank(), bank()

                def fq(e, h=h, b1=b1, o=0):
                    for kc in range(2):
                        ins = e.matmul(PS(b1, rows=96), uq[:, kc, o + h * 96:o + (h + 1) * 96], dqn[:, kc, :], start=(kc == 0),
                                       stop=(kc == 1))
                    return ins

                def fqs(e, h=h, b2=b2, o=384):
                    for kc in range(2):
                        ins = e.matmul(PS(b2, rows=96), uq[:, kc, o + h * 96:o + (h + 1) * 96], dqn[:, kc, :], start=(kc == 0),
                                       stop=(kc == 1))
                    return ins
                sc.op('pe', fq, reads=[('dqn', 0), ('dqn', 1), 'uq'], writes=[('ps', b1)])
                sc.op('pe', fqs, reads=[('dqn', 0), ('dqn', 1), 'uq'], writes=[('ps', b2)])
                sc.op('act', (lambda e, h=h, b1=b1: e.copy(stDq[sl][0:64, h, :], PS(b1, rows=64))), reads=[('ps', b1)],
                      writes=[('stDq', sl, h, 'n')])
                sc.op('dve', (lambda e, b1=b1: e.tensor_tensor(t1[0][64:96, :], PS(b1, rows=32, r0=64), R[64:96, 2, :], ALU.mult)),
                      reads=[('ps', b1), ('ropet', sl)], writes=[('t1', 0)])
                sc.op('dve', (lambda e, b2=b2: e.tensor_tensor(t2[0][64:96, :], PS(b2, rows=32, r0=64), R[64:96, 3, :], ALU.mult)),
                      reads=[('ps', b2), ('ropet', sl)], writes=[('t2', 0)])
                sc.op('dve', (lambda e, h=h: e.tensor_tensor(stDq[sl][64:96, h, :], t1[0][64:96, :], t2[0][64:96, :], ALU.add)),
                      reads=[('t1', 0), ('t2', 0)], writes=[('stDq', sl, h, 'r')])
            sc.dma('sp', 'd_oDq%d' % sl, fmDq.ap()[:, :, t0:t0 + 512].rearrange("h p t -> p h t"), stDq[sl][0:96],
                   reads=[('stDq', sl, h, x) for h in range(4) for x in 'nr'], writes=[('fmDq', s)])
            for h in range(4):
                b1 = bank()
                sc.op('pe', (lambda e, h=h, b1=b1: e.matmul(PS(b1, rows=64), ukv[:, h * 64:(h + 1) * 64], dkvn, start=True,
                                                           stop=True)), reads=['dkvn', 'ukv'], writes=[('ps', b1)])
                sc.op('act', (lambda e, h=h, b1=b1: e.copy(stDk[sl][0:64, h, :], PS(b1, rows=64))), reads=[('ps', b1)],
                      writes=[('stDk', sl, h, 'n')])
            b1, b2 = bank(), bank()
            fm_chunk(2176, 96, b1)
            fm_chunk(2272, 96, b2)
            sc.op('dve', (lambda e, b1=b1: e.tensor_tensor(t1[1][64:96, :], PS(b1, rows=32, r0=64), R[64:96, 2, :], ALU.mult)),
                  reads=[('ps', b1), ('ropet', sl)], writes=[('t1', 1)])
            sc.op('dve', (lambda e, b2=b2: e.tensor_tensor(t2[1][64:96, :], PS(b2, rows=32, r0=64), R[64:96, 3, :], ALU.mult)),
                  reads=[('ps', b2), ('ropet', sl)], writes=[('t2', 1)])
            for h in range(4):
                sc.op('dve', (lambda e, h=h: e.tensor_tensor(stDk[sl][64:96, h, :], t1[1][64:96, :], t2[1][64:96, :], ALU.add)),
                      reads=[('t1', 1), ('t2', 1)], writes=[('stDk', sl, h, 'r')])
            sc.dma('sp', 'd_oDk%d' % sl, fmDk.ap()[:, :, t0:t0 + 512].rearrange("h p t -> p h t"), stDk[sl][0:96],
                   reads=[('stDk', sl, h, x) for h in range(4) for x in 'nr'], writes=[('fmDk', s)])

            if STOP <= 5:
                return
            for j in range(4):
                b1, b2, b3 = bank(), bank(), bank()

                def fv(e, j=j, b=b1, c0=2368, w=384):
                    for kc in range(8):
                        ins = e.matmul(PS(b, cols=w), hT[:, kc, j * 128:(j + 1) * 128], wext[:, kc, c0:c0 + w], start=(kc == 0),
                                       stop=(kc == 7))
                    return ins

                def fv2(e, j=j, b=b2, c0=2752, w=256):
                    for kc in range(8):
                        ins = e.matmul(PS(b, cols=w), hT[:, kc, j * 128:(j + 1) * 128], wext[:, kc, c0:c0 + w], start=(kc == 0),
                                       stop=(kc == 7))
                    return ins
                sc.op('pe', fv, reads=hT_r + ['wext'], writes=[('ps', b1)])
                sc.op('pe', fv2, reads=hT_r + ['wext'], writes=[('ps', b2)])
                sc.op('pe', (lambda e, j=j, b3=b3: e.matmul(PS(b3, cols=256), dkvn[:, j * 128:(j + 1) * 128], ukv[:, 256:512],
                                                           start=True, stop=True)), reads=['dkvn', 'ukv'], writes=[('ps', b3)])

                def v65(st, j, H):
                    return st[sl][:, j, :].rearrange("p (h e) -> p h e", e=65)[:, :, 0:64]

                def p64(b, c0, H):
                    return PSfull(b)[:, c0:c0 + H * 64].rearrange("p (h d) -> p h d", d=64)
                if KV & 4:
                  sc.op('act', (lambda e, j=j, b1=b1: e.copy(v65(stVA, j, 4), p64(b1, 0, 4))), reads=[('ps', b1)],
                      writes=[('stVA', sl, j)])
                if KV & 8:
                  sc.op('act', (lambda e, j=j, b1=b1: e.copy(v65(stVB, j, 2), p64(b1, 256, 2))), reads=[('ps', b1)],
                      writes=[('stVB', sl, j)])
                if KV & 4:
                  sc.op('act', (lambda e, j=j, b2=b2: e.copy(v65(stVC, j, 4), p64(b2, 0, 4))), reads=[('ps', b2)],
                      writes=[('stVC', sl, j)])
                if KV & 8:
                  sc.op('act', (lambda e, j=j, b3=b3: e.copy(v65(stVD, j, 4), p64(b3, 0, 4))), reads=[('ps', b3)],
                      writes=[('stVD', sl, j)])
            if STOP <= 6:
                return
            for (st, dst, nm, w) in ((stVA, vA, 'stVA', 260), (stVB, vB, 'stVB', 130), (stVC, vC, 'stVC', 260),
                                     (stVD, vD, 'stVD', 260)):
                sc.dma('sp', 'd_o%s%d' % (nm, sl), dst.ap()[t0:t0 + 512, :].rearrange("(j p) w -> p j w", p=128), st[sl],
                       reads=[(nm, sl, j) for j in range(4)] + [(nm, sl, 'ones')], writes=[(nm[2:], s)])
        prep(0)
        for tt in range(8):
            do_tile(tt)
        sc.barrier()

    def setup_amask():
        cv = Carver()
        t5s = cv.take([128, 128], F32)
        aoh = cv.take([128, LG], F32)
        amu = cv.take([4, LG], F32)
        gr = cv.take([4, LG], F32)
        repa = cv.take([128, 4, LG], F32)
        sc.op('dve', (lambda e: e.memset(t5s, 0.0)), writes=['t5s'])
        sc.op('pool', (lambda e: e.memset(aoh, 0.0)), writes=['aoh'])
        sc.dma('sp', 'd_m0', t5s[0:32, 0:4], t5_in, writes=['t5s'])
        sc.dma('sp', 'd_m0', aoh[0:32, :], aoh_in, writes=['aoh'])
        sc.dma('sp', 'd_m0', amu, amult_in, writes=['amu'])
        for c0 in range(0, LG, 512):
            w = 512
            b = bank()
            sc.op('pe', (lambda e, b=b, c0=c0, w=w: e.matmul(PS(b, cols=w), t5s, aoh[:, c0:c0 + w], start=True, stop=True)),
                  reads=['t5s', 'aoh'], writes=[('ps', b)])
            sc.op('act', (lambda e, b=b, c0=c0, w=w: e.activation(gr[:, c0:c0 + w], PS(b, rows=4, cols=w), AF.Exp)),
                  reads=[('ps', b)], writes=[('gr', c0)])
            sc.op('dve', (lambda e, c0=c0, w=w: e.tensor_tensor(gr[:, c0:c0 + w], gr[:, c0:c0 + w], amu[:, c0:c0 + w], ALU.mult)),
                  reads=[('gr', c0), 'amu'], writes=[('gr', c0)])
        sc.dma('sp', 'd_m1', growA.ap(), gr, reads=[('gr', c0) for c0 in range(0, LG, 512)], writes=['growA'])
        sc.dma('sp', 'd_m1', repa, growA.ap().rearrange("(o h) n -> o h n", o=1).partition_broadcast(128) if False else
               bass.AP(growA, 0, [[0, 128], [LG, 4], [1, LG]]), reads=['growA'], writes=['repa'])
        sc.dma('sp', 'd_m1', toeA.ap().rearrange("h p n -> p h n"), repa, reads=['repa'], writes=['toeA'])
        sc.barrier()

    def setup_cmask(l):
        cv = Carver()
        e60 = cv.take([60, 32], F32)
        rpad = cv.take([60, 128], F32)
        repc = cv.take([64, 60, 128], F32)
        sc.dma('sp', 'd_m0', e60[:, 0:31], rpb_in[l], writes=['e60'])
        sc.op('dve', (lambda e: e.memset(rpad, 0.0)), writes=['rpad'])
        sc.op('act', (lambda e: e.activation(rpad[:, 0:16], e60[:, 15:31], AF.Exp)), reads=['e60', 'rpad'], writes=['rpad'])
        sc.op('act', (lambda e: e.activation(rpad[:, 113:128], e60[:, 0:15], AF.Exp)), reads=['e60', 'rpad'], writes=['rpad'])
        sc.dma('sp', 'd_m1', growC.ap(), rpad, reads=['rpad'], writes=['growC'])
        sc.dma('sp', 'd_m1', repc, bass.AP(growC, 0, [[0, 64], [128, 60], [1, 128]]), reads=['growC'], writes=['repc'])
        sc.dma('sp', 'd_m1', toeC.ap().rearrange("r p n -> p r n"), repc, reads=['repc'], writes=['toeC'])
        sc.barrier()

    def phase_p2(s, l):
        cv = Carver()
        KTs = [cv.take([128, 4, S], BF16) for _ in range(2)]
        Ves = [cv.take([128, 32, 260], BF16) for _ in range(2)]
        Vo = cv.take([128, 32, 260], BF16)
        Qt = [cv.take([128, 4, 512], BF16) for _ in range(2)]
        PT = [cv.take([128, 2, 512], BF16) for _ in range(3)]
        strip = cv.take([128, 4, UA], BF16)
        cm32 = cv.take([128, 14, 4, 64], F32)
        cmask = cv.take([128, 14, 4, 64], BF16)
        ccv = cv.take([128, 64], F32)
        gout = cv.take([128, DM], F32)
        og = [cv.take([128, 4, 256], F32) for _ in range(2)]
        onb = [cv.take([128, 4, 256], BF16) for _ in range(2)]
        junk = cv.take([128, 256], BF16)
        rc = cv.take([128, 16], F32)
        ssn = cv.take([128, 8], F32)
        sc.dma('sp', 'd_w1', gout, rowg_in[2 + l:3 + l, :].partition_broadcast(128), writes=['gout'])
        psT_b = [psT[:, 0:512], psT[:, 512:1024]]
        obanks = [(('ps', 4), PS(4)), (('ps', 5), PS(5)), (('psT', 0), psT_b[0]), (('psT', 1), psT_b[1])]
        pt_rr = [0]
        ep_rr = [0]

        def epilogue_head(oreg, oap, ogt, h, nj, u):
            o3 = oap[:, 0:nj * 65].rearrange("p (j e) -> p j e", e=65)
            sc.op('dve', (lambda e: e.reciprocal(rc[:, u * 4:u * 4 + nj], o3[:, :, 64])), reads=[oreg], writes=[('rc', u)])
            for j in range(nj):
                sc.op('dve', (lambda e, j=j: e.tensor_scalar_mul(ogt[:, j, h * 64:(h + 1) * 64], o3[:, j, 0:64],
                                                                 rc[:, u * 4 + j:u * 4 + j + 1])),
                      reads=[oreg, ('rc', u)], writes=[('og', h, j)])

        def group_norm_store(g, t0, ogt, slot):
            for j in range(4):
                sc.op('act', (lambda e, j=j: e.activation(junk, ogt[:, j, :], AF.Square, accum_out=ssn[:, j:j + 1])),
                      reads=[('og', h, j) for h in range(4)], writes=['junk2', ('ssn', j)])
            sc.op('act', (lambda e: e.activation(ssn[:, 4:8], ssn[:, 0:4], AF.Ln, bias=epsb[:, 0:1], scale=1.0 / 256)),
                  reads=[('ssn', j) for j in range(4)] + ['epsb'], writes=['ssr0'])
            sc.op('act', (lambda e: e.activation(ssn[:, 4:8], ssn[:, 4:8], AF.Exp, scale=-0.5)), reads=['ssr0'], writes=['ssr'])
            for j in range(4):
                sc.op('dve', (lambda e, j=j: e.scalar_tensor_tensor(onb[slot][:, j, :], ogt[:, j, :], ssn[:, 4 + j:5 + j],
                                                                    gout[:, g * 256:(g + 1) * 256], ALU.mult, ALU.mult)),
                      reads=[('og', h, j) for h in range(4)] + ['ssr', 'gout'], writes=[('onb', slot, j)])
            sc.dma('sp', 'd_on%d' % slot, oN.ap()[t0:t0 + 512, g * 256:(g + 1) * 256].rearrange("(j p) w -> p j w", p=128),
                   onb[slot], reads=[('onb', slot, j) for j in range(4)], writes=[('oN', s)])

        def dense_group(g, name, bi, mode):
            KT, Ve = KTs[bi], Ves[bi]
            kreg, vreg_ = ('KT', bi), ('Ve', bi)
            nchunk = {'A': 2, 'B': 1, 'D': 4}[name]
            scale = (96.0 if name == 'D' else 64.0) ** -0.5
            if mode == 'load':
                if name == 'A':
                    sc.dma('sp', 'd_k%d' % bi, KT[:, 0:2, :], fmAC.ap()[2:4].rearrange("c p t -> p c t"), reads=[('fmAC', s)], writes=[kreg])
                    vsrc, vw, nq = vA, 260, 2
                    for h in range(4):
                        sc.dma('pool', 'd_strip', strip[:, h, :], bass.AP(toeA, h * 128 * LG, [[LG - 1, 128], [1, UA]]), reads=['toeA'],
                               writes=['strip'])
                elif name == 'B':
                    sc.dma('sp', 'd_k%d' % bi, KT[:, 0:1, :], fmB.ap()[2:3].rearrange("c p t -> p c t"), reads=[('fmB', s)], writes=[kreg])
                    vsrc, vw, nq = vB, 130, 2
                else:
                    sc.dma('sp', 'd_k%d' % bi, KT[0:96, 0:4, :], fmDk.ap().rearrange("h p t -> p h t"), reads=[('fmDk', s)], writes=[kreg])
                    vsrc, vw, nq = vD, 260, 4
                sc.dma('sp', 'd_k%d' % bi, Ve[:, :, 0:vw], vsrc.ap().rearrange("(j p) w -> p j w", p=128), reads=[(vsrc.name if False else name + 'v', s)] if False else [({'A': 'VA', 'B': 'VB', 'D': 'VD'}[name], s)], writes=[vreg_])

                return

            def loadq(qt):
                sl = qt % 2
                t0 = qt * 512
                if name == 'A':
                    sc.dma('sp', 'd_q%d' % sl, Qt[sl][:, 0:2, :], fmAC.ap()[0:2, :, t0:t0 + 512].rearrange("c p t -> p c t"),
                           reads=[('fmAC', s)], writes=[('Qt', sl)])
                elif name == 'B':
                    sc.dma('sp', 'd_q%d' % sl, Qt[sl][:, 0:2, :], fmB.ap()[0:2, :, t0:t0 + 512].rearrange("c p t -> p c t"),
                           reads=[('fmB', s)], writes=[('Qt', sl)])
                else:
                    sc.dma('sp', 'd_q%d' % sl, Qt[sl][0:96, 0:4, :], fmDq.ap()[:, :, t0:t0 + 512].rearrange("h p t -> p h t"),
                           reads=[('fmDq', s)], writes=[('Qt', sl)])

            def kbs_of(qt):
                t0 = qt * 512
                if name == 'A':
                    return [kb for kb in range(32) if (kb * 128 + 127 >= t0 - 1024) and (kb * 128 <= t0 + 511 + 1024)]
                return list(range(32))

            def views(qt, pr):
                sl = qt % 2
                if name == 'A':
                    heads = (2 * pr, 2 * pr + 1)
                    qv = [Qt[sl][0:64, pr, :], Qt[sl][64:128, pr, :]]
                    kv = [KT[0:64, pr, :], KT[64:128, pr, :]]
                    vh = heads
                elif name == 'B':
                    heads = (pr, pr + 2)
                    qv = [Qt[sl][0:64, pr, :], Qt[sl][64:128, pr, :]]
                    kv = [KT[0:64, 0, :], KT[64:128, 0, :]]
                    vh = (0, 1)
                else:
                    heads = (2 * pr, 2 * pr + 1)
                    qv = [Qt[sl][0:96, heads[0], :], Qt[sl][0:96, heads[1], :]]
                    kv = [KT[0:96, heads[0], :], KT[0:96, heads[1], :]]
                    vh = heads
                return heads, qv, kv, vh

            units = []
            for qt in range(8):
                ks = kbs_of(qt)
                for pr in range(2):
                    for ki, kb in enumerate(ks):
                        units.append(dict(qt=qt, pr=pr, ki=ki, kb=kb, nk=len(ks), idx=len(units)))

            def emit_qk(u):
                qt, pr, kb = u['qt'], u['pr'], u['kb']
                sl = qt % 2
                if pr == 0 and u['ki'] == 0 and qt + 1 < 8:
                    loadq(qt + 1)
                heads, qv, kv, vh = views(qt, pr)
                sb_ = (u['idx'] % 3) * 2

                def qk(e, kb=kb, sb_=sb_, qv=qv, kv=kv):
                    for _ in range(1 + KDUP):
                        e.matmul(PS(sb_), kv[0][:, kb * 128:(kb + 1) * 128], qv[0], start=True, stop=True)
                        ins = e.matmul(PS(sb_ + 1), kv[1][:, kb * 128:(kb + 1) * 128], qv[1], start=True, stop=True)
                    return ins
                sc.op('pe', qk, reads=[kreg, ('Qt', sl)], writes=[('ps', sb_), ('ps', sb_ + 1)])

            def emit_rest(u):
                qt, pr, kb, ki, nk = u['qt'], u['pr'], u['kb'], u['ki'], u['nk']
                t0 = qt * 512
                heads, qv, kv, vh = views(qt, pr)
                sb_ = (u['idx'] % 3) * 2
                pslot = u['idx'] % 3
                P_ = PT[pslot]
                ob = [obanks[2], obanks[3]]
                ogt = og[qt % 2]
                sc.op('act', (lambda e: e.activation(P_.rearrange("p a n -> p (a n)"), psA[:, sb_ * 512:(sb_ + 2) * 512], AF.Exp,
                                                     scale=scale)),
                      reads=[('ps', sb_), ('ps', sb_ + 1)], writes=[('PT', pslot, 0), ('PT', pslot, 1)])
                if name == 'A':
                    off = 1408 - (kb * 128 - t0)
                    h0 = heads[0]
                    sc.op('dve', (lambda e: e.tensor_tensor(P_, P_, strip[:, h0:h0 + 2, off:off + 512], ALU.mult)),
                          reads=[('PT', pslot, 0), ('PT', pslot, 1), 'strip'], writes=[('PT', pslot, 0), ('PT', pslot, 1)])
                first, last = (ki == 0), (ki == nk - 1)

                def pv(e):
                    for i in range(2):
                        for j in range(4):
                            ins = e.matmul(ob[i][1][:, j * 65:(j + 1) * 65], P_[:, i, j * 128:(j + 1) * 128],
                                           Ve[:, kb, vh[i] * 65:(vh[i] + 1) * 65], start=(first and j == 0), stop=last,
                                           skip_group_check=True)
                    return ins
                sc.op('pe', pv, reads=[('PT', pslot, 0), ('PT', pslot, 1), vreg_], writes=[ob[0][0], ob[1][0]])
                if last:
                    for i in range(2):
                        epilogue_head(ob[i][0], ob[i][1], ogt, heads[i], 4, i)
                    if pr == 1:
                        group_norm_store(g, t0, ogt, qt % 2)

            loadq(0)
            emit_qk(units[0])
            emit_qk(units[1])
            for n, u in enumerate(units):
                if n + 2 < len(units):
                    emit_qk(units[n + 2])
                emit_rest(u)

        def group_c(g, bi, mode):
            KT, Ve = KTs[bi], Ves[bi]
            kreg, vreg_ = ('KT', bi), ('Ve', bi)
            if mode == 'load':
                sc.dma('sp', 'd_k%d' % bi, KT[:, 0:2, :], fmAC.ap()[6:8].rearrange("c p t -> p c t"), reads=[('fmAC', s)], writes=[kreg])
                sc.dma('sp', 'd_k%d' % bi, Ve, vC.ap().rearrange("(j p) w -> p j w", p=128), reads=[('VC', s)], writes=[vreg_])
                sc.dma('sp', 'd_kc', Vo[:, 0:31, :], vC.ap()[64:64 + 31 * 128, :].rearrange("(j p) w -> p j w", p=128), reads=[('VC', s)],
                       writes=['Vo'])
                sc.dma('sp', 'd_kc', ccv, ccv_in, writes=['ccv'])
                for pos, h in enumerate((0, 2, 1, 3)):
                    for a_ in range(2):
                        sc.dma('sp', 'd_kc', cm32[a_ * 64:(a_ + 1) * 64, :, pos, :],
                               bass.AP(toeC, (h * 15 + a_) * 8192, [[127, 64], [8192, 14], [1, 64]]), reads=['toeC'], writes=['cm32'])
                for m in range(14):
                    for pos in range(4):
                        sc.op('dve', (lambda e, m=m, pos=pos: e.tensor_tensor(cmask[:, m, pos, :], cm32[:, m, pos, :], ccv, ALU.mult)),
                              reads=['cm32', 'ccv'], writes=['cmask'])

                return

            def loadq(qt):
                sl = qt % 2
                t0 = qt * 512
                sc.dma('sp', 'd_q%d' % sl, Qt[sl][:, 0:2, :], fmAC.ap()[4:6, :, t0:t0 + 512].rearrange("c p t -> p c t"),
                       reads=[('fmAC', s)], writes=[('Qt', sl)])

            units = []
            for r in range(64):
                for jw in range(4):
                    units.append(dict(r=r, jw=jw, idx=len(units)))

            def emit_qk(u):
                r, jw = u['r'], u['jw']
                qt = r // 8
                sl = qt % 2
                if r % 8 == 0 and jw == 0 and qt + 1 < 8:
                    loadq(qt + 1)
                rs_ = min(max(r - 4, 0), 56)
                kw = rs_ * 64 + 128 * jw
                sb_ = (u['idx'] % 2) * 2
                qc = (r % 8) * 64

                def qk(e):
                    for h in range(4):
                        c, lo = h // 2, 64 * (h % 2)
                        bnk = sb_ + (h % 2)
                        ins = e.matmul(psA[:, bnk * 512 + (h // 2) * 64: bnk * 512 + (h // 2) * 64 + 64],
                                       KT[lo:lo + 64, c, kw:kw + 128], Qt[sl][lo:lo + 64, c, qc:qc + 64], start=True, stop=True)
                    return ins
                sc.op('pe', qk, reads=[kreg, ('Qt', sl)], writes=[('ps', sb_), ('ps', sb_ + 1)])

            def emit_rest(u):
                r, jw = u['r'], u['jw']
                qt = r // 8
                ogt = og[qt % 2]
                i4 = (r % 8) // 2
                half = r % 2
                rs_ = min(max(r - 4, 0), 56)
                oreg, oap = obanks[(r // 2) % 4]
                kw = rs_ * 64 + 128 * jw
                m = rs_ + 2 * jw - r + 7
                sb_ = (u['idx'] % 2) * 2
                pslot = u['idx'] % 3
                P_ = PT[pslot]
                pout = P_[:, :, 0:128]
                pin = psA[:, sb_ * 512:(sb_ + 2) * 512].rearrange("p (b n) -> p b n", b=2)[:, :, 0:128]
                sc.op('act', (lambda e: e.activation(pout, pin, AF.Exp, scale=0.125)),
                      reads=[('ps', sb_), ('ps', sb_ + 1)], writes=[('PT', pslot, 0), ('PT', pslot, 1)])
                mk = cmask[:, m, :, :].rearrange("p (b u) c -> p b (u c)", b=2)
                sc.op('dve', (lambda e: e.tensor_tensor(pout, pout, mk, ALU.mult)),
                      reads=[('PT', pslot, 0), ('PT', pslot, 1), 'cmask'], writes=[('PT', pslot, 0), ('PT', pslot, 1)])
                if kw % 128 == 0:
                    vt, vi, vreg = Ve, kw // 128, vreg_
                else:
                    vt, vi, vreg = Vo, (kw - 64) // 128, 'Vo'
                first, last = (jw == 0), (jw == 3)

                def pv(e):
                    for h in range(4):
                        b_, u_ = h % 2, h // 2
                        ins = e.matmul(oap[half * 64:half * 64 + 64, h * 65:(h + 1) * 65],
                                       P_[:, b_, u_ * 64:u_ * 64 + 64], vt[:, vi, h * 65:(h + 1) * 65], start=(first and h == 0),
                                       stop=last, skip_group_check=True)
                    return ins
                sc.op('pe', pv, reads=[('PT', pslot, 0), ('PT', pslot, 1), vreg], writes=[oreg])
                if half == 1 and last:
                    o3 = oap[:, 0:260].rearrange("p (h e) -> p h e", e=65)
                    sc.op('dve', (lambda e: e.reciprocal(rc[:, 0:4], o3[:, :, 64])), reads=[oreg], writes=[('rc', 0)])
                    for h in range(4):
                        sc.op('dve', (lambda e, h=h: e.tensor_scalar_mul(ogt[:, i4, h * 64:(h + 1) * 64], o3[:, h, 0:64],
                                                                         rc[:, h:h + 1])),
                              reads=[oreg, ('rc', 0)], writes=[('og', h, i4)])
                    if r % 8 == 7:
                        group_norm_store(g, qt * 512, ogt, qt % 2)

            loadq(0)
            emit_qk(units[0])
            for n, u in enumerate(units):
                if n + 1 < len(units):
                    emit_qk(units[n + 1])
                emit_rest(u)

        grp = os.environ.get('KGROUPS', 'ABCD')
        order = [x for x in (('B', 1), ('D', 3), ('A', 0), ('C', 2)) if x[0] in grp]

        def run_group(k, mode):
            nm, g = order[k]
            if nm == 'C':
                group_c(g, k % 2, mode)
            else:
                dense_group(g, nm, k % 2, mode)
        run_group(0, 'load')
        for k in range(len(order)):
            if k + 1 < len(order):
                run_group(k + 1, 'load')
            run_group(k, 'compute')
        sc.barrier()

    def phase_p3(s, l, xsrc, last):
        cv = Carver()
        wout = cv.take([128, 8, DM], BF16)
        wg = cv.take([128, 8, DFF], BF16)
        wu = cv.take([128, 8, DFF], BF16)
        wd = cv.take([128, 22, DM], BF16)
        gffn = cv.take([128, DM], F32)
        gfin = cv.take([128, DM], F32)
        xt = [cv.take([128, DM], F32) for _ in range(2)]
        ont = [cv.take([128, DM], BF16) for _ in range(2)]
        onT = cv.take([128, 8, 128], BF16)
        x1s = [cv.take([128, DM], F32) for _ in range(2)]
        hb2 = cv.take([128, DM], BF16)
        h2T = cv.take([128, 8, 128], BF16)
        sil = [cv.take([128, 512], BF16) for _ in range(2)]
        act = cv.take([128, DFF], BF16)
        actT = cv.take([128, 22, 128], BF16)
        yout = [cv.take([128, DM], F32) for _ in range(2)]
        sss = [cv.take([128, 8], F32) for _ in range(2)]
        sc.dma('sp', 'd_w1', wout, wout_b.ap()[l].rearrange("(k p) n -> p k n", p=128), reads=[('wout_b', l)], writes=['wout'])
        sc.dma('sp', 'd_w1', wg, wg_b.ap()[l].rearrange("(k p) n -> p k n", p=128), reads=[('wg_b', l)], writes=['wg'])
        sc.dma('sp', 'd_w1', wu, wu_b.ap()[l].rearrange("(k p) n -> p k n", p=128), reads=[('wu_b', l)], writes=['wu'])
        sc.dma('sp', 'd_w1', wd, wd_b.ap()[l].rearrange("(k p) n -> p k n", p=128), reads=[('wd_b', l)], writes=['wd'])
        sc.dma('sp', 'd_w1', gffn, rowg_in[4 + l:5 + l, :].partition_broadcast(128), writes=['gffn'])
        sc.dma('sp', 'd_w1', gfin, rowg_in[6:7, :].partition_broadcast(128), writes=['gfin'])
        psT_b = [psT[:, 0:512], psT[:, 512:1024]]
        tb = [0]

        def tbank():
            k = tb[0] % 2
            tb[0] += 1
            return ('psT', k), psT_b[k]

        def load(i):
            sl = i % 2
            sc.dma('sp', 'd_x%d' % sl, xt[sl], xsrc[i * 128:(i + 1) * 128, :], reads=[('xres', s, i)], writes=[('xt', sl)])
            sc.dma('sp', 'd_x%d' % sl, ont[sl], oN.ap()[i * 128:(i + 1) * 128, :], reads=[('oN', s)], writes=[('ont', sl)])

        def transposes(srcfn, n, dst, dreg, sreads):
            for k0 in range(0, n, 4):
                kk = min(4, n - k0)
                bT = bank()
                treg, tap = ('ps', bT), PS(bT)

                def tr(e, k0=k0, kk=kk, tap=tap):
                    for q in range(kk):
                        ins = e.matmul(tap[:, q * 128:(q + 1) * 128], srcfn(k0 + q), ident[:], start=True, stop=True)
                    return ins
                sc.op('pe', tr, reads=sreads + ['ident'], writes=[treg])
                dv = dst[:, k0:k0 + kk, :].rearrange("p k t -> p (k t)")
                if (k0 // 4) % 2 == 0:
                    sc.op('dve', (lambda e, dv=dv, tap=tap, kk=kk: e.tensor_copy(dv, tap[:, 0:kk * 128])), reads=[treg],
                          writes=[(dreg, k0)])
                else:
                    sc.op('act', (lambda e, dv=dv, tap=tap, kk=kk: e.copy(dv, tap[:, 0:kk * 128])), reads=[treg],
                          writes=[(dreg, k0)])

        def front1(i):
            sl = i % 2
            if i + 1 < 32:
                load(i + 1)
            X = xt[sl]
            ON = ont[sl]
            x1 = x1s[sl]
            ss = sss[sl]
            transposes(lambda k: ON[:, k * 128:(k + 1) * 128], 8, onT, 'onT', [('ont', sl)])
            for half in range(2):
                b = bank()

                def fo(e, half=half, b=b):
                    for c in range(8):
                        ins = e.matmul(PS(b), onT[:, c, :], wout[:, c, half * 512:(half + 1) * 512], start=(c == 0), stop=(c == 7))
                    return ins
                sc.op('pe', fo, reads=[('onT', 0), ('onT', 4), 'wout'], writes=[('ps', b)])
                sc.op('dve', (lambda e, half=half, b=b: e.tensor_tensor(x1[:, half * 512:(half + 1) * 512], PS(b),
                                                                        X[:, half * 512:(half + 1) * 512], ALU.add)),
                      reads=[('ps', b), ('xt', sl)], writes=[('x1', sl, half)])
            sc.op('act', (lambda e: e.activation(hb2, x1, AF.Square, accum_out=ss[:, 0:1])), reads=[('x1', sl, 0), ('x1', sl, 1)],
                  writes=['hb2', ('ss0', sl)])
            sc.op('act', (lambda e: e.activation(ss[:, 1:2], ss[:, 0:1], AF.Sqrt, bias=epsb[:, 0:1], scale=1.0 / DM)),
                  reads=[('ss0', sl), 'epsb'], writes=[('ss1', sl)])
            sc.op('dve', (lambda e: e.reciprocal(ss[:, 1:2], ss[:, 1:2])), reads=[('ss1', sl)], writes=[('ss1r', sl)])
            sc.op('dve', (lambda e: e.scalar_tensor_tensor(hb2, x1, ss[:, 1:2], gffn, ALU.mult, ALU.mult)),
                  reads=[('x1', sl, 0), ('x1', sl, 1), ('ss1r', sl), 'gffn'], writes=['hb2'])

        def front2(i):
            sl = i % 2
            transposes(lambda k: hb2[:, k * 128:(k + 1) * 128], 8, h2T, 'h2T', ['hb2'])
            for fi, f0 in enumerate(range(0, DFF, 512)):
                w = min(512, DFF - f0)
                bg, bu = bank(), bank()

                def fg(e, f0=f0, w=w, bg=bg):
                    for kc in range(8):
                        ins = e.matmul(PS(bg, cols=w), h2T[:, kc, :], wg[:, kc, f0:f0 + w], start=(kc == 0), stop=(kc == 7))
                    return ins

                def fu(e, f0=f0, w=w, bu=bu):
                    for kc in range(8):
                        ins = e.matmul(PS(bu, cols=w), h2T[:, kc, :], wu[:, kc, f0:f0 + w], start=(kc == 0), stop=(kc == 7))
                    return ins
                sc.op('pe', fg, reads=[('h2T', 0), ('h2T', 4), 'wg'], writes=[('ps', bg)])
                sc.op('pe', fu, reads=[('h2T', 0), ('h2T', 4), 'wu'], writes=[('ps', bu)])
                u = fi % 2
                sc.op('act', (lambda e, w=w, bg=bg, u=u: e.activation(sil[u][:, 0:w], PS(bg, cols=w), AF.Silu)),
                      reads=[('ps', bg)], writes=[('sil', u)])
                sc.op('dve', (lambda e, f0=f0, w=w, bu=bu, u=u: e.tensor_tensor(act[:, f0:f0 + w], PS(bu, cols=w), sil[u][:, 0:w],
                                                                               ALU.mult)),
                      reads=[('ps', bu), ('sil', u)], writes=[('act', fi)])

        def back(i):
            sl = i % 2
            x1 = x1s[sl]
            ss = sss[sl]
            Y = yout[sl]
            transposes(lambda k: act[:, k * 128:(k + 1) * 128], 22, actT, 'actT', [('act', fi) for fi in range(6)])
            bd = [bank(), bank()]
            for k0 in range(0, 22, 4):
                def fd(e, k0=k0):
                    for f in range(k0, min(22, k0 + 4)):
                        for half in range(2):
                            ins = e.matmul(PS(bd[half]), actT[:, f, :], wd[:, f, half * 512:(half + 1) * 512], start=(f == 0),
                                           stop=(f == 21))
                    return ins
                sc.op('pe', fd, reads=[('actT', k0), 'wd'], writes=[('ps', bd[0]), ('ps', bd[1])])
            for half in range(2):
                b = bd[half]
                sc.op('dve', (lambda e, half=half, b=b: e.tensor_tensor(Y[:, half * 512:(half + 1) * 512], PS(b),
                                                                        x1[:, half * 512:(half + 1) * 512], ALU.add)),
                      reads=[('ps', b), ('x1', sl, half)], writes=[('yout', sl, half)])
            if last:
                sc.op('act', (lambda e: e.activation(act[:, 0:DM], Y, AF.Square, accum_out=ss[:, 2:3])),
                      reads=[('yout', sl, 0), ('yout', sl, 1)], writes=[('act', fi) for fi in range(2)] + [('ss2', sl)])
                sc.op('act', (lambda e: e.activation(ss[:, 3:4], ss[:, 2:3], AF.Sqrt, bias=epsb[:, 0:1], scale=1.0 / DM)),
                      reads=[('ss2', sl), 'epsb'], writes=[('ss3', sl)])
                sc.op('dve', (lambda e: e.reciprocal(ss[:, 3:4], ss[:, 3:4])), reads=[('ss3', sl)], writes=[('ss3r', sl)])
                sc.op('dve', (lambda e: e.scalar_tensor_tensor(Y, Y, ss[:, 3:4], gfin, ALU.mult, ALU.mult)),
                      reads=[('yout', sl, 0), ('yout', sl, 1), ('ss3r', sl), 'gfin'], writes=[('yout', sl, 0), ('yout', sl, 1)])
            sc.dma('sp', 'd_y%d' % sl, y_out[s][i * 128:(i + 1) * 128, :], Y, reads=[('yout', sl, 0), ('yout', sl, 1)],
                   writes=[('xres', s, i)])

        load(0)
        front1(0)
        front2(0)
        for i in range(32):
            if i + 1 < 32:
                front1(i + 1)
            back(i)
            if i + 1 < 32:
                front2(i + 1)
        sc.barrier()

    phases = os.environ.get('KPH', 'ac123')
    nlayers = int(os.environ.get('KLAYERS', '2'))
    if 'a' in phases:
        setup_amask()
    for l in range(nlayers):
        if 'c' in phases:
            setup_cmask(l)
        for s in range(nseq):
            xsrc = x_in[s] if l == 0 else y_out[s]
            phase_p1(s, l, xsrc)
            if '2' in phases:
                phase_p2(s, l)
            if '3' in phases:
                phase_p3(s, l, xsrc, last=(l == 1))
    if dbg:
        cvd = Carver()
        dst_ = [cvd.take([128, 16384], BF16) for _ in range(2)]
        di = 0
        srcs = dict(fmAC=fmAC, fmB=fmB, fmDq=fmDq, fmDk=fmDk, vA=vA, vB=vB, vC=vC, vD=vD, oN=oN, toeA=toeA, toeC=toeC)
        for nm, shp, dt in dbg:
            src_ = srcs[nm]
            fac = 2 if dt == F32 else 1
            if len(shp) == 3:
                views = [(src_.ap()[c], dbg_out[nm].ap()[c], shp[1], shp[2]) for c in range(shp[0])]
            else:
                nj = shp[0] // 128
                jc = max(1, 16384 // shp[1])
                views = [(src_.ap()[j0 * 128:min(nj, j0 + jc) * 128].rearrange("(j p) w -> p j w", p=128),
                          dbg_out[nm].ap()[j0 * 128:min(nj, j0 + jc) * 128].rearrange("(j p) w -> p j w", p=128),
                          128, (min(nj, j0 + jc) - j0) * shp[1]) for j0 in range(0, nj, jc)]
            for (sv, dv, rows, n) in views:
                st = dst_[di % 2][0:rows, 0:n * fac]
                if dt == F32:
                    st = st.bitcast(F32)[:, 0:n]
                if len(shp) == 2:
                    st = st.rearrange("p (j w) -> p j w", w=shp[1])
                sc.dma('sp', 'd_dbgi%d' % (di % 2), st, sv, reads=[('scr', nm)], writes=[('dbgst', di % 2)])
                sc.dma('sp', 'd_dbgo%d' % (di % 2), dv, st, reads=[('dbgst', di % 2)], writes=[('dbg', nm)])
                di += 1
    sc.barrier()
    sc.emit()
    return nc


def prep_shared(inp):
    f = lambda a: np.ascontiguousarray(np.asarray(a, dtype=np.float32))
    w_in = f(inp['w_in'])
    d = {}
    d['wext'] = _gather_cols(w_in, _wext_cols())
    uq = f(inp['d_w_uq'])
    swc = np.array([h * 96 + (dd if dd < 64 else 64 + ((dd - 64) ^ 16)) for h in range(4) for dd in range(96)])
    d['uq'] = np.ascontiguousarray(np.concatenate([uq, uq[:, :, swc]], axis=-1))
    ukv = f(inp['d_w_ukv'])
    kc_ = np.array([h * 128 + dd for h in range(4) for dd in range(64)])
    vc_ = np.array([h * 128 + 64 + dd for h in range(4) for dd in range(64)])
    d['ukv'] = np.ascontiguousarray(np.concatenate([ukv[:, :, kc_], ukv[:, :, vc_]], axis=-1))
    d['wout'] = f(inp['w_out'])
    d['wg'] = f(inp['w_gate'])
    d['wu'] = f(inp['w_up'])
    d['wd'] = f(inp['w_down'])
    d['rowg'] = np.ascontiguousarray(np.concatenate([f(inp['norm_mix']), f(inp['out_gain']), f(inp['norm_ffn']),
                                                     f(inp['final_norm'])[None, :]], axis=0))
    sw = np.array([dd ^ 16 for dd in range(64)])
    colc = np.zeros((2, 128, 8), np.float32)
    bq, bk = f(inp['b_q_gain']), f(inp['b_k_gain'])
    dqg, dkvg = f(inp['d_q_gain']), f(inp['d_kv_gain'])
    for l in range(2):
        colc[l, :, 0] = np.tile(bq[l], 2)
        colc[l, :, 1] = np.tile(bq[l][sw], 2)
        colc[l, :, 2] = np.tile(bk[l], 2)
        colc[l, :, 3] = np.tile(bk[l][sw], 2)
        colc[l, :, 4] = dqg[l][0:128]
        colc[l, :, 5] = dqg[l][128:256]
        colc[l, :, 6] = dkvg[l]
    d['colc'] = colc
    d['t5'] = f(inp['t5_bias'])
    d['rpbf'] = np.ascontiguousarray(f(inp['c_rpb'])[:, :, :, ::-1].reshape(2, 60, 31))
    d.update(_consts())
    return d


_NC_CACHE = {}


def kernel(**inputs):
    sh = prep_shared(inputs)
    xp = np.asarray(inputs['x_prompt'], dtype=np.float32)
    xs = np.asarray(inputs['x_sample'], dtype=np.float32)
    xall = np.concatenate([xp, xs], axis=0)
    nseq = xall.shape[0] // NCORES
    if nseq not in _NC_CACHE:
        _NC_CACHE[nseq] = build(nseq)
    nc = _NC_CACHE[nseq]
    in_maps = []
    for c in range(NCORES):
        m = dict(sh)
        m['x'] = np.ascontiguousarray(xall[c * nseq:(c + 1) * nseq])
        in_maps.append(m)
    res = run_bass_kernel_spmd(nc, in_maps, core_ids=list(range(NCORES)))
    y = np.concatenate([np.asarray(r['y'], dtype=np.float32) for r in res.results], axis=0)
    return (np.ascontiguousarray(y[:xp.shape[0]]), np.ascontiguousarray(y[xp.shape[0]:]))
```

```python
import os
from contextlib import ExitStack
import numpy as np
import ml_dtypes
import concourse.bass as bass
import concourse.mybir as mybir
from concourse.bass_utils import run_bass_kernel_spmd

F32, BF16 = mybir.dt.float32, mybir.dt.bfloat16
AF = mybir.ActivationFunctionType
ALU = mybir.AluOpType
AX = mybir.AxisListType

S = 4096
DM = 1024
DFF = 2816
NCOL = 3008
EPS = 1e-6
STOP = int(os.environ.get('KSTOP', '99'))
KV = int(os.environ.get('KV', '15'))
KDUP = int(os.environ.get('KDUP', '0'))
DEFER = int(os.environ.get('KDEFER', '1'))
NCORES = 8
ENG = ('pe', 'act', 'dve', 'pool', 'sp')


class Sched:
    def __init__(self, nc):
        self.nc = nc
        self.streams = {e: [] for e in ENG}
        self.cnt = {e: 0 for e in ENG}
        self.dcnt = {}
        self.lastw = {}
        self.readers = {}
        self.known = {e: {} for e in ENG}
        self.vc = {e: {} for e in ENG}

    def _need(self, eng, tok):
        kind, key, val = tok
        if kind == 'eng':
            if key == 'pe' and eng == 'pe':
                return
            sem = 'E_' + key
        else:
            sem = key
            val = self.dcnt[key]
        if self.known[eng].get(sem, 0) >= val:
            return
        self.known[eng][sem] = val
        self.streams[eng].append(('wait', sem, val))
        if kind == 'eng':
            snap = self.vc[key].get(val)
            if snap is not None:
                kn = self.known[eng]
                for x, c in zip(ENG, snap):
                    if c > kn.get('E_' + x, 0):
                        kn['E_' + x] = c

    def _deps(self, eng, reads, writes):
        for r in reads:
            t = self.lastw.get(r)
            if t is not None:
                self._need(eng, t)
        for w in writes:
            t = self.lastw.get(w)
            if t is not None:
                self._need(eng, t)
            for k, v in self.readers.get(w, {}).items():
                self._need(eng, (k[0], k[1], v))

    def _commit(self, tok, reads, writes):
        for r in reads:
            d = self.readers.setdefault(r, {})
            k = (tok[0], tok[1])
            if d.get(k, 0) < tok[2]:
                d[k] = tok[2]
        for w in writes:
            self.lastw[w] = tok
            self.readers[w] = {}

    def op(self, eng, fn, reads=(), writes=()):
        self._deps(eng, reads, writes)
        self.cnt[eng] += 1
        tok = ('eng', eng, self.cnt[eng])
        self.streams[eng].append(('op', fn))
        kn = self.known[eng]
        self.vc[eng][self.cnt[eng]] = tuple(kn.get('E_' + x, 0) for x in ENG)
        self._commit(tok, reads, writes)

    def dma(self, q, sem, out, in_, reads=(), writes=()):
        self._deps(q, reads, writes)
        self.dcnt[sem] = self.dcnt.get(sem, 0) + 16
        tok = ('dma', sem, self.dcnt[sem])
        self.streams[q].append(('dma', out, in_, sem))
        self._commit(tok, reads, writes)

    def barrier(self):
        for e in ENG:
            for x in ENG:
                if self.cnt[x] > 0 and not (x == 'pe' and e == 'pe'):
                    self._need(e, ('eng', x, self.cnt[x]))
            for s in list(self.dcnt.keys()):
                self._need(e, ('dma', s, 0))

    def emit(self):
        nc = self.nc
        names = ['E_' + e for e in ENG] + list(self.dcnt.keys())
        with ExitStack() as es:
            semh = {n: es.enter_context(nc.semaphore(n)) for n in names}
            block = es.enter_context(nc.Block())

            def run(e, name):
                esem = semh['E_' + name]
                for it in self.streams[name]:
                    if it[0] == 'wait':
                        e.wait_ge(semh[it[1]], it[2])
                    elif it[0] == 'op':
                        it[1](e).then_inc(esem, 1)
                    else:
                        e.dma_start(out=it[1], in_=it[2]).then_inc(semh[it[3]], 16)

            @block.tensor
            def _(e):
                run(e, 'pe')

            @block.scalar
            def _(e):
                run(e, 'act')

            @block.vector
            def _(e):
                run(e, 'dve')

            @block.gpsimd
            def _(e):
                run(e, 'pool')

            @block.sync
            def _(e):
                run(e, 'sp')


OFF = dict(aq=0, ak=256, av=512, bq=768, bk=1024, bv=1152, cq=1280, ck=1536, cv=1792, dq=2048, dkv=2304,
           dkr=2432)


def _wext_cols():
    cols = []
    for nm in ('aq', 'ak', 'cq', 'ck'):
        cols += list(range(OFF[nm], OFF[nm] + 256))
    sw = [d ^ 16 for d in range(64)]
    for pair in ((0, 2), (1, 3)):
        for h in pair:
            cols += [OFF['bq'] + h * 64 + d for d in range(64)]
    for pair in ((0, 2), (1, 3)):
        for h in pair:
            cols += [OFF['bq'] + h * 64 + sw[d] for d in range(64)]
    for h in (0, 1):
        cols += [OFF['bk'] + h * 64 + d for d in range(64)]
    for h in (0, 1):
        cols += [OFF['bk'] + h * 64 + sw[d] for d in range(64)]
    cols += list(range(OFF['dq'], OFF['dq'] + 256))
    cols += list(range(OFF['dkv'], OFF['dkv'] + 128))
    cols += [-1] * 64 + [OFF['dkr'] + d for d in range(32)]
    cols += [-1] * 64 + [OFF['dkr'] + (d ^ 16) for d in range(32)]
    cols += list(range(OFF['av'], OFF['av'] + 256))
    cols += list(range(OFF['bv'], OFF['bv'] + 128))
    cols += list(range(OFF['cv'], OFF['cv'] + 256))
    assert len(cols) == NCOL
    return np.array(cols)


def _gather_cols(w, cols):
    out = np.zeros(w.shape[:-1] + (len(cols),), dtype=w.dtype)
    m = cols >= 0
    out[..., m] = w[..., cols[m]]
    return out


def _t5_bucket(rel):
    nb, max_exact = 16, 8
    n = np.abs(rel)
    n_f = np.maximum(n, max_exact).astype(np.float32)
    large = max_exact + (np.log(n_f / np.float32(max_exact)) / np.float32(np.log(1024 / max_exact))
                         * np.float32(nb - max_exact)).astype(np.int32)
    large = np.minimum(large, nb - 1)
    return np.where(rel > 0, nb, 0) + np.where(n < max_exact, n, large)


LG = 3072
UA = 2944


def _consts():
    c = {}
    c['ident'] = np.eye(128, dtype=np.float32).astype(ml_dtypes.bfloat16)
    bo = np.zeros((128, 128), np.float32)
    bo[:64, :64] = 1
    bo[64:, 64:] = 1
    c['blockones'] = bo
    c['allones'] = np.ones((128, 128), np.float32)
    inv = (1.0 / (10000.0 ** (np.arange(0, 32, 2, dtype=np.float32) / np.float32(32)))).astype(np.float32)
    t = np.arange(S)
    cb = np.zeros((128, S), np.float32)
    sb = np.zeros((128, S), np.float32)
    for p in range(128):
        d = p % 64
        i = d % 16
        pos = (t // 64) if d < 32 else (t % 64)
        ang = (pos.astype(np.float32) * inv[i]).astype(np.float32).astype(np.float64)
        sgn = -1.0 if (d % 32) < 16 else 1.0
        cb[p] = np.cos(ang)
        sb[p] = sgn * np.sin(ang)
    cd = np.zeros((128, S), np.float32)
    sd = np.zeros((128, S), np.float32)
    for p in range(64, 96):
        d = p - 64
        i = d % 16
        ang = (t.astype(np.float32) * inv[i]).astype(np.float32).astype(np.float64)
        sgn = -1.0 if d < 16 else 1.0
        cd[p] = np.cos(ang)
        sd[p] = sgn * np.sin(ang)
    c['rope'] = np.stack([cb, sb, cd, sd], 0)
    kk = np.arange(LG)
    off = np.where(kk <= 2943, 1408 - kk, 4480 - kk)
    bk = _t5_bucket(off)
    mult = ((np.abs(off) <= 64).astype(np.float32)
            + ((np.abs(off) <= 256) & (off % 4 == 0)).astype(np.float32)
            + ((np.abs(off) <= 1024) & (off % 16 == 0)).astype(np.float32))
    mult[2944] = 0.0
    oh = np.zeros((32, LG), np.float32)
    oh[bk, np.arange(LG)] = 1.0
    c['a_onehot'] = oh
    c['a_mult'] = np.tile(mult[None, :], (4, 1)).astype(np.float32)
    cols = np.arange(64)
    cs = np.clip(cols - 8, 0, 48)
    kc = np.arange(64)[:, None]
    cv = ((kc >= cs[None, :]) & (kc < cs[None, :] + 16)).astype(np.float32)
    c['c_cv'] = np.concatenate([cv, cv], 0)
    return c


def build(nseq, dbg=None):
    nc = bass.Bass("TRN2", target_bir_lowering=False)
    sc = Sched(nc)

    def din(name, shape, dt=F32):
        return nc.dram_tensor(name, list(shape), dt, kind="ExternalInput")

    def dscr(name, shape, dt=BF16):
        return nc.dram_tensor(name, list(shape), dt, kind="Internal")

    x_in = din("x", [nseq, S, DM]).ap()
    y_out = nc.dram_tensor("y", [nseq, S, DM], F32, kind="ExternalOutput").ap()
    wext_in = din("wext", [2, DM, NCOL]).ap()
    uq_in = din("uq", [2, 256, 768]).ap()
    ukv_in = din("ukv", [2, 128, 512]).ap()
    wout_in = din("wout", [2, DM, DM]).ap()
    wg_in = din("wg", [2, DM, DFF]).ap()
    wu_in = din("wu", [2, DM, DFF]).ap()
    wd_in = din("wd", [2, DFF, DM]).ap()
    rowg_in = din("rowg", [7, DM]).ap()
    colc_in = din("colc", [2, 128, 8]).ap()
    t5_in = din("t5", [32, 4]).ap()
    rpb_in = din("rpbf", [2, 60, 31]).ap()
    ident_in = din("ident", [128, 128], BF16).ap()
    bones_in = din("blockones", [128, 128]).ap()
    aones_in = din("allones", [128, 128]).ap()
    rope_in = din("rope", [4, 128, S]).ap()
    aoh_in = din("a_onehot", [32, LG]).ap()
    amult_in = din("a_mult", [4, LG]).ap()
    ccv_in = din("c_cv", [128, 64]).ap()

    wext_b = dscr("wext_b", [2, DM, NCOL])
    uq_b = dscr("uq_b", [2, 256, 768])
    ukv_b = dscr("ukv_b", [2, 128, 512])
    wout_b = dscr("wout_b", [2, DM, DM])
    wg_b = dscr("wg_b", [2, DM, DFF])
    wu_b = dscr("wu_b", [2, DM, DFF])
    wd_b = dscr("wd_b", [2, DFF, DM])
    fmAC = dscr("fmAC", [8, 128, S])
    fmB = dscr("fmB", [3, 128, S])
    fmDq = dscr("fmDq", [4, 96, S])
    fmDk = dscr("fmDk", [4, 96, S])
    vA = dscr("vA", [S, 260])
    vB = dscr("vB", [S, 130])
    vC = dscr("vC", [S, 260])
    vD = dscr("vD", [S, 260])
    oN = dscr("oN", [S, DM])
    toeA = dscr("toeA", [4, 128, LG], F32)
    toeC = dscr("toeC", [60, 64, 128], F32)
    growA = dscr("growA", [4, LG], F32)
    growC = dscr("growC", [60, 128], F32)

    dbg_out = {}
    if dbg:
        for nm, shp, dt in dbg:
            dbg_out[nm] = nc.dram_tensor("dbg_" + nm, list(shp), dt, kind="ExternalOutput")

    def sb(name, shape, dt):
        return nc.alloc_sbuf_tensor(name, list(shape), dt)

    ident = sb("ident_s", [128, 128], BF16)
    bones = sb("bones_s", [128, 128], F32)
    aones = sb("aones_s", [128, 128], F32)
    colc = sb("colc_s", [128, 2, 8], F32)
    epsb = sb("epsb_s", [128, 1], F32)
    ARENA_ELEMS = 105000
    arena = sb("arena", [128, ARENA_ELEMS // 2], F32)
    NBANK = 6
    psA = nc.alloc_psum_tensor("psA", [128, NBANK * 512], F32)
    psT = nc.alloc_psum_tensor("psT", [128, 1024], F32)

    class Carver:
        def __init__(self):
            self.off = 0

        def take(self, shape, dt):
            n = int(np.prod(shape[1:]))
            ne = n * (2 if dt == F32 else 1)
            if self.off % 2:
                self.off += 1
            if ne % 2:
                ne += 1
            v = arena[0:shape[0], self.off // 2:(self.off + ne) // 2]
            self.last_off32 = self.off // 2
            self.off += ne
            assert self.off <= ARENA_ELEMS, (self.off, ARENA_ELEMS)
            if dt == BF16:
                v = v.bitcast(BF16)[:, 0:n]
            if len(shape) == 3:
                v = v.rearrange("p (a b) -> p a b", a=shape[1])
            elif len(shape) == 4:
                v = v.rearrange("p (a b c) -> p a b c", a=shape[1], b=shape[2])
            return v

    bank_rr = [0]

    def bank():
        b = bank_rr[0]
        bank_rr[0] = (b + 1) % 8
        return b

    def PSfull(b):
        return psA[:, b * 512:(b + 1) * 512] if b < NBANK else psT[:, (b - NBANK) * 512:(b - NBANK + 1) * 512]

    def PS(b, rows=128, cols=512, r0=0):
        return PSfull(b)[r0:r0 + rows, 0:cols]

    sc.dma('sp', 'd_c0', ident[:], ident_in, writes=['ident'])
    sc.dma('sp', 'd_c0', bones[:], bones_in, writes=['bones'])
    sc.dma('sp', 'd_c0', aones[:], aones_in, writes=['aones'])
    sc.op('dve', (lambda e: e.memset(epsb[:], EPS)), writes=['epsb'])
    sc.dma('sp', 'd_c0', colc[:], colc_in.rearrange("l p c -> p l c"), writes=['colc'])
    cvw = Carver()
    wstage = [cvw.take([128, 24064], BF16) for _ in range(2)]
    wi = 0
    deferred = []
    for l in range(2):
        for (dst, src_, nm, rows, ncol) in ((wext_b, wext_in, 'wext', DM, NCOL), (uq_b, uq_in, 'uq', 256, 768),
                                            (ukv_b, ukv_in, 'ukv', 128, 512), (wout_b, wout_in, 'wout', DM, DM),
                                            (wg_b, wg_in, 'wg', DM, DFF), (wu_b, wu_in, 'wu', DM, DFF),
                                            (wd_b, wd_in, 'wd', DFF, DM)):
            k = rows // 128
            if DEFER and not (l == 0 and nm in ('wext', 'uq', 'ukv')):
                deferred.append((dst, src_, nm, l, k, ncol))
                continue
            st = wstage[wi % 2][:, 0:k * ncol].rearrange("p (k n) -> p k n", k=k)
            sc.dma('pool', 'd_wc%d' % (wi % 2), st, src_[l].rearrange("(k p) n -> p k n", p=128), writes=[('wstage', wi % 2)])
            sc.dma('sp', 'd_wo%d' % (wi % 2), dst.ap()[l].rearrange("(k p) n -> p k n", p=128), st,
                   reads=[('wstage', wi % 2)], writes=[(nm + '_b', l)])
            wi += 1
    sc.barrier()

    def phase_p1(s, l, xsrc):
        cv = Carver()
        wext = cv.take([128, 8, NCOL], BF16)
        uq = cv.take([128, 2, 768], BF16)
        ukv = cv.take([128, 512], BF16)
        xt = [cv.take([128, 4, DM], F32) for _ in range(2)]
        ropet = [cv.take([128, 4, 512], F32) for _ in range(2)]
        hbs = [cv.take([128, 4, DM], BF16) for _ in range(2)]
        hT = cv.take([128, 8, 512], BF16)
        junk = cv.take([128, DM], BF16)
        ss = cv.take([128, 8], F32)
        sq = [cv.take([128, 512], F32) for _ in range(2)]
        rs = [cv.take([128, 512], F32) for _ in range(2)]
        t1 = [cv.take([128, 512], F32) for _ in range(2)]
        t2 = [cv.take([128, 512], F32) for _ in range(2)]
        dqn = cv.take([128, 2, 512], BF16)
        dkvn = cv.take([128, 512], BF16)
        stAC = [cv.take([128, 8, 512], BF16) for _ in range(2)]
        stB = [cv.take([128, 3, 512], BF16) for _ in range(2)]
        stDq = [cv.take([128, 4, 512], BF16) for _ in range(2)]
        stDk = [cv.take([128, 4, 512], BF16) for _ in range(2)]
        stVA = [cv.take([128, 4, 260], BF16) for _ in range(2)]
        stVB = [cv.take([128, 4, 130], BF16) for _ in range(2)]
        stVC = [cv.take([128, 4, 260], BF16) for _ in range(2)]
        stVD = [cv.take([128, 4, 260], BF16) for _ in range(2)]
        gmix = cv.take([128, DM], F32)
        sc.dma('sp', 'd_wgn', gmix, rowg_in[l:l + 1, :].partition_broadcast(128), writes=[('rowg', l)])

        sc.dma('sp', 'd_w1', wext, wext_b.ap()[l].rearrange("(k p) n -> p k n", p=128),
               reads=[('wext_b', l)] + [('wext_b', l, k_) for k_ in range(8)], writes=['wext'])
        sc.dma('sp', 'd_w1', uq, uq_b.ap()[l].rearrange("(k p) n -> p k n", p=128), reads=[('uq_b', l)] + [('uq_b', l, k_) for k_ in range(2)], writes=['uq'])
        sc.dma('sp', 'd_w1', ukv, ukv_b.ap()[l], reads=[('ukv_b', l)] + [('ukv_b', l, k_) for k_ in range(1)], writes=['ukv'])
        for i in range(2):
            for (st, H, nm) in ((stVA, 4, 'stVA'), (stVB, 2, 'stVB'), (stVC, 4, 'stVC'), (stVD, 4, 'stVD')):
                v = st[i].rearrange("p j (h e) -> p (j h) e", e=65)
                sc.op('pool', (lambda e, v=v: e.memset(v[:, :, 64:65], 1.0)), writes=[(nm, i, 'ones')])

        def load(tt):
            sl = tt % 2
            t0 = tt * 512
            sc.dma('sp', 'd_x%d' % sl, xt[sl], xsrc[t0:t0 + 512, :].rearrange("(j p) d -> p j d", p=128),
                   reads=[('xres', s, 4 * tt + j) for j in range(4)], writes=[('xt', sl)])
            sc.dma('sp', 'd_x%d' % sl, ropet[sl], rope_in[:, :, t0:t0 + 512].rearrange("a p t -> p a t"),
                   writes=[('ropet', sl)])

        load(0)

        def prep(tt):
            sl = tt % 2
            X = xt[sl]
            hb = hbs[sl]
            for j in range(4):
                sc.op('act', (lambda e, j=j: e.activation(junk, X[:, j, :], AF.Square, accum_out=ss[:, j:j + 1])),
                      reads=[('xt', sl)], writes=['junk', ('ssj', j)])
            sc.op('act', (lambda e: e.activation(ss[:, 4:8], ss[:, 0:4], AF.Sqrt, bias=epsb[:, 0:1], scale=1.0 / DM)),
                  reads=[('ssj', j) for j in range(4)] + ['epsb'], writes=['rstd0'])
            sc.op('dve', (lambda e: e.reciprocal(ss[:, 4:8], ss[:, 4:8])), reads=['rstd0'], writes=['rstd'])
            for j in range(4):
                sc.op('dve', (lambda e, j=j: e.scalar_tensor_tensor(hb[:, j, :], X[:, j, :], ss[:, 4 + j:5 + j], gmix,
                                                                   ALU.mult, ALU.mult)),
                      reads=[('xt', sl), 'rstd', ('rowg', l)], writes=[('hb', sl, j)])

        def do_tile(tt):
            sl = tt % 2
            t0 = tt * 512
            if tt + 1 < 8:
                load(tt + 1)
            R = ropet[sl]
            hb = hbs[sl]
            for kc in range(8):
                bT = bank()
                pT = PS(bT)

                def tr(e, kc=kc, pT=pT):
                    for j in range(4):
                        ins = e.matmul(pT[:, j * 128:(j + 1) * 128], hb[:, j, kc * 128:(kc + 1) * 128], ident[:], start=True,
                                       stop=True)
                    return ins
                sc.op('pe', tr, reads=[('hb', sl, j) for j in range(4)] + ['ident'], writes=[('ps', bT)])
                eng = 'act' if kc % 2 == 0 else 'dve'
                if eng == 'act':
                    sc.op('act', (lambda e, kc=kc, pT=pT: e.copy(hT[:, kc, :], pT)), reads=[('ps', bT)], writes=[('hT', kc)])
                else:
                    sc.op('dve', (lambda e, kc=kc, pT=pT: e.tensor_copy(hT[:, kc, :], pT)), reads=[('ps', bT)],
                          writes=[('hT', kc)])
            if tt + 1 < 8:
                prep(tt + 1)
            hT_r = [('hT', kc) for kc in range(8)]

            def fm_chunk(c0, width, b):
                def f(e):
                    for kc in range(8):
                        ins = e.matmul(PS(b, rows=width), wext[:, kc, c0:c0 + width], hT[:, kc, :], start=(kc == 0),
                                       stop=(kc == 7))
                    return ins
                sc.op('pe', f, reads=hT_r + ['wext'], writes=[('ps', b)])

            if STOP <= 2:
                return
            for c in range(8):
                b = bank()
                fm_chunk(c * 128, 128, b)
                sc.op('act', (lambda e, c=c, b=b: e.copy(stAC[sl][:, c, :], PS(b))), reads=[('ps', b)],
                      writes=[('stAC', sl, c)])
            sc.dma('sp', 'd_oAC%d' % sl, fmAC.ap()[:, :, t0:t0 + 512].rearrange("c p t -> p c t"), stAC[sl],
                   reads=[('stAC', sl, c) for c in range(8)], writes=[('fmAC', s)])

            if STOP <= 3:
                return
            for ci, (c_raw, c_sw, gcol) in enumerate(((8, 10, 0), (9, 11, 0), (12, 13, 2))):
                b1, b2, b3 = bank(), bank(), bank()
                u = ci % 2
                fm_chunk(1024 + (c_raw - 8) * 128, 128, b1)
                fm_chunk(1024 + (c_sw - 8) * 128, 128, b2)
                sc.op('act', (lambda e, b1=b1, u=u: e.activation(sq[u], PS(b1), AF.Square)), reads=[('ps', b1)],
                      writes=[('sq', u)])
                sc.op('pe', (lambda e, b3=b3, u=u: e.matmul(PS(b3), bones[:], sq[u], start=True, stop=True)),
                      reads=[('sq', u), 'bones'], writes=[('ps', b3)])
                sc.op('act', (lambda e, b3=b3, u=u: e.activation(rs[u], PS(b3), AF.Sqrt, bias=epsb[:, 0:1], scale=1.0 / 64)),
                      reads=[('ps', b3), 'epsb'], writes=[('rs', u)])
                sc.op('dve', (lambda e, u=u: e.reciprocal(rs[u], rs[u])), reads=[('rs', u)], writes=[('rs', u)])
                sc.op('dve', (lambda e, b1=b1, u=u, gcol=gcol: e.scalar_tensor_tensor(
                    t1[u], PS(b1), colc[:, l, gcol:gcol + 1], R[:, 0, :], ALU.mult, ALU.mult)),
                    reads=[('ps', b1), 'colc', ('ropet', sl)], writes=[('t1', u)])
                sc.op('dve', (lambda e, b2=b2, u=u, gcol=gcol: e.scalar_tensor_tensor(
                    t2[u], PS(b2), colc[:, l, gcol + 1:gcol + 2], R[:, 1, :], ALU.mult, ALU.mult)),
                    reads=[('ps', b2), 'colc', ('ropet', sl)], writes=[('t2', u)])
                sc.op('pool', (lambda e, u=u: e.tensor_tensor(t1[u], t1[u], t2[u], ALU.add)), reads=[('t1', u), ('t2', u)],
                      writes=[('t1', u)])
                sc.op('dve', (lambda e, u=u, ci=ci: e.tensor_tensor(stB[sl][:, ci, :], t1[u], rs[u], ALU.mult)),
                      reads=[('t1', u), ('rs', u)], writes=[('stB', sl, ci)])
            sc.dma('sp', 'd_oB%d' % sl, fmB.ap()[:, :, t0:t0 + 512].rearrange("c p t -> p c t"), stB[sl],
                   reads=[('stB', sl, c) for c in range(3)], writes=[('fmB', s)])

            if STOP <= 4:
                return
            bq0, bq1, bkv, bsq, bskv = bank(), bank(), bank(), bank(), bank()
            fm_chunk(1792, 128, bq0)
            fm_chunk(1920, 128, bq1)
            fm_chunk(2048, 128, bkv)
            sc.op('act', (lambda e: e.activation(sq[0], PS(bq0), AF.Square)), reads=[('ps', bq0)], writes=[('sq', 0)])
            sc.op('act', (lambda e: e.activation(sq[1], PS(bq1), AF.Square)), reads=[('ps', bq1)], writes=[('sq', 1)])

            def ssd(e):
                e.matmul(PS(bsq), aones[:], sq[0], start=True, stop=False)
                return e.matmul(PS(bsq), aones[:], sq[1], start=False, stop=True)
            sc.op('pe', ssd, reads=[('sq', 0), ('sq', 1), 'aones'], writes=[('ps', bsq)])
            sc.op('act', (lambda e: e.activation(rs[0], PS(bsq), AF.Sqrt, bias=epsb[:, 0:1], scale=1.0 / 256)),
                  reads=[('ps', bsq), 'epsb'], writes=[('rs', 0)])
            sc.op('dve', (lambda e: e.reciprocal(rs[0], rs[0])), reads=[('rs', 0)], writes=[('rs', 0)])
            sc.op('dve', (lambda e: e.scalar_tensor_tensor(dqn[:, 0, :], PS(bq0), colc[:, l, 4:5], rs[0], ALU.mult, ALU.mult)),
                  reads=[('ps', bq0), ('rs', 0), 'colc'], writes=[('dqn', 0)])
            sc.op('dve', (lambda e: e.scalar_tensor_tensor(dqn[:, 1, :], PS(bq1), colc[:, l, 5:6], rs[0], ALU.mult, ALU.mult)),
                  reads=[('ps', bq1), ('rs', 0), 'colc'], writes=[('dqn', 1)])
            sc.op('act', (lambda e: e.activation(t2[0], PS(bkv), AF.Square)), reads=[('ps', bkv)], writes=[('t2', 0)])
            sc.op('pe', (lambda e: e.matmul(PS(bskv), aones[:], t2[0], start=True, stop=True)), reads=[('t2', 0), 'aones'],
                  writes=[('ps', bskv)])
            sc.op('act', (lambda e: e.activation(rs[1], PS(bskv), AF.Sqrt, bias=epsb[:, 0:1], scale=1.0 / 128)),
                  reads=[('ps', bskv), 'epsb'], writes=[('rs', 1)])
            sc.op('dve', (lambda e: e.reciprocal(rs[1], rs[1])), reads=[('rs', 1)], writes=[('rs', 1)])
            sc.op('dve', (lambda e: e.scalar_tensor_tensor(dkvn, PS(bkv), colc[:, l, 6:7], rs[1], ALU.mult, ALU.mult)),
                  reads=[('ps', bkv), ('rs', 1), 'colc'], writes=['dkvn'])

            for h in range(4):
                b1, b2 = bank(), bank()

                def fq(e, h=h, b1=b1, o=0):
                    for kc in range(2):
                        ins = e.matmul(PS(b1, rows=96), uq[:, kc, o + h * 96:o + (h + 1) * 96], dqn[:, kc, :], start=(kc == 0),
                                       stop=(kc == 1))
                    return ins

                def fqs(e, h=h, b2=b2, o=384):
                    for kc in range(2):
                        ins = e.matmul(PS(b2, rows=96), uq[:, kc, o + h * 96:o + (h + 1) * 96], dqn[:, kc, :], start=(kc == 0),
                                       stop=(kc == 1))
                    return ins
                sc.op('pe', fq, reads=[('dqn', 0), ('dqn', 1), 'uq'], writes=[('ps', b1)])
                sc.op('pe', fqs, reads=[('dqn', 0), ('dqn', 1), 'uq'], writes=[('ps', b2)])
                sc.op('act', (lambda e, h=h, b1=b1: e.copy(stDq[sl][0:64, h, :], PS(b1, rows=64))), reads=[('ps', b1)],
                      writes=[('stDq', sl, h, 'n')])
                sc.op('dve', (lambda e, b1=b1: e.tensor_tensor(t1[0][64:96, :], PS(b1, rows=32, r0=64), R[64:96, 2, :], ALU.mult)),
                      reads=[('ps', b1), ('ropet', sl)], writes=[('t1', 0)])
                sc.op('dve', (lambda e, b2=b2: e.tensor_tensor(t2[0][64:96, :], PS(b2, rows=32, r0=64), R[64:96, 3, :], ALU.mult)),
                      reads=[('ps', b2), ('ropet', sl)], writes=[('t2', 0)])
                sc.op('dve', (lambda e, h=h: e.tensor_tensor(stDq[sl][64:96, h, :], t1[0][64:96, :], t2[0][64:96, :], ALU.add)),
                      reads=[('t1', 0), ('t2', 0)], writes=[('stDq', sl, h, 'r')])
            sc.dma('sp', 'd_oDq%d' % sl, fmDq.ap()[:, :, t0:t0 + 512].rearrange("h p t -> p h t"), stDq[sl][0:96],
                   reads=[('stDq', sl, h, x) for h in range(4) for x in 'nr'], writes=[('fmDq', s)])
            for h in range(4):
                b1 = bank()
                sc.op('pe', (lambda e, h=h, b1=b1: e.matmul(PS(b1, rows=64), ukv[:, h * 64:(h + 1) * 64], dkvn, start=True,
                                                           stop=True)), reads=['dkvn', 'ukv'], writes=[('ps', b1)])
                sc.op('act', (lambda e, h=h, b1=b1: e.copy(stDk[sl][0:64, h, :], PS(b1, rows=64))), reads=[('ps', b1)],
                      writes=[('stDk', sl, h, 'n')])
            b1, b2 = bank(), bank()
            fm_chunk(2176, 96, b1)
            fm_chunk(2272, 96, b2)
            sc.op('dve', (lambda e, b1=b1: e.tensor_tensor(t1[1][64:96, :], PS(b1, rows=32, r0=64), R[64:96, 2, :], ALU.mult)),
                  reads=[('ps', b1), ('ropet', sl)], writes=[('t1', 1)])
            sc.op('dve', (lambda e, b2=b2: e.tensor_tensor(t2[1][64:96, :], PS(b2, rows=32, r0=64), R[64:96, 3, :], ALU.mult)),
                  reads=[('ps', b2), ('ropet', sl)], writes=[('t2', 1)])
            for h in range(4):
                sc.op('dve', (lambda e, h=h: e.tensor_tensor(stDk[sl][64:96, h, :], t1[1][64:96, :], t2[1][64:96, :], ALU.add)),
                      reads=[('t1', 1), ('t2', 1)], writes=[('stDk', sl, h, 'r')])
            sc.dma('sp', 'd_oDk%d' % sl, fmDk.ap()[:, :, t0:t0 + 512].rearrange("h p t -> p h t"), stDk[sl][0:96],
                   reads=[('stDk', sl, h, x) for h in range(4) for x in 'nr'], writes=[('fmDk', s)])

            if STOP <= 5:
                return
            for j in range(4):
                b1, b2, b3 = bank(), bank(), bank()

                def fv(e, j=j, b=b1, c0=2368, w=384):
                    for kc in range(8):
                        ins = e.matmul(PS(b, cols=w), hT[:, kc, j * 128:(j + 1) * 128], wext[:, kc, c0:c0 + w], start=(kc == 0),
                                       stop=(kc == 7))
                    return ins

                def fv2(e, j=j, b=b2, c0=2752, w=256):
                    for kc in range(8):
                        ins = e.matmul(PS(b, cols=w), hT[:, kc, j * 128:(j + 1) * 128], wext[:, kc, c0:c0 + w], start=(kc == 0),
                                       stop=(kc == 7))
                    return ins
                sc.op('pe', fv, reads=hT_r + ['wext'], writes=[('ps', b1)])
                sc.op('pe', fv2, reads=hT_r + ['wext'], writes=[('ps', b2)])
                sc.op('pe', (lambda e, j=j, b3=b3: e.matmul(PS(b3, cols=256), dkvn[:, j * 128:(j + 1) * 128], ukv[:, 256:512],
                                                           start=True, stop=True)), reads=['dkvn', 'ukv'], writes=[('ps', b3)])

                def v65(st, j, H):
                    return st[sl][:, j, :].rearrange("p (h e) -> p h e", e=65)[:, :, 0:64]

                def p64(b, c0, H):
                    return PSfull(b)[:, c0:c0 + H * 64].rearrange("p (h d) -> p h d", d=64)
                if KV & 4:
                  sc.op('act', (lambda e, j=j, b1=b1: e.copy(v65(stVA, j, 4), p64(b1, 0, 4))), reads=[('ps', b1)],
                      writes=[('stVA', sl, j)])
                if KV & 8:
                  sc.op('act', (lambda e, j=j, b1=b1: e.copy(v65(stVB, j, 2), p64(b1, 256, 2))), reads=[('ps', b1)],
                      writes=[('stVB', sl, j)])
                if KV & 4:
                  sc.op('act', (lambda e, j=j, b2=b2: e.copy(v65(stVC, j, 4), p64(b2, 0, 4))), reads=[('ps', b2)],
                      writes=[('stVC', sl, j)])
                if KV & 8:
                  sc.op('act', (lambda e, j=j, b3=b3: e.copy(v65(stVD, j, 4), p64(b3, 0, 4))), reads=[('ps', b3)],
                      writes=[('stVD', sl, j)])
            if STOP <= 6:
                return
            for (st, dst, nm, w) in ((stVA, vA, 'stVA', 260), (stVB, vB, 'stVB', 130), (stVC, vC, 'stVC', 260),
                                     (stVD, vD, 'stVD', 260)):
                sc.dma('sp', 'd_o%s%d' % (nm, sl), dst.ap()[t0:t0 + 512, :].rearrange("(j p) w -> p j w", p=128), st[sl],
                       reads=[(nm, sl, j) for j in range(4)] + [(nm, sl, 'ones')], writes=[(nm[2:], s)])
        prep(0)
        for tt in range(8):
            do_tile(tt)
        sc.barrier()

    def setup_amask():
        cv = Carver()
        t5s = cv.take([128, 128], F32)
        aoh = cv.take([128, LG], F32)
        amu = cv.take([4, LG], F32)
        gr = cv.take([4, LG], F32)
        repa = cv.take([128, 4, LG], F32)
        sc.op('dve', (lambda e: e.memset(t5s, 0.0)), writes=['t5s'])
        sc.op('pool', (lambda e: e.memset(aoh, 0.0)), writes=['aoh'])
        sc.dma('sp', 'd_m0', t5s[0:32, 0:4], t5_in, writes=['t5s'])
        sc.dma('sp', 'd_m0', aoh[0:32, :], aoh_in, writes=['aoh'])
        sc.dma('sp', 'd_m0', amu, amult_in, writes=['amu'])
        for c0 in range(0, LG, 512):
            w = 512
            b = bank()
            sc.op('pe', (lambda e, b=b, c0=c0, w=w: e.matmul(PS(b, cols=w), t5s, aoh[:, c0:c0 + w], start=True, stop=True)),
                  reads=['t5s', 'aoh'], writes=[('ps', b)])
            sc.op('act', (lambda e, b=b, c0=c0, w=w: e.activation(gr[:, c0:c0 + w], PS(b, rows=4, cols=w), AF.Exp)),
                  reads=[('ps', b)], writes=[('gr', c0)])
            sc.op('dve', (lambda e, c0=c0, w=w: e.tensor_tensor(gr[:, c0:c0 + w], gr[:, c0:c0 + w], amu[:, c0:c0 + w], ALU.mult)),
                  reads=[('gr', c0), 'amu'], writes=[('gr', c0)])
        sc.dma('sp', 'd_m1', growA.ap(), gr, reads=[('gr', c0) for c0 in range(0, LG, 512)], writes=['growA'])
        sc.dma('sp', 'd_m1', repa, growA.ap().rearrange("(o h) n -> o h n", o=1).partition_broadcast(128) if False else
               bass.AP(growA, 0, [[0, 128], [LG, 4], [1, LG]]), reads=['growA'], writes=['repa'])
        sc.dma('sp', 'd_m1', toeA.ap().rearrange("h p n -> p h n"), repa, reads=['repa'], writes=['toeA'])
        sc.barrier()

    def setup_cmask(l):
        cv = Carver()
        e60 = cv.take([60, 32], F32)
        rpad = cv.take([60, 128], F32)
        repc = cv.take([64, 60, 128], F32)
        sc.dma('sp', 'd_m0', e60[:, 0:31], rpb_in[l], writes=['e60'])
        sc.op('dve', (lambda e: e.memset(rpad, 0.0)), writes=['rpad'])
        sc.op('act', (lambda e: e.activation(rpad[:, 0:16], e60[:, 15:31], AF.Exp)), reads=['e60', 'rpad'], writes=['rpad'])
        sc.op('act', (lambda e: e.activation(rpad[:, 113:128], e60[:, 0:15], AF.Exp)), reads=['e60', 'rpad'], writes=['rpad'])
        sc.dma('sp', 'd_m1', growC.ap(), rpad, reads=['rpad'], writes=['growC'])
        sc.dma('sp', 'd_m1', repc, bass.AP(growC, 0, [[0, 64], [128, 60], [1, 128]]), reads=['growC'], writes=['repc'])
        sc.dma('sp', 'd_m1', toeC.ap().rearrange("r p n -> p r n"), repc, reads=['repc'], writes=['toeC'])
        sc.barrier()

    def phase_p2(s, l):
        cv = Carver()
        KTs = [cv.take([128, 4, S], BF16) for _ in range(2)]
        Ves = [cv.take([128, 32, 260], BF16) for _ in range(2)]
        Vo = cv.take([128, 32, 260], BF16)
        Qt = [cv.take([128, 4, 512], BF16) for _ in range(2)]
        PT = [cv.take([128, 2, 512], BF16) for _ in range(3)]
        strip = cv.take([128, 4, UA], BF16)
        cm32 = cv.take([128, 14, 4, 64], F32)
        cmask = cv.take([128, 14, 4, 64], BF16)
        ccv = cv.take([128, 64], F32)
        gout = cv.take([128, DM], F32)
        og = [cv.take([128, 4, 256], F32) for _ in range(2)]
        onb = [cv.take([128, 4, 256], BF16) for _ in range(2)]
        junk = cv.take([128, 256], BF16)
        rc = cv.take([128, 16], F32)
        ssn = cv.take([128, 8], F32)
        sc.dma('sp', 'd_w1', gout, rowg_in[2 + l:3 + l, :].partition_broadcast(128), writes=['gout'])
        if deferred and s == 0 and l == 0:
            cst = [cv.take([128, NCOL], BF16) for _ in range(2)]
            ji = 0
            for (dst, src_, nm, l_, k, ncol) in deferred:
                for k_ in range(k):
                    slot = ji % 2
                    sc.dma('pool', 'd_cj%d' % slot, cst[slot][:, 0:ncol], src_[l_][k_ * 128:(k_ + 1) * 128, :], writes=[('cst', slot)])
                    sc.dma('pool', 'd_cjo%d' % slot, dst.ap()[l_][k_ * 128:(k_ + 1) * 128, :], cst[slot][:, 0:ncol],
                           reads=[('cst', slot)], writes=[(nm + '_b', l_, k_)])
                    ji += 1
            del deferred[:]
        psT_b = [psT[:, 0:512], psT[:, 512:1024]]
        obanks = [(('ps', 4), PS(4)), (('ps', 5), PS(5)), (('psT', 0), psT_b[0]), (('psT', 1), psT_b[1])]
        pt_rr = [0]
        ep_rr = [0]

        def epilogue_head(oreg, oap, ogt, h, nj, u):
            o3 = oap[:, 0:nj * 65].rearrange("p (j e) -> p j e", e=65)
            sc.op('dve', (lambda e: e.reciprocal(rc[:, u * 4:u * 4 + nj], o3[:, :, 64])), reads=[oreg], writes=[('rc', u)])
            for j in range(nj):
                sc.op('dve', (lambda e, j=j: e.tensor_scalar_mul(ogt[:, j, h * 64:(h + 1) * 64], o3[:, j, 0:64],
                                                                 rc[:, u * 4 + j:u * 4 + j + 1])),
                      reads=[oreg, ('rc', u)], writes=[('og', h, j)])

        def group_norm_store(g, t0, ogt, slot):
            for j in range(4):
                sc.op('act', (lambda e, j=j: e.activation(junk, ogt[:, j, :], AF.Square, accum_out=ssn[:, j:j + 1])),
                      reads=[('og', h, j) for h in range(4)], writes=['junk2', ('ssn', j)])
            sc.op('act', (lambda e: e.activation(ssn[:, 4:8], ssn[:, 0:4], AF.Ln, bias=epsb[:, 0:1], scale=1.0 / 256)),
                  reads=[('ssn', j) for j in range(4)] + ['epsb'], writes=['ssr0'])
            sc.op('act', (lambda e: e.activation(ssn[:, 4:8], ssn[:, 4:8], AF.Exp, scale=-0.5)), reads=['ssr0'], writes=['ssr'])
            for j in range(4):
                sc.op('dve', (lambda e, j=j: e.scalar_tensor_tensor(onb[slot][:, j, :], ogt[:, j, :], ssn[:, 4 + j:5 + j],
                                                                    gout[:, g * 256:(g + 1) * 256], ALU.mult, ALU.mult)),
                      reads=[('og', h, j) for h in range(4)] + ['ssr', 'gout'], writes=[('onb', slot, j)])
            sc.dma('sp', 'd_on%d' % slot, oN.ap()[t0:t0 + 512, g * 256:(g + 1) * 256].rearrange("(j p) w -> p j w", p=128),
                   onb[slot], reads=[('onb', slot, j) for j in range(4)], writes=[('oN', s)])

        def dense_group(g, name, bi, mode):
            KT, Ve = KTs[bi], Ves[bi]
            kreg, vreg_ = ('KT', bi), ('Ve', bi)
            nchunk = {'A': 2, 'B': 1, 'D': 4}[name]
            scale = (96.0 if name == 'D' else 64.0) ** -0.5
            if mode == 'load':
                if name == 'A':
                    sc.dma('sp', 'd_k%d' % bi, KT[:, 0:2, :], fmAC.ap()[2:4].rearrange("c p t -> p c t"), reads=[('fmAC', s)], writes=[kreg])
                    vsrc, vw, nq = vA, 260, 2
                    for h in range(4):
                        sc.dma('pool', 'd_strip', strip[:, h, :], bass.AP(toeA, h * 128 * LG, [[LG - 1, 128], [1, UA]]), reads=['toeA'],
                               writes=['strip'])
                elif name == 'B':
                    sc.dma('sp', 'd_k%d' % bi, KT[:, 0:1, :], fmB.ap()[2:3].rearrange("c p t -> p c t"), reads=[('fmB', s)], writes=[kreg])
                    vsrc, vw, nq = vB, 130, 2
                else:
                    sc.dma('sp', 'd_k%d' % bi, KT[0:96, 0:4, :], fmDk.ap().rearrange("h p t -> p h t"), reads=[('fmDk', s)], writes=[kreg])
                    vsrc, vw, nq = vD, 260, 4
                sc.dma('sp', 'd_k%d' % bi, Ve[:, :, 0:vw], vsrc.ap().rearrange("(j p) w -> p j w", p=128), reads=[(vsrc.name if False else name + 'v', s)] if False else [({'A': 'VA', 'B': 'VB', 'D': 'VD'}[name], s)], writes=[vreg_])

                return

            def loadq(qt):
                sl = qt % 2
                t0 = qt * 512
                if name == 'A':
                    sc.dma('sp', 'd_q%d' % sl, Qt[sl][:, 0:2, :], fmAC.ap()[0:2, :, t0:t0 + 512].rearrange("c p t -> p c t"),
                           reads=[('fmAC', s)], writes=[('Qt', sl)])
                elif name == 'B':
                    sc.dma('sp', 'd_q%d' % sl, Qt[sl][:, 0:2, :], fmB.ap()[0:2, :, t0:t0 + 512].rearrange("c p t -> p c t"),
                           reads=[('fmB', s)], writes=[('Qt', sl)])
                else:
                    sc.dma('sp', 'd_q%d' % sl, Qt[sl][0:96, 0:4, :], fmDq.ap()[:, :, t0:t0 + 512].rearrange("h p t -> p h t"),
                           reads=[('fmDq', s)], writes=[('Qt', sl)])

            def kbs_of(qt):
                t0 = qt * 512
                if name == 'A':
                    return [kb for kb in range(32) if (kb * 128 + 127 >= t0 - 1024) and (kb * 128 <= t0 + 511 + 1024)]
                return list(range(32))

            def views(qt, pr):
                sl = qt % 2
                if name == 'A':
                    heads = (2 * pr, 2 * pr + 1)
                    qv = [Qt[sl][0:64, pr, :], Qt[sl][64:128, pr, :]]
                    kv = [KT[0:64, pr, :], KT[64:128, pr, :]]
                    vh = heads
                elif name == 'B':
                    heads = (pr, pr + 2)
                    qv = [Qt[sl][0:64, pr, :], Qt[sl][64:128, pr, :]]
                    kv = [KT[0:64, 0, :], KT[64:128, 0, :]]
                    vh = (0, 1)
                else:
                    heads = (2 * pr, 2 * pr + 1)
                    qv = [Qt[sl][0:96, heads[0], :], Qt[sl][0:96, heads[1], :]]
                    kv = [KT[0:96, heads[0], :], KT[0:96, heads[1], :]]
                    vh = heads
                return heads, qv, kv, vh

            units = []
            for qt in range(8):
                ks = kbs_of(qt)
                for pr in range(2):
                    for ki, kb in enumerate(ks):
                        units.append(dict(qt=qt, pr=pr, ki=ki, kb=kb, nk=len(ks), idx=len(units)))

            def emit_qk(u):
                qt, pr, kb = u['qt'], u['pr'], u['kb']
                sl = qt % 2
                if pr == 0 and u['ki'] == 0 and qt + 1 < 8:
                    loadq(qt + 1)
                heads, qv, kv, vh = views(qt, pr)
                sb_ = (u['idx'] % 3) * 2

                def qk(e, kb=kb, sb_=sb_, qv=qv, kv=kv):
                    for _ in range(1 + KDUP):
                        e.matmul(PS(sb_), kv[0][:, kb * 128:(kb + 1) * 128], qv[0], start=True, stop=True)
                        ins = e.matmul(PS(sb_ + 1), kv[1][:, kb * 128:(kb + 1) * 128], qv[1], start=True, stop=True)
                    return ins
                sc.op('pe', qk, reads=[kreg, ('Qt', sl)], writes=[('ps', sb_), ('ps', sb_ + 1)])

            def emit_rest(u):
                qt, pr, kb, ki, nk = u['qt'], u['pr'], u['kb'], u['ki'], u['nk']
                t0 = qt * 512
                heads, qv, kv, vh = views(qt, pr)
                sb_ = (u['idx'] % 3) * 2
                pslot = u['idx'] % 3
                P_ = PT[pslot]
                ob = [obanks[2], obanks[3]]
                ogt = og[qt % 2]
                sc.op('act', (lambda e: e.activation(P_.rearrange("p a n -> p (a n)"), psA[:, sb_ * 512:(sb_ + 2) * 512], AF.Exp,
                                                     scale=scale)),
                      reads=[('ps', sb_), ('ps', sb_ + 1)], writes=[('PT', pslot, 0), ('PT', pslot, 1)])
                if name == 'A':
                    off = 1408 - (kb * 128 - t0)
                    h0 = heads[0]
                    sc.op('dve', (lambda e: e.tensor_tensor(P_, P_, strip[:, h0:h0 + 2, off:off + 512], ALU.mult)),
                          reads=[('PT', pslot, 0), ('PT', pslot, 1), 'strip'], writes=[('PT', pslot, 0), ('PT', pslot, 1)])
                first, last = (ki == 0), (ki == nk - 1)

                def pv(e):
                    for i in range(2):
                        for j in range(4):
                            ins = e.matmul(ob[i][1][:, j * 65:(j + 1) * 65], P_[:, i, j * 128:(j + 1) * 128],
                                           Ve[:, kb, vh[i] * 65:(vh[i] + 1) * 65], start=(first and j == 0), stop=last,
                                           skip_group_check=True)
                    return ins
                sc.op('pe', pv, reads=[('PT', pslot, 0), ('PT', pslot, 1), vreg_], writes=[ob[0][0], ob[1][0]])
                if last:
                    for i in range(2):
                        epilogue_head(ob[i][0], ob[i][1], ogt, heads[i], 4, i)
                    if pr == 1:
                        group_norm_store(g, t0, ogt, qt % 2)

            loadq(0)
            emit_qk(units[0])
            emit_qk(units[1])
            for n, u in enumerate(units):
                if n + 2 < len(units):
                    emit_qk(units[n + 2])
                emit_rest(u)

        def group_c(g, bi, mode):
            KT, Ve = KTs[bi], Ves[bi]
            kreg, vreg_ = ('KT', bi), ('Ve', bi)
            if mode == 'load':
                sc.dma('sp', 'd_k%d' % bi, KT[:, 0:2, :], fmAC.ap()[6:8].rearrange("c p t -> p c t"), reads=[('fmAC', s)], writes=[kreg])
                sc.dma('sp', 'd_k%d' % bi, Ve, vC.ap().rearrange("(j p) w -> p j w", p=128), reads=[('VC', s)], writes=[vreg_])
                sc.dma('sp', 'd_kc', Vo[:, 0:31, :], vC.ap()[64:64 + 31 * 128, :].rearrange("(j p) w -> p j w", p=128), reads=[('VC', s)],
                       writes=['Vo'])
                sc.dma('sp', 'd_kc', ccv, ccv_in, writes=['ccv'])
                for pos, h in enumerate((0, 2, 1, 3)):
                    for a_ in range(2):
                        sc.dma('sp', 'd_kc', cm32[a_ * 64:(a_ + 1) * 64, :, pos, :],
                               bass.AP(toeC, (h * 15 + a_) * 8192, [[127, 64], [8192, 14], [1, 64]]), reads=['toeC'], writes=['cm32'])
                for m in range(14):
                    for pos in range(4):
                        sc.op('dve', (lambda e, m=m, pos=pos: e.tensor_tensor(cmask[:, m, pos, :], cm32[:, m, pos, :], ccv, ALU.mult)),
                              reads=['cm32', 'ccv'], writes=['cmask'])

                return

            def loadq(qt):
                sl = qt % 2
                t0 = qt * 512
                sc.dma('sp', 'd_q%d' % sl, Qt[sl][:, 0:2, :], fmAC.ap()[4:6, :, t0:t0 + 512].rearrange("c p t -> p c t"),
                       reads=[('fmAC', s)], writes=[('Qt', sl)])

            units = []
            for r in range(64):
                for jw in range(4):
                    units.append(dict(r=r, jw=jw, idx=len(units)))

            def emit_qk(u):
                r, jw = u['r'], u['jw']
                qt = r // 8
                sl = qt % 2
                if r % 8 == 0 and jw == 0 and qt + 1 < 8:
                    loadq(qt + 1)
                rs_ = min(max(r - 4, 0), 56)
                kw = rs_ * 64 + 128 * jw
                sb_ = (u['idx'] % 2) * 2
                qc = (r % 8) * 64

                def qk(e):
                    for h in range(4):
                        c, lo = h // 2, 64 * (h % 2)
                        bnk = sb_ + (h % 2)
                        ins = e.matmul(psA[:, bnk * 512 + (h // 2) * 64: bnk * 512 + (h // 2) * 64 + 64],
                                       KT[lo:lo + 64, c, kw:kw + 128], Qt[sl][lo:lo + 64, c, qc:qc + 64], start=True, stop=True)
                    return ins
                sc.op('pe', qk, reads=[kreg, ('Qt', sl)], writes=[('ps', sb_), ('ps', sb_ + 1)])

            def emit_rest(u):
                r, jw = u['r'], u['jw']
                qt = r // 8
                ogt = og[qt % 2]
                i4 = (r % 8) // 2
                half = r % 2
                rs_ = min(max(r - 4, 0), 56)
                oreg, oap = obanks[(r // 2) % 4]
                kw = rs_ * 64 + 128 * jw
                m = rs_ + 2 * jw - r + 7
                sb_ = (u['idx'] % 2) * 2
                pslot = u['idx'] % 3
                P_ = PT[pslot]
                pout = P_[:, :, 0:128]
                pin = psA[:, sb_ * 512:(sb_ + 2) * 512].rearrange("p (b n) -> p b n", b=2)[:, :, 0:128]
                sc.op('act', (lambda e: e.activation(pout, pin, AF.Exp, scale=0.125)),
                      reads=[('ps', sb_), ('ps', sb_ + 1)], writes=[('PT', pslot, 0), ('PT', pslot, 1)])
                mk = cmask[:, m, :, :].rearrange("p (b u) c -> p b (u c)", b=2)
                sc.op('dve', (lambda e: e.tensor_tensor(pout, pout, mk, ALU.mult)),
                      reads=[('PT', pslot, 0), ('PT', pslot, 1), 'cmask'], writes=[('PT', pslot, 0), ('PT', pslot, 1)])
                if kw % 128 == 0:
                    vt, vi, vreg = Ve, kw // 128, vreg_
                else:
                    vt, vi, vreg = Vo, (kw - 64) // 128, 'Vo'
                first, last = (jw == 0), (jw == 3)

                def pv(e):
                    for h in range(4):
                        b_, u_ = h % 2, h // 2
                        ins = e.matmul(oap[half * 64:half * 64 + 64, h * 65:(h + 1) * 65],
                                       P_[:, b_, u_ * 64:u_ * 64 + 64], vt[:, vi, h * 65:(h + 1) * 65], start=(first and h == 0),
                                       stop=last, skip_group_check=True)
                    return ins
                sc.op('pe', pv, reads=[('PT', pslot, 0), ('PT', pslot, 1), vreg], writes=[oreg])
                if half == 1 and last:
                    o3 = oap[:, 0:260].rearrange("p (h e) -> p h e", e=65)
                    sc.op('dve', (lambda e: e.reciprocal(rc[:, 0:4], o3[:, :, 64])), reads=[oreg], writes=[('rc', 0)])
                    for h in range(4):
                        sc.op('dve', (lambda e, h=h: e.tensor_scalar_mul(ogt[:, i4, h * 64:(h + 1) * 64], o3[:, h, 0:64],
                                                                         rc[:, h:h + 1])),
                              reads=[oreg, ('rc', 0)], writes=[('og', h, i4)])
                    if r % 8 == 7:
                        group_norm_store(g, qt * 512, ogt, qt % 2)

            loadq(0)
            emit_qk(units[0])
            for n, u in enumerate(units):
                if n + 1 < len(units):
                    emit_qk(units[n + 1])
                emit_rest(u)

        grp = os.environ.get('KGROUPS', 'ABCD')
        order = [x for x in (('B', 1), ('D', 3), ('A', 0), ('C', 2)) if x[0] in grp]

        def run_group(k, mode):
            nm, g = order[k]
            if nm == 'C':
                group_c(g, k % 2, mode)
            else:
                dense_group(g, nm, k % 2, mode)
        run_group(0, 'load')
        for k in range(len(order)):
            if k + 1 < len(order):
                run_group(k + 1, 'load')
            run_group(k, 'compute')
        sc.barrier()

    def phase_p3(s, l, xsrc, last):
        cv = Carver()
        wout = cv.take([128, 8, DM], BF16)
        wg = cv.take([128, 8, DFF], BF16)
        wu = cv.take([128, 8, DFF], BF16)
        wd = cv.take([128, 22, DM], BF16)
        gffn = cv.take([128, DM], F32)
        gfin = cv.take([128, DM], F32)
        xt = [cv.take([128, DM], F32) for _ in range(2)]
        ont = [cv.take([128, DM], BF16) for _ in range(2)]
        onT = cv.take([128, 8, 128], BF16)
        x1s = [cv.take([128, DM], F32) for _ in range(2)]
        hb2 = cv.take([128, DM], BF16)
        h2T = cv.take([128, 8, 128], BF16)
        sil = [cv.take([128, 512], BF16) for _ in range(2)]
        act = cv.take([128, DFF], BF16)
        actT = cv.take([128, 22, 128], BF16)
        yout = [cv.take([128, DM], F32) for _ in range(2)]
        sss = [cv.take([128, 8], F32) for _ in range(2)]
        sc.dma('sp', 'd_wo_', wout, wout_b.ap()[l].rearrange("(k p) n -> p k n", p=128), reads=[('wout_b', l)] + [('wout_b', l, k_) for k_ in range(8)], writes=['wout'])
        sc.dma('sp', 'd_wg_', wg, wg_b.ap()[l].rearrange("(k p) n -> p k n", p=128), reads=[('wg_b', l)] + [('wg_b', l, k_) for k_ in range(8)], writes=['wg'])
        sc.dma('sp', 'd_wu_', wu, wu_b.ap()[l].rearrange("(k p) n -> p k n", p=128), reads=[('wu_b', l)] + [('wu_b', l, k_) for k_ in range(8)], writes=['wu'])
        sc.dma('sp', 'd_wd_', wd, wd_b.ap()[l].rearrange("(k p) n -> p k n", p=128), reads=[('wd_b', l)] + [('wd_b', l, k_) for k_ in range(22)], writes=['wd'])
        sc.dma('sp', 'd_wgn', gffn, rowg_in[4 + l:5 + l, :].partition_broadcast(128), writes=['gffn'])
        sc.dma('sp', 'd_wgn', gfin, rowg_in[6:7, :].partition_broadcast(128), writes=['gfin'])
        psT_b = [psT[:, 0:512], psT[:, 512:1024]]
        tb = [0]

        def tbank():
            k = tb[0] % 2
            tb[0] += 1
            return ('psT', k), psT_b[k]

        def load(i):
            sl = i % 2
            sc.dma('sp', 'd_x%d' % sl, xt[sl], xsrc[i * 128:(i + 1) * 128, :], reads=[('xres', s, i)], writes=[('xt', sl)])
            sc.dma('sp', 'd_x%d' % sl, ont[sl], oN.ap()[i * 128:(i + 1) * 128, :], reads=[('oN', s)], writes=[('ont', sl)])

        def transposes(srcfn, n, dst, dreg, sreads):
            for k0 in range(0, n, 4):
                kk = min(4, n - k0)
                bT = bank()
                treg, tap = ('ps', bT), PS(bT)

                def tr(e, k0=k0, kk=kk, tap=tap):
                    for q in range(kk):
                        ins = e.matmul(tap[:, q * 128:(q + 1) * 128], srcfn(k0 + q), ident[:], start=True, stop=True)
                    return ins
                sc.op('pe', tr, reads=sreads + ['ident'], writes=[treg])
                dv = dst[:, k0:k0 + kk, :].rearrange("p k t -> p (k t)")
                if (k0 // 4) % 2 == 0:
                    sc.op('dve', (lambda e, dv=dv, tap=tap, kk=kk: e.tensor_copy(dv, tap[:, 0:kk * 128])), reads=[treg],
                          writes=[(dreg, k0)])
                else:
                    sc.op('act', (lambda e, dv=dv, tap=tap, kk=kk: e.copy(dv, tap[:, 0:kk * 128])), reads=[treg],
                          writes=[(dreg, k0)])

        def front1(i):
            sl = i % 2
            if i + 1 < 32:
                load(i + 1)
            X = xt[sl]
            ON = ont[sl]
            x1 = x1s[sl]
            ss = sss[sl]
            transposes(lambda k: ON[:, k * 128:(k + 1) * 128], 8, onT, 'onT', [('ont', sl)])
            for half in range(2):
                b = bank()

                def fo(e, half=half, b=b):
                    for c in range(8):
                        ins = e.matmul(PS(b), onT[:, c, :], wout[:, c, half * 512:(half + 1) * 512], start=(c == 0), stop=(c == 7))
                    return ins
                sc.op('pe', fo, reads=[('onT', 0), ('onT', 4), 'wout'], writes=[('ps', b)])
                sc.op('dve', (lambda e, half=half, b=b: e.tensor_tensor(x1[:, half * 512:(half + 1) * 512], PS(b),
                                                                        X[:, half * 512:(half + 1) * 512], ALU.add)),
                      reads=[('ps', b), ('xt', sl)], writes=[('x1', sl, half)])
            sc.op('act', (lambda e: e.activation(hb2, x1, AF.Square, accum_out=ss[:, 0:1])), reads=[('x1', sl, 0), ('x1', sl, 1)],
                  writes=['hb2', ('ss0', sl)])
            sc.op('act', (lambda e: e.activation(ss[:, 1:2], ss[:, 0:1], AF.Sqrt, bias=epsb[:, 0:1], scale=1.0 / DM)),
                  reads=[('ss0', sl), 'epsb'], writes=[('ss1', sl)])
            sc.op('dve', (lambda e: e.reciprocal(ss[:, 1:2], ss[:, 1:2])), reads=[('ss1', sl)], writes=[('ss1r', sl)])
            sc.op('dve', (lambda e: e.scalar_tensor_tensor(hb2, x1, ss[:, 1:2], gffn, ALU.mult, ALU.mult)),
                  reads=[('x1', sl, 0), ('x1', sl, 1), ('ss1r', sl), 'gffn'], writes=['hb2'])

        def front2(i):
            sl = i % 2
            transposes(lambda k: hb2[:, k * 128:(k + 1) * 128], 8, h2T, 'h2T', ['hb2'])
            for fi, f0 in enumerate(range(0, DFF, 512)):
                w = min(512, DFF - f0)
                bg, bu = bank(), bank()

                def fg(e, f0=f0, w=w, bg=bg):
                    for kc in range(8):
                        ins = e.matmul(PS(bg, cols=w), h2T[:, kc, :], wg[:, kc, f0:f0 + w], start=(kc == 0), stop=(kc == 7))
                    return ins

                def fu(e, f0=f0, w=w, bu=bu):
                    for kc in range(8):
                        ins = e.matmul(PS(bu, cols=w), h2T[:, kc, :], wu[:, kc, f0:f0 + w], start=(kc == 0), stop=(kc == 7))
                    return ins
                sc.op('pe', fg, reads=[('h2T', 0), ('h2T', 4), 'wg'], writes=[('ps', bg)])
                sc.op('pe', fu, reads=[('h2T', 0), ('h2T', 4), 'wu'], writes=[('ps', bu)])
                u = fi % 2
                sc.op('act', (lambda e, w=w, bg=bg, u=u: e.activation(sil[u][:, 0:w], PS(bg, cols=w), AF.Silu)),
                      reads=[('ps', bg)], writes=[('sil', u)])
                sc.op('dve', (lambda e, f0=f0, w=w, bu=bu, u=u: e.tensor_tensor(act[:, f0:f0 + w], PS(bu, cols=w), sil[u][:, 0:w],
                                                                               ALU.mult)),
                      reads=[('ps', bu), ('sil', u)], writes=[('act', fi)])

        def back(i):
            sl = i % 2
            x1 = x1s[sl]
            ss = sss[sl]
            Y = yout[sl]
            transposes(lambda k: act[:, k * 128:(k + 1) * 128], 22, actT, 'actT', [('act', fi) for fi in range(6)])
            bd = [bank(), bank()]
            for k0 in range(0, 22, 4):
                def fd(e, k0=k0):
                    for f in range(k0, min(22, k0 + 4)):
                        for half in range(2):
                            ins = e.matmul(PS(bd[half]), actT[:, f, :], wd[:, f, half * 512:(half + 1) * 512], start=(f == 0),
                                           stop=(f == 21))
                    return ins
                sc.op('pe', fd, reads=[('actT', k0), 'wd'], writes=[('ps', bd[0]), ('ps', bd[1])])
            for half in range(2):
                b = bd[half]
                sc.op('dve', (lambda e, half=half, b=b: e.tensor_tensor(Y[:, half * 512:(half + 1) * 512], PS(b),
                                                                        x1[:, half * 512:(half + 1) * 512], ALU.add)),
                      reads=[('ps', b), ('x1', sl, half)], writes=[('yout', sl, half)])
            if last:
                sc.op('act', (lambda e: e.activation(act[:, 0:DM], Y, AF.Square, accum_out=ss[:, 2:3])),
                      reads=[('yout', sl, 0), ('yout', sl, 1)], writes=[('act', fi) for fi in range(2)] + [('ss2', sl)])
                sc.op('act', (lambda e: e.activation(ss[:, 3:4], ss[:, 2:3], AF.Sqrt, bias=epsb[:, 0:1], scale=1.0 / DM)),
                      reads=[('ss2', sl), 'epsb'], writes=[('ss3', sl)])
                sc.op('dve', (lambda e: e.reciprocal(ss[:, 3:4], ss[:, 3:4])), reads=[('ss3', sl)], writes=[('ss3r', sl)])
                sc.op('dve', (lambda e: e.scalar_tensor_tensor(Y, Y, ss[:, 3:4], gfin, ALU.mult, ALU.mult)),
                      reads=[('yout', sl, 0), ('yout', sl, 1), ('ss3r', sl), 'gfin'], writes=[('yout', sl, 0), ('yout', sl, 1)])
            sc.dma('sp', 'd_y%d' % sl, y_out[s][i * 128:(i + 1) * 128, :], Y, reads=[('yout', sl, 0), ('yout', sl, 1)],
                   writes=[('xres', s, i)])

        load(0)
        front1(0)
        front2(0)
        for i in range(32):
            if i + 1 < 32:
                front1(i + 1)
            back(i)
            if i + 1 < 32:
                front2(i + 1)
        sc.barrier()

    phases = os.environ.get('KPH', 'ac123')
    nlayers = int(os.environ.get('KLAYERS', '2'))
    if 'a' in phases:
        setup_amask()
    for l in range(nlayers):
        if 'c' in phases:
            setup_cmask(l)
        for s in range(nseq):
            xsrc = x_in[s] if l == 0 else y_out[s]
            phase_p1(s, l, xsrc)
            if '2' in phases:
                phase_p2(s, l)
            if '3' in phases:
                phase_p3(s, l, xsrc, last=(l == 1))
    if dbg:
        cvd = Carver()
        dst_ = [cvd.take([128, 16384], BF16) for _ in range(2)]
        di = 0
        srcs = dict(fmAC=fmAC, fmB=fmB, fmDq=fmDq, fmDk=fmDk, vA=vA, vB=vB, vC=vC, vD=vD, oN=oN, toeA=toeA, toeC=toeC)
        for nm, shp, dt in dbg:
            src_ = srcs[nm]
            fac = 2 if dt == F32 else 1
            if len(shp) == 3:
                views = [(src_.ap()[c], dbg_out[nm].ap()[c], shp[1], shp[2]) for c in range(shp[0])]
            else:
                nj = shp[0] // 128
                jc = max(1, 16384 // shp[1])
                views = [(src_.ap()[j0 * 128:min(nj, j0 + jc) * 128].rearrange("(j p) w -> p j w", p=128),
                          dbg_out[nm].ap()[j0 * 128:min(nj, j0 + jc) * 128].rearrange("(j p) w -> p j w", p=128),
                          128, (min(nj, j0 + jc) - j0) * shp[1]) for j0 in range(0, nj, jc)]
            for (sv, dv, rows, n) in views:
                st = dst_[di % 2][0:rows, 0:n * fac]
                if dt == F32:
                    st = st.bitcast(F32)[:, 0:n]
                if len(shp) == 2:
                    st = st.rearrange("p (j w) -> p j w", w=shp[1])
                sc.dma('sp', 'd_dbgi%d' % (di % 2), st, sv, reads=[('scr', nm)], writes=[('dbgst', di % 2)])
                sc.dma('sp', 'd_dbgo%d' % (di % 2), dv, st, reads=[('dbgst', di % 2)], writes=[('dbg', nm)])
                di += 1
    sc.barrier()
    sc.emit()
    return nc


def prep_shared(inp):
    f = lambda a: np.ascontiguousarray(np.asarray(a, dtype=np.float32))
    w_in = f(inp['w_in'])
    d = {}
    d['wext'] = _gather_cols(w_in, _wext_cols())
    uq = f(inp['d_w_uq'])
    swc = np.array([h * 96 + (dd if dd < 64 else 64 + ((dd - 64) ^ 16)) for h in range(4) for dd in range(96)])
    d['uq'] = np.ascontiguousarray(np.concatenate([uq, uq[:, :, swc]], axis=-1))
    ukv = f(inp['d_w_ukv'])
    kc_ = np.array([h * 128 + dd for h in range(4) for dd in range(64)])
    vc_ = np.array([h * 128 + 64 + dd for h in range(4) for dd in range(64)])
    d['ukv'] = np.ascontiguousarray(np.concatenate([ukv[:, :, kc_], ukv[:, :, vc_]], axis=-1))
    d['wout'] = f(inp['w_out'])
    d['wg'] = f(inp['w_gate'])
    d['wu'] = f(inp['w_up'])
    d['wd'] = f(inp['w_down'])
    d['rowg'] = np.ascontiguousarray(np.concatenate([f(inp['norm_mix']), f(inp['out_gain']), f(inp['norm_ffn']),
                                                     f(inp['final_norm'])[None, :]], axis=0))
    sw = np.array([dd ^ 16 for dd in range(64)])
    colc = np.zeros((2, 128, 8), np.float32)
    bq, bk = f(inp['b_q_gain']), f(inp['b_k_gain'])
    dqg, dkvg = f(inp['d_q_gain']), f(inp['d_kv_gain'])
    for l in range(2):
        colc[l, :, 0] = np.tile(bq[l], 2)
        colc[l, :, 1] = np.tile(bq[l][sw], 2)
        colc[l, :, 2] = np.tile(bk[l], 2)
        colc[l, :, 3] = np.tile(bk[l][sw], 2)
        colc[l, :, 4] = dqg[l][0:128]
        colc[l, :, 5] = dqg[l][128:256]
        colc[l, :, 6] = dkvg[l]
    d['colc'] = colc
    d['t5'] = f(inp['t5_bias'])
    d['rpbf'] = np.ascontiguousarray(f(inp['c_rpb'])[:, :, :, ::-1].reshape(2, 60, 31))
    d.update(_consts())
    return d


_NC_CACHE = {}


def kernel(**inputs):
    sh = prep_shared(inputs)
    xp = np.asarray(inputs['x_prompt'], dtype=np.float32)
    xs = np.asarray(inputs['x_sample'], dtype=np.float32)
    xall = np.concatenate([xp, xs], axis=0)
    nseq = xall.shape[0] // NCORES
    if nseq not in _NC_CACHE:
        _NC_CACHE[nseq] = build(nseq)
    nc = _NC_CACHE[nseq]
    in_maps = []
    for c in range(NCORES):
        m = dict(sh)
        m['x'] = np.ascontiguousarray(xall[c * nseq:(c + 1) * nseq])
        in_maps.append(m)
    res = run_bass_kernel_spmd(nc, in_maps, core_ids=list(range(NCORES)))
    y = np.concatenate([np.asarray(r['y'], dtype=np.float32) for r in res.results], axis=0)
    return (np.ascontiguousarray(y[:xp.shape[0]]), np.ascontiguousarray(y[xp.shape[0]:]))
```

```python
import os
from contextlib import ExitStack
import numpy as np
import ml_dtypes
import concourse.bass as bass
import concourse.mybir as mybir
from concourse.bass_utils import run_bass_kernel_spmd

F32, BF16 = mybir.dt.float32, mybir.dt.bfloat16
AF = mybir.ActivationFunctionType
ALU = mybir.AluOpType
AX = mybir.AxisListType

S = 4096
DM = 1024
DFF = 2816
NCOL = 3008
EPS = 1e-6
STOP = int(os.environ.get('KSTOP', '99'))
KV = int(os.environ.get('KV', '15'))
KDUP = int(os.environ.get('KDUP', '0'))
DEFER = int(os.environ.get('KDEFER', '1'))
NCORES = 8
ENG = ('pe', 'act', 'dve', 'pool', 'sp')


class Sched:
    def __init__(self, nc):
        self.nc = nc
        self.streams = {e: [] for e in ENG}
        self.cnt = {e: 0 for e in ENG}
        self.dcnt = {}
        self.lastw = {}
        self.readers = {}
        self.known = {e: {} for e in ENG}
        self.vc = {e: {} for e in ENG}

    def _need(self, eng, tok):
        kind, key, val = tok
        if kind == 'eng':
            if key == 'pe' and eng == 'pe':
                return
            sem = 'E_' + key
        else:
            sem = key
            val = self.dcnt[key]
        if self.known[eng].get(sem, 0) >= val:
            return
        self.known[eng][sem] = val
        self.streams[eng].append(('wait', sem, val))
        if kind == 'eng':
            snap = self.vc[key].get(val)
            if snap is not None:
                kn = self.known[eng]
                for x, c in zip(ENG, snap):
                    if c > kn.get('E_' + x, 0):
                        kn['E_' + x] = c

    def _deps(self, eng, reads, writes):
        for r in reads:
            t = self.lastw.get(r)
            if t is not None:
                self._need(eng, t)
        for w in writes:
            t = self.lastw.get(w)
            if t is not None:
                self._need(eng, t)
            for k, v in self.readers.get(w, {}).items():
                self._need(eng, (k[0], k[1], v))

    def _commit(self, tok, reads, writes):
        for r in reads:
            d = self.readers.setdefault(r, {})
            k = (tok[0], tok[1])
            if d.get(k, 0) < tok[2]:
                d[k] = tok[2]
        for w in writes:
            self.lastw[w] = tok
            self.readers[w] = {}

    def op(self, eng, fn, reads=(), writes=()):
        self._deps(eng, reads, writes)
        self.cnt[eng] += 1
        tok = ('eng', eng, self.cnt[eng])
        self.streams[eng].append(('op', fn))
        kn = self.known[eng]
        self.vc[eng][self.cnt[eng]] = tuple(kn.get('E_' + x, 0) for x in ENG)
        self._commit(tok, reads, writes)

    def dma(self, q, sem, out, in_, reads=(), writes=()):
        self._deps(q, reads, writes)
        self.dcnt[sem] = self.dcnt.get(sem, 0) + 16
        tok = ('dma', sem, self.dcnt[sem])
        self.streams[q].append(('dma', out, in_, sem))
        self._commit(tok, reads, writes)

    def barrier(self):
        for e in ENG:
            for x in ENG:
                if self.cnt[x] > 0 and not (x == 'pe' and e == 'pe'):
                    self._need(e, ('eng', x, self.cnt[x]))
            for s in list(self.dcnt.keys()):
                self._need(e, ('dma', s, 0))

    def emit(self):
        nc = self.nc
        names = ['E_' + e for e in ENG] + list(self.dcnt.keys())
        with ExitStack() as es:
            semh = {n: es.enter_context(nc.semaphore(n)) for n in names}
            block = es.enter_context(nc.Block())

            def run(e, name):
                esem = semh['E_' + name]
                for it in self.streams[name]:
                    if it[0] == 'wait':
                        e.wait_ge(semh[it[1]], it[2])
                    elif it[0] == 'op':
                        it[1](e).then_inc(esem, 1)
                    else:
                        e.dma_start(out=it[1], in_=it[2]).then_inc(semh[it[3]], 16)

            @block.tensor
            def _(e):
                run(e, 'pe')

            @block.scalar
            def _(e):
                run(e, 'act')

            @block.vector
            def _(e):
                run(e, 'dve')

            @block.gpsimd
            def _(e):
                run(e, 'pool')

            @block.sync
            def _(e):
                run(e, 'sp')


OFF = dict(aq=0, ak=256, av=512, bq=768, bk=1024, bv=1152, cq=1280, ck=1536, cv=1792, dq=2048, dkv=2304,
           dkr=2432)


def _wext_cols():
    cols = []
    for nm in ('aq', 'ak', 'cq', 'ck'):
        cols += list(range(OFF[nm], OFF[nm] + 256))
    sw = [d ^ 16 for d in range(64)]
    for pair in ((0, 2), (1, 3)):
        for h in pair:
            cols += [OFF['bq'] + h * 64 + d for d in range(64)]
    for pair in ((0, 2), (1, 3)):
        for h in pair:
            cols += [OFF['bq'] + h * 64 + sw[d] for d in range(64)]
    for h in (0, 1):
        cols += [OFF['bk'] + h * 64 + d for d in range(64)]
    for h in (0, 1):
        cols += [OFF['bk'] + h * 64 + sw[d] for d in range(64)]
    cols += list(range(OFF['dq'], OFF['dq'] + 256))
    cols += list(range(OFF['dkv'], OFF['dkv'] + 128))
    cols += [-1] * 64 + [OFF['dkr'] + d for d in range(32)]
    cols += [-1] * 64 + [OFF['dkr'] + (d ^ 16) for d in range(32)]
    cols += list(range(OFF['av'], OFF['av'] + 256))
    cols += list(range(OFF['bv'], OFF['bv'] + 128))
    cols += list(range(OFF['cv'], OFF['cv'] + 256))
    assert len(cols) == NCOL
    return np.array(cols)


def _gather_cols(w, cols):
    out = np.zeros(w.shape[:-1] + (len(cols),), dtype=w.dtype)
    m = cols >= 0
    out[..., m] = w[..., cols[m]]
    return out


def _t5_bucket(rel):
    nb, max_exact = 16, 8
    n = np.abs(rel)
    n_f = np.maximum(n, max_exact).astype(np.float32)
    large = max_exact + (np.log(n_f / np.float32(max_exact)) / np.float32(np.log(1024 / max_exact))
                         * np.float32(nb - max_exact)).astype(np.int32)
    large = np.minimum(large, nb - 1)
    return np.where(rel > 0, nb, 0) + np.where(n < max_exact, n, large)


LG = 3072
UA = 2944


def _consts():
    c = {}
    c['ident'] = np.eye(128, dtype=np.float32).astype(ml_dtypes.bfloat16)
    bo = np.zeros((128, 128), np.float32)
    bo[:64, :64] = 1
    bo[64:, 64:] = 1
    c['blockones'] = bo
    c['allones'] = np.ones((128, 128), np.float32)
    inv = (1.0 / (10000.0 ** (np.arange(0, 32, 2, dtype=np.float32) / np.float32(32)))).astype(np.float32)
    t = np.arange(S)
    cb = np.zeros((128, S), np.float32)
    sb = np.zeros((128, S), np.float32)
    for p in range(128):
        d = p % 64
        i = d % 16
        pos = (t // 64) if d < 32 else (t % 64)
        ang = (pos.astype(np.float32) * inv[i]).astype(np.float32).astype(np.float64)
        sgn = -1.0 if (d % 32) < 16 else 1.0
        cb[p] = np.cos(ang)
        sb[p] = sgn * np.sin(ang)
    cd = np.zeros((128, S), np.float32)
    sd = np.zeros((128, S), np.float32)
    for p in range(64, 96):
        d = p - 64
        i = d % 16
        ang = (t.astype(np.float32) * inv[i]).astype(np.float32).astype(np.float64)
        sgn = -1.0 if d < 16 else 1.0
        cd[p] = np.cos(ang)
        sd[p] = sgn * np.sin(ang)
    c['rope'] = np.stack([cb, sb, cd, sd], 0)
    kk = np.arange(LG)
    off = np.where(kk <= 2943, 1408 - kk, 4480 - kk)
    bk = _t5_bucket(off)
    mult = ((np.abs(off) <= 64).astype(np.float32)
            + ((np.abs(off) <= 256) & (off % 4 == 0)).astype(np.float32)
            + ((np.abs(off) <= 1024) & (off % 16 == 0)).astype(np.float32))
    mult[2944] = 0.0
    oh = np.zeros((32, LG), np.float32)
    oh[bk, np.arange(LG)] = 1.0
    c['a_onehot'] = oh
    c['a_mult'] = np.tile(mult[None, :], (4, 1)).astype(np.float32)
    cols = np.arange(64)
    cs = np.clip(cols - 8, 0, 48)
    kc = np.arange(64)[:, None]
    cv = ((kc >= cs[None, :]) & (kc < cs[None, :] + 16)).astype(np.float32)
    c['c_cv'] = np.concatenate([cv, cv], 0)
    return c


def build(nseq, dbg=None):
    nc = bass.Bass("TRN2", target_bir_lowering=False)
    sc = Sched(nc)

    def din(name, shape, dt=F32):
        return nc.dram_tensor(name, list(shape), dt, kind="ExternalInput")

    def dscr(name, shape, dt=BF16):
        return nc.dram_tensor(name, list(shape), dt, kind="Internal")

    x_in = din("x", [nseq, S, DM]).ap()
    y_out = nc.dram_tensor("y", [nseq, S, DM], F32, kind="ExternalOutput").ap()
    wext_in = din("wext", [2, DM, NCOL]).ap()
    uq_in = din("uq", [2, 256, 768]).ap()
    ukv_in = din("ukv", [2, 128, 512]).ap()
    wout_in = din("wout", [2, DM, DM]).ap()
    wg_in = din("wg", [2, DM, DFF]).ap()
    wu_in = din("wu", [2, DM, DFF]).ap()
    wd_in = din("wd", [2, DFF, DM]).ap()
    rowg_in = din("rowg", [7, DM]).ap()
    colc_in = din("colc", [2, 128, 8]).ap()
    t5_in = din("t5", [32, 4]).ap()
    rpb_in = din("rpbf", [2, 60, 31]).ap()
    ident_in = din("ident", [128, 128], BF16).ap()
    bones_in = din("blockones", [128, 128]).ap()
    aones_in = din("allones", [128, 128]).ap()
    rope_in = din("rope", [4, 128, S]).ap()
    aoh_in = din("a_onehot", [32, LG]).ap()
    amult_in = din("a_mult", [4, LG]).ap()
    ccv_in = din("c_cv", [128, 64]).ap()

    wext_b = dscr("wext_b", [2, DM, NCOL])
    uq_b = dscr("uq_b", [2, 256, 768])
    ukv_b = dscr("ukv_b", [2, 128, 512])
    wout_b = dscr("wout_b", [2, DM, DM])
    wg_b = dscr("wg_b", [2, DM, DFF])
    wu_b = dscr("wu_b", [2, DM, DFF])
    wd_b = dscr("wd_b", [2, DFF, DM])
    fmAC = dscr("fmAC", [8, 128, S])
    fmB = dscr("fmB", [3, 128, S])
    fmDq = dscr("fmDq", [4, 96, S])
    fmDk = dscr("fmDk", [4, 96, S])
    vA = dscr("vA", [S, 260])
    vB = dscr("vB", [S, 130])
    vC = dscr("vC", [S, 260])
    vD = dscr("vD", [S, 260])
    oN = dscr("oN", [S, DM])
    toeA = dscr("toeA", [4, 128, LG], F32)
    toeC = dscr("toeC", [60, 64, 128], F32)
    growA = dscr("growA", [4, LG], F32)
    growC = dscr("growC", [60, 128], F32)

    dbg_out = {}
    if dbg:
        for nm, shp, dt in dbg:
            dbg_out[nm] = nc.dram_tensor("dbg_" + nm, list(shp), dt, kind="ExternalOutput")

    def sb(name, shape, dt):
        return nc.alloc_sbuf_tensor(name, list(shape), dt)

    ident = sb("ident_s", [128, 128], BF16)
    bones = sb("bones_s", [128, 128], F32)
    aones = sb("aones_s", [128, 128], F32)
    colc = sb("colc_s", [128, 2, 8], F32)
    epsb = sb("epsb_s", [128, 1], F32)
    ARENA_ELEMS = 105000
    arena = sb("arena", [128, ARENA_ELEMS // 2], F32)
    NBANK = 6
    psA = nc.alloc_psum_tensor("psA", [128, NBANK * 512], F32)
    psT = nc.alloc_psum_tensor("psT", [128, 1024], F32)

    class Carver:
        def __init__(self):
            self.off = 0

        def take(self, shape, dt):
            n = int(np.prod(shape[1:]))
            ne = n * (2 if dt == F32 else 1)
            if self.off % 2:
                self.off += 1
            if ne % 2:
                ne += 1
            v = arena[0:shape[0], self.off // 2:(self.off + ne) // 2]
            self.last_off32 = self.off // 2
            self.off += ne
            assert self.off <= ARENA_ELEMS, (self.off, ARENA_ELEMS)
            if dt == BF16:
                v = v.bitcast(BF16)[:, 0:n]
            if len(shape) == 3:
                v = v.rearrange("p (a b) -> p a b", a=shape[1])
            elif len(shape) == 4:
                v = v.rearrange("p (a b c) -> p a b c", a=shape[1], b=shape[2])
            return v

    bank_rr = [0]

    def bank():
        b = bank_rr[0]
        bank_rr[0] = (b + 1) % 8
        return b

    def PSfull(b):
        return psA[:, b * 512:(b + 1) * 512] if b < NBANK else psT[:, (b - NBANK) * 512:(b - NBANK + 1) * 512]

    def PS(b, rows=128, cols=512, r0=0):
        return PSfull(b)[r0:r0 + rows, 0:cols]

    sc.dma('sp', 'd_c0', ident[:], ident_in, writes=['ident'])
    sc.dma('sp', 'd_c0', bones[:], bones_in, writes=['bones'])
    sc.dma('sp', 'd_c0', aones[:], aones_in, writes=['aones'])
    sc.op('dve', (lambda e: e.memset(epsb[:], EPS)), writes=['epsb'])
    sc.dma('sp', 'd_c0', colc[:], colc_in.rearrange("l p c -> p l c"), writes=['colc'])
    cvw = Carver()
    wstage = [cvw.take([128, 24064], BF16) for _ in range(2)]
    wi = 0
    deferred = []
    for l in range(2):
        for (dst, src_, nm, rows, ncol) in ((wext_b, wext_in, 'wext', DM, NCOL), (uq_b, uq_in, 'uq', 256, 768),
                                            (ukv_b, ukv_in, 'ukv', 128, 512), (wout_b, wout_in, 'wout', DM, DM),
                                            (wg_b, wg_in, 'wg', DM, DFF), (wu_b, wu_in, 'wu', DM, DFF),
                                            (wd_b, wd_in, 'wd', DFF, DM)):
            k = rows // 128
            if DEFER and not (l == 0 and nm in ('wext', 'uq', 'ukv')):
                deferred.append((dst, src_, nm, l, k, ncol))
                continue
            st = wstage[wi % 2][:, 0:k * ncol].rearrange("p (k n) -> p k n", k=k)
            sc.dma('pool', 'd_wc%d' % (wi % 2), st, src_[l].rearrange("(k p) n -> p k n", p=128), writes=[('wstage', wi % 2)])
            sc.dma('sp', 'd_wo%d' % (wi % 2), dst.ap()[l].rearrange("(k p) n -> p k n", p=128), st,
                   reads=[('wstage', wi % 2)], writes=[(nm + '_b', l)])
            wi += 1
    sc.barrier()

    def phase_p1(s, l, xsrc):
        cv = Carver()
        wext = cv.take([128, 8, NCOL], BF16)
        uq = cv.take([128, 2, 768], BF16)
        ukv = cv.take([128, 512], BF16)
        xt = [cv.take([128, 4, DM], F32) for _ in range(2)]
        ropet = [cv.take([128, 4, 512], F32) for _ in range(2)]
        hbs = [cv.take([128, 4, DM], BF16) for _ in range(2)]
        hT = cv.take([128, 8, 512], BF16)
        junk = cv.take([128, DM], BF16)
        ss = cv.take([128, 8], F32)
        sq = [cv.take([128, 512], F32) for _ in range(2)]
        rs = [cv.take([128, 512], F32) for _ in range(2)]
        t1 = [cv.take([128, 512], F32) for _ in range(2)]
        t2 = [cv.take([128, 512], F32) for _ in range(2)]
        dqn = cv.take([128, 2, 512], BF16)
        dkvn = cv.take([128, 512], BF16)
        stAC = [cv.take([128, 8, 512], BF16) for _ in range(2)]
        stB = [cv.take([128, 3, 512], BF16) for _ in range(2)]
        stDq = [cv.take([128, 4, 512], BF16) for _ in range(2)]
        stDk = [cv.take([128, 4, 512], BF16) for _ in range(2)]
        stVA = [cv.take([128, 4, 260], BF16) for _ in range(2)]
        stVB = [cv.take([128, 4, 130], BF16) for _ in range(2)]
        stVC = [cv.take([128, 4, 260], BF16) for _ in range(2)]
        stVD = [cv.take([128, 4, 260], BF16) for _ in range(2)]
        gmix = cv.take([128, DM], F32)
        sc.dma('sp', 'd_wgn', gmix, rowg_in[l:l + 1, :].partition_broadcast(128), writes=[('rowg', l)])

        sc.dma('sp', 'd_w1', wext, wext_b.ap()[l].rearrange("(k p) n -> p k n", p=128),
               reads=[('wext_b', l)] + [('wext_b', l, k_) for k_ in range(8)], writes=['wext'])
        sc.dma('sp', 'd_w1', uq, uq_b.ap()[l].rearrange("(k p) n -> p k n", p=128), reads=[('uq_b', l)] + [('uq_b', l, k_) for k_ in range(2)], writes=['uq'])
        sc.dma('sp', 'd_w1', ukv, ukv_b.ap()[l], reads=[('ukv_b', l)] + [('ukv_b', l, k_) for k_ in range(1)], writes=['ukv'])
        for i in range(2):
            for (st, H, nm) in ((stVA, 4, 'stVA'), (stVB, 2, 'stVB'), (stVC, 4, 'stVC'), (stVD, 4, 'stVD')):
                v = st[i].rearrange("p j (h e) -> p (j h) e", e=65)
                sc.op('pool', (lambda e, v=v: e.memset(v[:, :, 64:65], 1.0)), writes=[(nm, i, 'ones')])

        def load(tt):
            sl = tt % 2
            t0 = tt * 512
            sc.dma('sp', 'd_x%d' % sl, xt[sl], xsrc[t0:t0 + 512, :].rearrange("(j p) d -> p j d", p=128),
                   reads=[('xres', s, 4 * tt + j) for j in range(4)], writes=[('xt', sl)])
            sc.dma('sp', 'd_x%d' % sl, ropet[sl], rope_in[:, :, t0:t0 + 512].rearrange("a p t -> p a t"),
                   writes=[('ropet', sl)])

        load(0)

        def prep(tt):
            sl = tt % 2
            X = xt[sl]
            hb = hbs[sl]
            for j in range(4):
                sc.op('act', (lambda e, j=j: e.activation(junk, X[:, j, :], AF.Square, accum_out=ss[:, j:j + 1])),
                      reads=[('xt', sl)], writes=['junk', ('ssj', j)])
            sc.op('act', (lambda e: e.activation(ss[:, 4:8], ss[:, 0:4], AF.Sqrt, bias=epsb[:, 0:1], scale=1.0 / DM)),
                  reads=[('ssj', j) for j in range(4)] + ['epsb'], writes=['rstd0'])
            sc.op('dve', (lambda e: e.reciprocal(ss[:, 4:8], ss[:, 4:8])), reads=['rstd0'], writes=['rstd'])
            for j in range(4):
                sc.op('dve', (lambda e, j=j: e.scalar_tensor_tensor(hb[:, j, :], X[:, j, :], ss[:, 4 + j:5 + j], gmix,
                                                                   ALU.mult, ALU.mult)),
                      reads=[('xt', sl), 'rstd', ('rowg', l)], writes=[('hb', sl, j)])

        def do_tile(tt):
            sl = tt % 2
            t0 = tt * 512
            if tt + 1 < 8:
                load(tt + 1)
            R = ropet[sl]
            hb = hbs[sl]
            for kc in range(8):
                bT = bank()
                pT = PS(bT)

                def tr(e, kc=kc, pT=pT):
                    for j in range(4):
                        ins = e.matmul(pT[:, j * 128:(j + 1) * 128], hb[:, j, kc * 128:(kc + 1) * 128], ident[:], start=True,
                                       stop=True)
                    return ins
                sc.op('pe', tr, reads=[('hb', sl, j) for j in range(4)] + ['ident'], writes=[('ps', bT)])
                eng = 'act' if kc % 2 == 0 else 'dve'
                if eng == 'act':
                    sc.op('act', (lambda e, kc=kc, pT=pT: e.copy(hT[:, kc, :], pT)), reads=[('ps', bT)], writes=[('hT', kc)])
                else:
                    sc.op('dve', (lambda e, kc=kc, pT=pT: e.tensor_copy(hT[:, kc, :], pT)), reads=[('ps', bT)],
                          writes=[('hT', kc)])
            if tt + 1 < 8:
                prep(tt + 1)
            hT_r = [('hT', kc) for kc in range(8)]

            def fm_chunk(c0, width, b):
                def f(e):
                    for kc in range(8):
                        ins = e.matmul(PS(b, rows=width), wext[:, kc, c0:c0 + width], hT[:, kc, :], start=(kc == 0),
                                       stop=(kc == 7))
                    return ins
                sc.op('pe', f, reads=hT_r + ['wext'], writes=[('ps', b)])

            if STOP <= 2:
                return
            for c in range(8):
                b = bank()
                fm_chunk(c * 128, 128, b)
                sc.op('act', (lambda e, c=c, b=b: e.copy(stAC[sl][:, c, :], PS(b))), reads=[('ps', b)],
                      writes=[('stAC', sl, c)])
            sc.dma('sp', 'd_oAC%d' % sl, fmAC.ap()[:, :, t0:t0 + 512].rearrange("c p t -> p c t"), stAC[sl],
                   reads=[('stAC', sl, c) for c in range(8)], writes=[('fmAC', s)])

            if STOP <= 3:
                return
            for ci, (c_raw, c_sw, gcol) in enumerate(((8, 10, 0), (9, 11, 0), (12, 13, 2))):
                b1, b2, b3 = bank(), bank(), bank()
                u = ci % 2
                fm_chunk(1024 + (c_raw - 8) * 128, 128, b1)
                fm_chunk(1024 + (c_sw - 8) * 128, 128, b2)
                sc.op('act', (lambda e, b1=b1, u=u: e.activation(sq[u], PS(b1), AF.Square)), reads=[('ps', b1)],
                      writes=[('sq', u)])
                sc.op('pe', (lambda e, b3=b3, u=u: e.matmul(PS(b3), bones[:], sq[u], start=True, stop=True)),
                      reads=[('sq', u), 'bones'], writes=[('ps', b3)])
                sc.op('act', (lambda e, b3=b3, u=u: e.activation(rs[u], PS(b3), AF.Sqrt, bias=epsb[:, 0:1], scale=1.0 / 64)),
                      reads=[('ps', b3), 'epsb'], writes=[('rs', u)])
                sc.op('dve', (lambda e, u=u: e.reciprocal(rs[u], rs[u])), reads=[('rs', u)], writes=[('rs', u)])
                sc.op('dve', (lambda e, b1=b1, u=u, gcol=gcol: e.scalar_tensor_tensor(
                    t1[u], PS(b1), colc[:, l, gcol:gcol + 1], R[:, 0, :], ALU.mult, ALU.mult)),
                    reads=[('ps', b1), 'colc', ('ropet', sl)], writes=[('t1', u)])
                sc.op('dve', (lambda e, b2=b2, u=u, gcol=gcol: e.scalar_tensor_tensor(
                    t2[u], PS(b2), colc[:, l, gcol + 1:gcol + 2], R[:, 1, :], ALU.mult, ALU.mult)),
                    reads=[('ps', b2), 'colc', ('ropet', sl)], writes=[('t2', u)])
                sc.op('pool', (lambda e, u=u: e.tensor_tensor(t1[u], t1[u], t2[u], ALU.add)), reads=[('t1', u), ('t2', u)],
                      writes=[('t1', u)])
                sc.op('dve', (lambda e, u=u, ci=ci: e.tensor_tensor(stB[sl][:, ci, :], t1[u], rs[u], ALU.mult)),
                      reads=[('t1', u), ('rs', u)], writes=[('stB', sl, ci)])
            sc.dma('sp', 'd_oB%d' % sl, fmB.ap()[:, :, t0:t0 + 512].rearrange("c p t -> p c t"), stB[sl],
                   reads=[('stB', sl, c) for c in range(3)], writes=[('fmB', s)])

            if STOP <= 4:
                return
            bq0, bq1, bkv, bsq, bskv = bank(), bank(), bank(), bank(), bank()
            fm_chunk(1792, 128, bq0)
            fm_chunk(1920, 128, bq1)
            fm_chunk(2048, 128, bkv)
            sc.op('act', (lambda e: e.activation(sq[0], PS(bq0), AF.Square)), reads=[('ps', bq0)], writes=[('sq', 0)])
            sc.op('act', (lambda e: e.activation(sq[1], PS(bq1), AF.Square)), reads=[('ps', bq1)], writes=[('sq', 1)])

            def ssd(e):
                e.matmul(PS(bsq), aones[:], sq[0], start=True, stop=False)
                return e.matmul(PS(bsq), aones[:], sq[1], start=False, stop=True)
            sc.op('pe', ssd, reads=[('sq', 0), ('sq', 1), 'aones'], writes=[('ps', bsq)])
            sc.op('act', (lambda e: e.activation(rs[0], PS(bsq), AF.Sqrt, bias=epsb[:, 0:1], scale=1.0 / 256)),
                  reads=[('ps', bsq), 'epsb'], writes=[('rs', 0)])
            sc.op('dve', (lambda e: e.reciprocal(rs[0], rs[0])), reads=[('rs', 0)], writes=[('rs', 0)])
            sc.op('dve', (lambda e: e.scalar_tensor_tensor(dqn[:, 0, :], PS(bq0), colc[:, l, 4:5], rs[0], ALU.mult, ALU.mult)),
                  reads=[('ps', bq0), ('rs', 0), 'colc'], writes=[('dqn', 0)])
            sc.op('dve', (lambda e: e.scalar_tensor_tensor(dqn[:, 1, :], PS(bq1), colc[:, l, 5:6], rs[0], ALU.mult, ALU.mult)),
                  reads=[('ps', bq1), ('rs', 0), 'colc'], writes=[('dqn', 1)])
            sc.op('act', (lambda e: e.activation(t2[0], PS(bkv), AF.Square)), reads=[('ps', bkv)], writes=[('t2', 0)])
            sc.op('pe', (lambda e: e.matmul(PS(bskv), aones[:], t2[0], start=True, stop=True)), reads=[('t2', 0), 'aones'],
                  writes=[('ps', bskv)])
            sc.op('act', (lambda e: e.activation(rs[1], PS(bskv), AF.Sqrt, bias=epsb[:, 0:1], scale=1.0 / 128)),
                  reads=[('ps', bskv), 'epsb'], writes=[('rs', 1)])
            sc.op('dve', (lambda e: e.reciprocal(rs[1], rs[1])), reads=[('rs', 1)], writes=[('rs', 1)])
            sc.op('dve', (lambda e: e.scalar_tensor_tensor(dkvn, PS(bkv), colc[:, l, 6:7], rs[1], ALU.mult, ALU.mult)),
                  reads=[('ps', bkv), ('rs', 1), 'colc'], writes=['dkvn'])

            for h in range(4):
                b1, b2 = bank(), bank()

                def fq(e, h=h, b1=b1, o=0):
                    for kc in range(2):
                        ins = e.matmul(PS(b1, rows=96), uq[:, kc, o + h * 96:o + (h + 1) * 96], dqn[:, kc, :], start=(kc == 0),
                                       stop=(kc == 1))
                    return ins

                def fqs(e, h=h, b2=b2, o=384):
                    for kc in range(2):
                        ins = e.matmul(PS(b2, rows=96), uq[:, kc, o + h * 96:o + (h + 1) * 96], dqn[:, kc, :], start=(kc == 0),
                                       stop=(kc == 1))
                    return ins
                sc.op('pe', fq, reads=[('dqn', 0), ('dqn', 1), 'uq'], writes=[('ps', b1)])
                sc.op('pe', fqs, reads=[('dqn', 0), ('dqn', 1), 'uq'], writes=[('ps', b2)])
                sc.op('act', (lambda e, h=h, b1=b1: e.copy(stDq[sl][0:64, h, :], PS(b1, rows=64))), reads=[('ps', b1)],
                      writes=[('stDq', sl, h, 'n')])
                sc.op('dve', (lambda e, b1=b1: e.tensor_tensor(t1[0][64:96, :], PS(b1, rows=32, r0=64), R[64:96, 2, :], ALU.mult)),
                      reads=[('ps', b1), ('ropet', sl)], writes=[('t1', 0)])
                sc.op('dve', (lambda e, b2=b2: e.tensor_tensor(t2[0][64:96, :], PS(b2, rows=32, r0=64), R[64:96, 3, :], ALU.mult)),
                      reads=[('ps', b2), ('ropet', sl)], writes=[('t2', 0)])
                sc.op('dve', (lambda e, h=h: e.tensor_tensor(stDq[sl][64:96, h, :], t1[0][64:96, :], t2[0][64:96, :], ALU.add)),
                      reads=[('t1', 0), ('t2', 0)], writes=[('stDq', sl, h, 'r')])
            sc.dma('sp', 'd_oDq%d' % sl, fmDq.ap()[:, :, t0:t0 + 512].rearrange("h p t -> p h t"), stDq[sl][0:96],
                   reads=[('stDq', sl, h, x) for h in range(4) for x in 'nr'], writes=[('fmDq', s)])
            for h in range(4):
                b1 = bank()
                sc.op('pe', (lambda e, h=h, b1=b1: e.matmul(PS(b1, rows=64), ukv[:, h * 64:(h + 1) * 64], dkvn, start=True,
                                                           stop=True)), reads=['dkvn', 'ukv'], writes=[('ps', b1)])
                sc.op('act', (lambda e, h=h, b1=b1: e.copy(stDk[sl][0:64, h, :], PS(b1, rows=64))), reads=[('ps', b1)],
                      writes=[('stDk', sl, h, 'n')])
            b1, b2 = bank(), bank()
            fm_chunk(2176, 96, b1)
            fm_chunk(2272, 96, b2)
            sc.op('dve', (lambda e, b1=b1: e.tensor_tensor(t1[1][64:96, :], PS(b1, rows=32, r0=64), R[64:96, 2, :], ALU.mult)),
                  reads=[('ps', b1), ('ropet', sl)], writes=[('t1', 1)])
            sc.op('dve', (lambda e, b2=b2: e.tensor_tensor(t2[1][64:96, :], PS(b2, rows=32, r0=64), R[64:96, 3, :], ALU.mult)),
                  reads=[('ps', b2), ('ropet', sl)], writes=[('t2', 1)])
            for h in range(4):
                sc.op('dve', (lambda e, h=h: e.tensor_tensor(stDk[sl][64:96, h, :], t1[1][64:96, :], t2[1][64:96, :], ALU.add)),
                      reads=[('t1', 1), ('t2', 1)], writes=[('stDk', sl, h, 'r')])
            sc.dma('sp', 'd_oDk%d' % sl, fmDk.ap()[:, :, t0:t0 + 512].rearrange("h p t -> p h t"), stDk[sl][0:96],
                   reads=[('stDk', sl, h, x) for h in range(4) for x in 'nr'], writes=[('fmDk', s)])

            if STOP <= 5:
                return
            for j in range(4):
                b1, b2, b3 = bank(), bank(), bank()

                def fv(e, j=j, b=b1, c0=2368, w=384):
                    for kc in range(8):
                        ins = e.matmul(PS(b, cols=w), hT[:, kc, j * 128:(j + 1) * 128], wext[:, kc, c0:c0 + w], start=(kc == 0),
                                       stop=(kc == 7))
                    return ins

                def fv2(e, j=j, b=b2, c0=2752, w=256):
                    for kc in range(8):
                        ins = e.matmul(PS(b, cols=w), hT[:, kc, j * 128:(j + 1) * 128], wext[:, kc, c0:c0 + w], start=(kc == 0),
                                       stop=(kc == 7))
                    return ins
                sc.op('pe', fv, reads=hT_r + ['wext'], writes=[('ps', b1)])
                sc.op('pe', fv2, reads=hT_r + ['wext'], writes=[('ps', b2)])
                sc.op('pe', (lambda e, j=j, b3=b3: e.matmul(PS(b3, cols=256), dkvn[:, j * 128:(j + 1) * 128], ukv[:, 256:512],
                                                           start=True, stop=True)), reads=['dkvn', 'ukv'], writes=[('ps', b3)])

                def v65(st, j, H):
                    return st[sl][:, j, :].rearrange("p (h e) -> p h e", e=65)[:, :, 0:64]

                def p64(b, c0, H):
                    return PSfull(b)[:, c0:c0 + H * 64].rearrange("p (h d) -> p h d", d=64)
                if KV & 4:
                  sc.op('act', (lambda e, j=j, b1=b1: e.copy(v65(stVA, j, 4), p64(b1, 0, 4))), reads=[('ps', b1)],
                      writes=[('stVA', sl, j)])
                if KV & 8:
                  sc.op('act', (lambda e, j=j, b1=b1: e.copy(v65(stVB, j, 2), p64(b1, 256, 2))), reads=[('ps', b1)],
                      writes=[('stVB', sl, j)])
                if KV & 4:
                  sc.op('act', (lambda e, j=j, b2=b2: e.copy(v65(stVC, j, 4), p64(b2, 0, 4))), reads=[('ps', b2)],
                      writes=[('stVC', sl, j)])
                if KV & 8:
                  sc.op('act', (lambda e, j=j, b3=b3: e.copy(v65(stVD, j, 4), p64(b3, 0, 4))), reads=[('ps', b3)],
                      writes=[('stVD', sl, j)])
            if STOP <= 6:
                return
            for (st, dst, nm, w) in ((stVA, vA, 'stVA', 260), (stVB, vB, 'stVB', 130), (stVC, vC, 'stVC', 260),
                                     (stVD, vD, 'stVD', 260)):
                sc.dma('sp', 'd_o%s%d' % (nm, sl), dst.ap()[t0:t0 + 512, :].rearrange("(j p) w -> p j w", p=128), st[sl],
                       reads=[(nm, sl, j) for j in range(4)] + [(nm, sl, 'ones')], writes=[(nm[2:], s)])
        prep(0)
        for tt in range(8):
            do_tile(tt)
        sc.barrier()

    def setup_amask():
        cv = Carver()
        t5s = cv.take([128, 128], F32)
        aoh = cv.take([128, LG], F32)
        amu = cv.take([4, LG], F32)
        gr = cv.take([4, LG], F32)
        repa = cv.take([128, 4, LG], F32)
        sc.op('dve', (lambda e: e.memset(t5s, 0.0)), writes=['t5s'])
        sc.op('pool', (lambda e: e.memset(aoh, 0.0)), writes=['aoh'])
        sc.dma('sp', 'd_m0', t5s[0:32, 0:4], t5_in, writes=['t5s'])
        sc.dma('sp', 'd_m0', aoh[0:32, :], aoh_in, writes=['aoh'])
        sc.dma('sp', 'd_m0', amu, amult_in, writes=['amu'])
        for c0 in range(0, LG, 512):
            w = 512
            b = bank()
            sc.op('pe', (lambda e, b=b, c0=c0, w=w: e.matmul(PS(b, cols=w), t5s, aoh[:, c0:c0 + w], start=True, stop=True)),
                  reads=['t5s', 'aoh'], writes=[('ps', b)])
            sc.op('act', (lambda e, b=b, c0=c0, w=w: e.activation(gr[:, c0:c0 + w], PS(b, rows=4, cols=w), AF.Exp)),
                  reads=[('ps', b)], writes=[('gr', c0)])
            sc.op('dve', (lambda e, c0=c0, w=w: e.tensor_tensor(gr[:, c0:c0 + w], gr[:, c0:c0 + w], amu[:, c0:c0 + w], ALU.mult)),
                  reads=[('gr', c0), 'amu'], writes=[('gr', c0)])
        sc.dma('sp', 'd_m1', growA.ap(), gr, reads=[('gr', c0) for c0 in range(0, LG, 512)], writes=['growA'])
        sc.dma('sp', 'd_m1', repa, growA.ap().rearrange("(o h) n -> o h n", o=1).partition_broadcast(128) if False else
               bass.AP(growA, 0, [[0, 128], [LG, 4], [1, LG]]), reads=['growA'], writes=['repa'])
        sc.dma('sp', 'd_m1', toeA.ap().rearrange("h p n -> p h n"), repa, reads=['repa'], writes=['toeA'])
        sc.barrier()

    def setup_cmask(l):
        cv = Carver()
        e60 = cv.take([60, 32], F32)
        rpad = cv.take([60, 128], F32)
        repc = cv.take([64, 60, 128], F32)
        sc.dma('sp', 'd_m0', e60[:, 0:31], rpb_in[l], writes=['e60'])
        sc.op('dve', (lambda e: e.memset(rpad, 0.0)), writes=['rpad'])
        sc.op('act', (lambda e: e.activation(rpad[:, 0:16], e60[:, 15:31], AF.Exp)), reads=['e60', 'rpad'], writes=['rpad'])
        sc.op('act', (lambda e: e.activation(rpad[:, 113:128], e60[:, 0:15], AF.Exp)), reads=['e60', 'rpad'], writes=['rpad'])
        sc.dma('sp', 'd_m1', growC.ap(), rpad, reads=['rpad'], writes=['growC'])
        sc.dma('sp', 'd_m1', repc, bass.AP(growC, 0, [[0, 64], [128, 60], [1, 128]]), reads=['growC'], writes=['repc'])
        sc.dma('sp', 'd_m1', toeC.ap().rearrange("r p n -> p r n"), repc, reads=['repc'], writes=['toeC'])
        sc.barrier()

    def phase_p2(s, l):
        cv = Carver()
        KTs = [cv.take([128, 4, S], BF16) for _ in range(2)]
        Ves = [cv.take([128, 32, 260], BF16) for _ in range(2)]
        Vo = cv.take([128, 32, 260], BF16)
        Qt = [cv.take([128, 4, 512], BF16) for _ in range(2)]
        PT = [cv.take([128, 2, 512], BF16) for _ in range(3)]
        strip = cv.take([128, 4, UA], BF16)
        cm32 = cv.take([128, 14, 4, 64], F32)
        cmask = cv.take([128, 14, 4, 64], BF16)
        ccv = cv.take([128, 64], F32)
        gout = cv.take([128, DM], F32)
        og = [cv.take([128, 4, 256], F32) for _ in range(2)]
        onb = [cv.take([128, 4, 256], BF16) for _ in range(2)]
        junk = cv.take([128, 256], BF16)
        rc = cv.take([128, 16], F32)
        ssn = cv.take([128, 8], F32)
        sc.dma('sp', 'd_w1', gout, rowg_in[2 + l:3 + l, :].partition_broadcast(128), writes=['gout'])
        if deferred and s == 0 and l == 0:
            cst = [cv.take([128, NCOL], BF16) for _ in range(2)]
            ji = 0
            for (dst, src_, nm, l_, k, ncol) in deferred:
                for k_ in range(k):
                    slot = ji % 2
                    sc.dma('pool', 'd_cj%d' % slot, cst[slot][:, 0:ncol], src_[l_][k_ * 128:(k_ + 1) * 128, :], writes=[('cst', slot)])
                    sc.dma('pool', 'd_cjo%d' % slot, dst.ap()[l_][k_ * 128:(k_ + 1) * 128, :], cst[slot][:, 0:ncol],
                           reads=[('cst', slot)], writes=[(nm + '_b', l_, k_)])
                    ji += 1
            del deferred[:]
        psT_b = [psT[:, 0:512], psT[:, 512:1024]]
        obanks = [(('ps', 4), PS(4)), (('ps', 5), PS(5)), (('psT', 0), psT_b[0]), (('psT', 1), psT_b[1])]
        pt_rr = [0]
        ep_rr = [0]

        def epilogue_head(oreg, oap, ogt, h, nj, u):
            o3 = oap[:, 0:nj * 65].rearrange("p (j e) -> p j e", e=65)
            sc.op('dve', (lambda e: e.reciprocal(rc[:, u * 4:u * 4 + nj], o3[:, :, 64])), reads=[oreg], writes=[('rc', u)])
            for j in range(nj):
                sc.op('dve', (lambda e, j=j: e.tensor_scalar_mul(ogt[:, j, h * 64:(h + 1) * 64], o3[:, j, 0:64],
                                                                 rc[:, u * 4 + j:u * 4 + j + 1])),
                      reads=[oreg, ('rc', u)], writes=[('og', h, j)])

        def group_norm_store(g, t0, ogt, slot):
            for j in range(4):
                sc.op('act', (lambda e, j=j: e.activation(junk, ogt[:, j, :], AF.Square, accum_out=ssn[:, j:j + 1])),
                      reads=[('og', h, j) for h in range(4)], writes=['junk2', ('ssn', j)])
            sc.op('act', (lambda e: e.activation(ssn[:, 4:8], ssn[:, 0:4], AF.Ln, bias=epsb[:, 0:1], scale=1.0 / 256)),
                  reads=[('ssn', j) for j in range(4)] + ['epsb'], writes=['ssr0'])
            sc.op('act', (lambda e: e.activation(ssn[:, 4:8], ssn[:, 4:8], AF.Exp, scale=-0.5)), reads=['ssr0'], writes=['ssr'])
            for j in range(4):
                sc.op('dve', (lambda e, j=j: e.scalar_tensor_tensor(onb[slot][:, j, :], ogt[:, j, :], ssn[:, 4 + j:5 + j],
                                                                    gout[:, g * 256:(g + 1) * 256], ALU.mult, ALU.mult)),
                      reads=[('og', h, j) for h in range(4)] + ['ssr', 'gout'], writes=[('onb', slot, j)])
            sc.dma('sp', 'd_on%d' % slot, oN.ap()[t0:t0 + 512, g * 256:(g + 1) * 256].rearrange("(j p) w -> p j w", p=128),
                   onb[slot], reads=[('onb', slot, j) for j in range(4)], writes=[('oN', s)])

        def dense_group(g, name, bi, mode):
            KT, Ve = KTs[bi], Ves[bi]
            kreg, vreg_ = ('KT', bi), ('Ve', bi)
            nchunk = {'A': 2, 'B': 1, 'D': 4}[name]
            scale = (96.0 if name == 'D' else 64.0) ** -0.5
            if mode == 'load':
                if name == 'A':
                    sc.dma('sp', 'd_k%d' % bi, KT[:, 0:2, :], fmAC.ap()[2:4].rearrange("c p t -> p c t"), reads=[('fmAC', s)], writes=[kreg])
                    vsrc, vw, nq = vA, 260, 2
                    for h in range(4):
                        sc.dma('pool', 'd_strip', strip[:, h, :], bass.AP(toeA, h * 128 * LG, [[LG - 1, 128], [1, UA]]), reads=['toeA'],
                               writes=['strip'])
                elif name == 'B':
                    sc.dma('sp', 'd_k%d' % bi, KT[:, 0:1, :], fmB.ap()[2:3].rearrange("c p t -> p c t"), reads=[('fmB', s)], writes=[kreg])
                    vsrc, vw, nq = vB, 130, 2
                else:
                    sc.dma('sp', 'd_k%d' % bi, KT[0:96, 0:4, :], fmDk.ap().rearrange("h p t -> p h t"), reads=[('fmDk', s)], writes=[kreg])
                    vsrc, vw, nq = vD, 260, 4
                sc.dma('sp', 'd_k%d' % bi, Ve[:, :, 0:vw], vsrc.ap().rearrange("(j p) w -> p j w", p=128), reads=[(vsrc.name if False else name + 'v', s)] if False else [({'A': 'VA', 'B': 'VB', 'D': 'VD'}[name], s)], writes=[vreg_])

                return

            def loadq(qt):
                sl = qt % 2
                t0 = qt * 512
                if name == 'A':
                    sc.dma('sp', 'd_q%d' % sl, Qt[sl][:, 0:2, :], fmAC.ap()[0:2, :, t0:t0 + 512].rearrange("c p t -> p c t"),
                           reads=[('fmAC', s)], writes=[('Qt', sl)])
                elif name == 'B':
                    sc.dma('sp', 'd_q%d' % sl, Qt[sl][:, 0:2, :], fmB.ap()[0:2, :, t0:t0 + 512].rearrange("c p t -> p c t"),
                           reads=[('fmB', s)], writes=[('Qt', sl)])
                else:
                    sc.dma('sp', 'd_q%d' % sl, Qt[sl][0:96, 0:4, :], fmDq.ap()[:, :, t0:t0 + 512].rearrange("h p t -> p h t"),
                           reads=[('fmDq', s)], writes=[('Qt', sl)])

            def kbs_of(qt):
                t0 = qt * 512
                if name == 'A':
                    return [kb for kb in range(32) if (kb * 128 + 127 >= t0 - 1024) and (kb * 128 <= t0 + 511 + 1024)]
                return list(range(32))

            def views(qt, pr):
                sl = qt % 2
                if name == 'A':
                    heads = (2 * pr, 2 * pr + 1)
                    qv = [Qt[sl][0:64, pr, :], Qt[sl][64:128, pr, :]]
                    kv = [KT[0:64, pr, :], KT[64:128, pr, :]]
                    vh = heads
                elif name == 'B':
                    heads = (pr, pr + 2)
                    qv = [Qt[sl][0:64, pr, :], Qt[sl][64:128, pr, :]]
                    kv = [KT[0:64, 0, :], KT[64:128, 0, :]]
                    vh = (0, 1)
                else:
                    heads = (2 * pr, 2 * pr + 1)
                    qv = [Qt[sl][0:96, heads[0], :], Qt[sl][0:96, heads[1], :]]
                    kv = [KT[0:96, heads[0], :], KT[0:96, heads[1], :]]
                    vh = heads
                return heads, qv, kv, vh

            units = []
            for qt in range(8):
                ks = kbs_of(qt)
                for pr in range(2):
                    for ki, kb in enumerate(ks):
                        units.append(dict(qt=qt, pr=pr, ki=ki, kb=kb, nk=len(ks), idx=len(units)))

            def emit_qk(u):
                qt, pr, kb = u['qt'], u['pr'], u['kb']
                sl = qt % 2
                if pr == 0 and u['ki'] == 0 and qt + 1 < 8:
                    loadq(qt + 1)
                heads, qv, kv, vh = views(qt, pr)
                sb_ = (u['idx'] % 3) * 2

                def qk(e, kb=kb, sb_=sb_, qv=qv, kv=kv):
                    for _ in range(1 + KDUP):
                        e.matmul(PS(sb_), kv[0][:, kb * 128:(kb + 1) * 128], qv[0], start=True, stop=True)
                        ins = e.matmul(PS(sb_ + 1), kv[1][:, kb * 128:(kb + 1) * 128], qv[1], start=True, stop=True)
                    return ins
                sc.op('pe', qk, reads=[kreg, ('Qt', sl)], writes=[('ps', sb_), ('ps', sb_ + 1)])

            def emit_rest(u):
                qt, pr, kb, ki, nk = u['qt'], u['pr'], u['kb'], u['ki'], u['nk']
                t0 = qt * 512
                heads, qv, kv, vh = views(qt, pr)
                sb_ = (u['idx'] % 3) * 2
                pslot = u['idx'] % 3
                P_ = PT[pslot]
                ob = [obanks[2], obanks[3]]
                ogt = og[qt % 2]
                sc.op('act', (lambda e: e.activation(P_.rearrange("p a n -> p (a n)"), psA[:, sb_ * 512:(sb_ + 2) * 512], AF.Exp,
                                                     scale=scale)),
                      reads=[('ps', sb_), ('ps', sb_ + 1)], writes=[('PT', pslot, 0), ('PT', pslot, 1)])
                if name == 'A':
                    off = 1408 - (kb * 128 - t0)
                    h0 = heads[0]
                    sc.op('dve', (lambda e: e.tensor_tensor(P_, P_, strip[:, h0:h0 + 2, off:off + 512], ALU.mult)),
                          reads=[('PT', pslot, 0), ('PT', pslot, 1), 'strip'], writes=[('PT', pslot, 0), ('PT', pslot, 1)])
                first, last = (ki == 0), (ki == nk - 1)

                def pv(e):
                    for i in range(2):
                        for j in range(4):
                            ins = e.matmul(ob[i][1][:, j * 65:(j + 1) * 65], P_[:, i, j * 128:(j + 1) * 128],
                                           Ve[:, kb, vh[i] * 65:(vh[i] + 1) * 65], start=(first and j == 0), stop=last,
                                           skip_group_check=True)
                    return ins
                sc.op('pe', pv, reads=[('PT', pslot, 0), ('PT', pslot, 1), vreg_], writes=[ob[0][0], ob[1][0]])
                if last:
                    for i in range(2):
                        epilogue_head(ob[i][0], ob[i][1], ogt, heads[i], 4, i)
                    if pr == 1:
                        group_norm_store(g, t0, ogt, qt % 2)

            loadq(0)
            emit_qk(units[0])
            emit_qk(units[1])
            for n, u in enumerate(units):
                if n + 2 < len(units):
                    emit_qk(units[n + 2])
                emit_rest(u)

        def group_c(g, bi, mode):
            KT, Ve = KTs[bi], Ves[bi]
            kreg, vreg_ = ('KT', bi), ('Ve', bi)
            if mode == 'load':
                sc.dma('sp', 'd_k%d' % bi, KT[:, 0:2, :], fmAC.ap()[6:8].rearrange("c p t -> p c t"), reads=[('fmAC', s)], writes=[kreg])
                sc.dma('sp', 'd_k%d' % bi, Ve, vC.ap().rearrange("(j p) w -> p j w", p=128), reads=[('VC', s)], writes=[vreg_])
                sc.dma('sp', 'd_kc', Vo[:, 0:31, :], vC.ap()[64:64 + 31 * 128, :].rearrange("(j p) w -> p j w", p=128), reads=[('VC', s)],
                       writes=['Vo'])
                sc.dma('sp', 'd_kc', ccv, ccv_in, writes=['ccv'])
                for pos, h in enumerate((0, 2, 1, 3)):
                    for a_ in range(2):
                        sc.dma('sp', 'd_kc', cm32[a_ * 64:(a_ + 1) * 64, :, pos, :],
                               bass.AP(toeC, (h * 15 + a_) * 8192, [[127, 64], [8192, 14], [1, 64]]), reads=['toeC'], writes=['cm32'])
                for m in range(14):
                    for pos in range(4):
                        sc.op('dve', (lambda e, m=m, pos=pos: e.tensor_tensor(cmask[:, m, pos, :], cm32[:, m, pos, :], ccv, ALU.mult)),
                              reads=['cm32', 'ccv'], writes=['cmask'])

                return

            def loadq(qt):
                sl = qt % 2
                t0 = qt * 512
                sc.dma('sp', 'd_q%d' % sl, Qt[sl][:, 0:2, :], fmAC.ap()[4:6, :, t0:t0 + 512].rearrange("c p t -> p c t"),
                       reads=[('fmAC', s)], writes=[('Qt', sl)])

            units = []
            for r in range(64):
                for jw in range(4):
                    units.append(dict(r=r, jw=jw, idx=len(units)))

            def emit_qk(u):
                r, jw = u['r'], u['jw']
                qt = r // 8
                sl = qt % 2
                if r % 8 == 0 and jw == 0 and qt + 1 < 8:
                    loadq(qt + 1)
                rs_ = min(max(r - 4, 0), 56)
                kw = rs_ * 64 + 128 * jw
                sb_ = (u['idx'] % 2) * 2
                qc = (r % 8) * 64

                def qk(e):
                    for h in range(4):
                        c, lo = h // 2, 64 * (h % 2)
                        bnk = sb_ + (h % 2)
                        ins = e.matmul(psA[:, bnk * 512 + (h // 2) * 64: bnk * 512 + (h // 2) * 64 + 64],
                                       KT[lo:lo + 64, c, kw:kw + 128], Qt[sl][lo:lo + 64, c, qc:qc + 64], start=True, stop=True)
                    return ins
                sc.op('pe', qk, reads=[kreg, ('Qt', sl)], writes=[('ps', sb_), ('ps', sb_ + 1)])

            def emit_rest(u):
                r, jw = u['r'], u['jw']
                qt = r // 8
                ogt = og[qt % 2]
                i4 = (r % 8) // 2
                half = r % 2
                rs_ = min(max(r - 4, 0), 56)
                oreg, oap = obanks[(r // 2) % 4]
                kw = rs_ * 64 + 128 * jw
                m = rs_ + 2 * jw - r + 7
                sb_ = (u['idx'] % 2) * 2
                pslot = u['idx'] % 3
                P_ = PT[pslot]
                pout = P_[:, :, 0:128]
                pin = psA[:, sb_ * 512:(sb_ + 2) * 512].rearrange("p (b n) -> p b n", b=2)[:, :, 0:128]
                sc.op('act', (lambda e: e.activation(pout, pin, AF.Exp, scale=0.125)),
                      reads=[('ps', sb_), ('ps', sb_ + 1)], writes=[('PT', pslot, 0), ('PT', pslot, 1)])
                mk = cmask[:, m, :, :].rearrange("p (b u) c -> p b (u c)", b=2)
                sc.op('dve', (lambda e: e.tensor_tensor(pout, pout, mk, ALU.mult)),
                      reads=[('PT', pslot, 0), ('PT', pslot, 1), 'cmask'], writes=[('PT', pslot, 0), ('PT', pslot, 1)])
                if kw % 128 == 0:
                    vt, vi, vreg = Ve, kw // 128, vreg_
                else:
                    vt, vi, vreg = Vo, (kw - 64) // 128, 'Vo'
                first, last = (jw == 0), (jw == 3)

                def pv(e):
                    for h in range(4):
                        b_, u_ = h % 2, h // 2
                        ins = e.matmul(oap[half * 64:half * 64 + 64, h * 65:(h + 1) * 65],
                                       P_[:, b_, u_ * 64:u_ * 64 + 64], vt[:, vi, h * 65:(h + 1) * 65], start=(first and h == 0),
                                       stop=last, skip_group_check=True)
                    return ins
                sc.op('pe', pv, reads=[('PT', pslot, 0), ('PT', pslot, 1), vreg], writes=[oreg])
                if half == 1 and last:
                    o3 = oap[:, 0:260].rearrange("p (h e) -> p h e", e=65)
                    sc.op('dve', (lambda e: e.reciprocal(rc[:, 0:4], o3[:, :, 64])), reads=[oreg], writes=[('rc', 0)])
                    for h in range(4):
                        sc.op('dve', (lambda e, h=h: e.tensor_scalar_mul(ogt[:, i4, h * 64:(h + 1) * 64], o3[:, h, 0:64],
                                                                         rc[:, h:h + 1])),
                              reads=[oreg, ('rc', 0)], writes=[('og', h, i4)])
                    if r % 8 == 7:
                        group_norm_store(g, qt * 512, ogt, qt % 2)

            loadq(0)
            emit_qk(units[0])
            for n, u in enumerate(units):
                if n + 1 < len(units):
                    emit_qk(units[n + 1])
                emit_rest(u)

        grp = os.environ.get('KGROUPS', 'ABCD')
        order = [x for x in (('B', 1), ('D', 3), ('A', 0), ('C', 2)) if x[0] in grp]

        def run_group(k, mode):
            nm, g = order[k]
            if nm == 'C':
                group_c(g, k % 2, mode)
            else:
                dense_group(g, nm, k % 2, mode)
        run_group(0, 'load')
        for k in range(len(order)):
            if k + 1 < len(order):
                run_group(k + 1, 'load')
            run_group(k, 'compute')
        sc.barrier()

    def phase_p3(s, l, xsrc, last):
        cv = Carver()
        wout = cv.take([128, 8, DM], BF16)
        wg = cv.take([128, 8, DFF], BF16)
        wu = cv.take([128, 8, DFF], BF16)
        wd = cv.take([128, 22, DM], BF16)
        gffn = cv.take([128, DM], F32)
        gfin = cv.take([128, DM], F32)
        xt = [cv.take([128, DM], F32) for _ in range(2)]
        ont = [cv.take([128, DM], BF16) for _ in range(2)]
        onT = cv.take([128, 8, 128], BF16)
        x1s = [cv.take([128, DM], F32) for _ in range(2)]
        hb2 = cv.take([128, DM], BF16)
        h2T = cv.take([128, 8, 128], BF16)
        sil = [cv.take([128, 512], BF16) for _ in range(2)]
        act = cv.take([128, DFF], BF16)
        actT = cv.take([128, 22, 128], BF16)
        yout = [cv.take([128, DM], F32) for _ in range(2)]
        sss = [cv.take([128, 8], F32) for _ in range(2)]
        sc.dma('sp', 'd_wo_', wout, wout_b.ap()[l].rearrange("(k p) n -> p k n", p=128), reads=[('wout_b', l)] + [('wout_b', l, k_) for k_ in range(8)], writes=['wout'])
        sc.dma('sp', 'd_wg_', wg, wg_b.ap()[l].rearrange("(k p) n -> p k n", p=128), reads=[('wg_b', l)] + [('wg_b', l, k_) for k_ in range(8)], writes=['wg'])
        sc.dma('sp', 'd_wu_', wu, wu_b.ap()[l].rearrange("(k p) n -> p k n", p=128), reads=[('wu_b', l)] + [('wu_b', l, k_) for k_ in range(8)], writes=['wu'])
        sc.dma('sp', 'd_wd_', wd, wd_b.ap()[l].rearrange("(k p) n -> p k n", p=128), reads=[('wd_b', l)] + [('wd_b', l, k_) for k_ in range(22)], writes=['wd'])
        sc.dma('sp', 'd_wgn', gffn, rowg_in[4 + l:5 + l, :].partition_broadcast(128), writes=['gffn'])
        sc.dma('sp', 'd_wgn', gfin, rowg_in[6:7, :].partition_broadcast(128), writes=['gfin'])
        psT_b = [psT[:, 0:512], psT[:, 512:1024]]
        tb = [0]

        def tbank():
            k = tb[0] % 2
            tb[0] += 1
            return ('psT', k), psT_b[k]

        def load(i):
            sl = i % 2
            sc.dma('sp', 'd_x%d' % sl, xt[sl], xsrc[i * 128:(i + 1) * 128, :], reads=[('xres', s, i)], writes=[('xt', sl)])
            sc.dma('sp', 'd_x%d' % sl, ont[sl], oN.ap()[i * 128:(i + 1) * 128, :], reads=[('oN', s)], writes=[('ont', sl)])

        def transposes(srcfn, n, dst, dreg, sreads):
            for k0 in range(0, n, 4):
                kk = min(4, n - k0)
                bT = bank()
                treg, tap = ('ps', bT), PS(bT)

                def tr(e, k0=k0, kk=kk, tap=tap):
                    for q in range(kk):
                        ins = e.matmul(tap[:, q * 128:(q + 1) * 128], srcfn(k0 + q), ident[:], start=True, stop=True)
                    return ins
                sc.op('pe', tr, reads=sreads + ['ident'], writes=[treg])
                dv = dst[:, k0:k0 + kk, :].rearrange("p k t -> p (k t)")
                if (k0 // 4) % 2 == 0:
                    sc.op('dve', (lambda e, dv=dv, tap=tap, kk=kk: e.tensor_copy(dv, tap[:, 0:kk * 128])), reads=[treg],
                          writes=[(dreg, k0)])
                else:
                    sc.op('act', (lambda e, dv=dv, tap=tap, kk=kk: e.copy(dv, tap[:, 0:kk * 128])), reads=[treg],
                          writes=[(dreg, k0)])

        def front1(i, part):
            sl = i % 2
            X = xt[sl]
            ON = ont[sl]
            x1 = x1s[sl]
            ss = sss[sl]
            if part == 'a':
                if i + 1 < 32:
                    load(i + 1)
                transposes(lambda k: ON[:, k * 128:(k + 1) * 128], 8, onT, 'onT', [('ont', sl)])
                return
            for half in range(2):
                b = bank()

                def fo(e, half=half, b=b):
                    for c in range(8):
                        ins = e.matmul(PS(b), onT[:, c, :], wout[:, c, half * 512:(half + 1) * 512], start=(c == 0), stop=(c == 7))
                    return ins
                sc.op('pe', fo, reads=[('onT', 0), ('onT', 4), 'wout'], writes=[('ps', b)])
                sc.op('dve', (lambda e, half=half, b=b: e.tensor_tensor(x1[:, half * 512:(half + 1) * 512], PS(b),
                                                                        X[:, half * 512:(half + 1) * 512], ALU.add)),
                      reads=[('ps', b), ('xt', sl)], writes=[('x1', sl, half)])
            sc.op('act', (lambda e: e.activation(hb2, x1, AF.Square, accum_out=ss[:, 0:1])), reads=[('x1', sl, 0), ('x1', sl, 1)],
                  writes=['hb2', ('ss0', sl)])
            sc.op('act', (lambda e: e.activation(ss[:, 1:2], ss[:, 0:1], AF.Sqrt, bias=epsb[:, 0:1], scale=1.0 / DM)),
                  reads=[('ss0', sl), 'epsb'], writes=[('ss1', sl)])
            sc.op('dve', (lambda e: e.reciprocal(ss[:, 1:2], ss[:, 1:2])), reads=[('ss1', sl)], writes=[('ss1r', sl)])
            sc.op('dve', (lambda e: e.scalar_tensor_tensor(hb2, x1, ss[:, 1:2], gffn, ALU.mult, ALU.mult)),
                  reads=[('x1', sl, 0), ('x1', sl, 1), ('ss1r', sl), 'gffn'], writes=['hb2'])

        def front2(i, part):
            sl = i % 2
            if part == 'a':
                transposes(lambda k: hb2[:, k * 128:(k + 1) * 128], 8, h2T, 'h2T', ['hb2'])
                return
            for fi, f0 in enumerate(range(0, DFF, 512)):
                w = min(512, DFF - f0)
                bg, bu = bank(), bank()

                def fg(e, f0=f0, w=w, bg=bg):
                    for kc in range(8):
                        ins = e.matmul(PS(bg, cols=w), h2T[:, kc, :], wg[:, kc, f0:f0 + w], start=(kc == 0), stop=(kc == 7))
                    return ins

                def fu(e, f0=f0, w=w, bu=bu):
                    for kc in range(8):
                        ins = e.matmul(PS(bu, cols=w), h2T[:, kc, :], wu[:, kc, f0:f0 + w], start=(kc == 0), stop=(kc == 7))
                    return ins
                sc.op('pe', fg, reads=[('h2T', 0), ('h2T', 4), 'wg'], writes=[('ps', bg)])
                sc.op('pe', fu, reads=[('h2T', 0), ('h2T', 4), 'wu'], writes=[('ps', bu)])
                u = fi % 2
                sc.op('act', (lambda e, w=w, bg=bg, u=u: e.activation(sil[u][:, 0:w], PS(bg, cols=w), AF.Silu)),
                      reads=[('ps', bg)], writes=[('sil', u)])
                sc.op('dve', (lambda e, f0=f0, w=w, bu=bu, u=u: e.tensor_tensor(act[:, f0:f0 + w], PS(bu, cols=w), sil[u][:, 0:w],
                                                                               ALU.mult)),
                      reads=[('ps', bu), ('sil', u)], writes=[('act', fi)])

        bdmap = {}

        def back(i, part):
            sl = i % 2
            x1 = x1s[sl]
            ss = sss[sl]
            Y = yout[sl]
            if part == 'a':
                transposes(lambda k: act[:, k * 128:(k + 1) * 128], 22, actT, 'actT', [('act', fi) for fi in range(6)])
                bdmap[i] = [bank(), bank()]
                return
            bd = bdmap[i]
            for k0 in (range(0, 16, 4) if part == 'b1' else range(16, 22, 4)):
                def fd(e, k0=k0):
                    for f in range(k0, min(22, k0 + 4)):
                        for half in range(2):
                            ins = e.matmul(PS(bd[half]), actT[:, f, :], wd[:, f, half * 512:(half + 1) * 512], start=(f == 0),
                                           stop=(f == 21))
                    return ins
                sc.op('pe', fd, reads=[('actT', k0), 'wd'], writes=[('ps', bd[0]), ('ps', bd[1])])
            if part == 'b1':
                return
            for half in range(2):
                b = bd[half]
                sc.op('dve', (lambda e, half=half, b=b: e.tensor_tensor(Y[:, half * 512:(half + 1) * 512], PS(b),
                                                                        x1[:, half * 512:(half + 1) * 512], ALU.add)),
                      reads=[('ps', b), ('x1', sl, half)], writes=[('yout', sl, half)])
            if last:
                sc.op('act', (lambda e: e.activation(act[:, 0:DM], Y, AF.Square, accum_out=ss[:, 2:3])),
                      reads=[('yout', sl, 0), ('yout', sl, 1)], writes=[('act', fi) for fi in range(2)] + [('ss2', sl)])
                sc.op('act', (lambda e: e.activation(ss[:, 3:4], ss[:, 2:3], AF.Sqrt, bias=epsb[:, 0:1], scale=1.0 / DM)),
                      reads=[('ss2', sl), 'epsb'], writes=[('ss3', sl)])
                sc.op('dve', (lambda e: e.reciprocal(ss[:, 3:4], ss[:, 3:4])), reads=[('ss3', sl)], writes=[('ss3r', sl)])
                sc.op('dve', (lambda e: e.scalar_tensor_tensor(Y, Y, ss[:, 3:4], gfin, ALU.mult, ALU.mult)),
                      reads=[('yout', sl, 0), ('yout', sl, 1), ('ss3r', sl), 'gfin'], writes=[('yout', sl, 0), ('yout', sl, 1)])
            sc.dma('sp', 'd_y%d' % sl, y_out[s][i * 128:(i + 1) * 128, :], Y, reads=[('yout', sl, 0), ('yout', sl, 1)],
                   writes=[('xres', s, i)])

        load(0)
        front1(0, 'a')
        front1(0, 'b')
        front2(0, 'a')
        front2(0, 'b')
        for i in range(32):
            nxt = i + 1 < 32
            if nxt:
                front1(i + 1, 'a')
            back(i, 'a')
            if nxt:
                front1(i + 1, 'b')
            back(i, 'b1')
            if nxt:
                front2(i + 1, 'a')
            back(i, 'b2')
            if nxt:
                front2(i + 1, 'b')
        sc.barrier()

    phases = os.environ.get('KPH', 'ac123')
    nlayers = int(os.environ.get('KLAYERS', '2'))
    if 'a' in phases:
        setup_amask()
    for l in range(nlayers):
        if 'c' in phases:
            setup_cmask(l)
        for s in range(nseq):
            xsrc = x_in[s] if l == 0 else y_out[s]
            phase_p1(s, l, xsrc)
            if '2' in phases:
                phase_p2(s, l)
            if '3' in phases:
                phase_p3(s, l, xsrc, last=(l == 1))
    if dbg:
        cvd = Carver()
        dst_ = [cvd.take([128, 16384], BF16) for _ in range(2)]
        di = 0
        srcs = dict(fmAC=fmAC, fmB=fmB, fmDq=fmDq, fmDk=fmDk, vA=vA, vB=vB, vC=vC, vD=vD, oN=oN, toeA=toeA, toeC=toeC)
        for nm, shp, dt in dbg:
            src_ = srcs[nm]
            fac = 2 if dt == F32 else 1
            if len(shp) == 3:
                views = [(src_.ap()[c], dbg_out[nm].ap()[c], shp[1], shp[2]) for c in range(shp[0])]
            else:
                nj = shp[0] // 128
                jc = max(1, 16384 // shp[1])
                views = [(src_.ap()[j0 * 128:min(nj, j0 + jc) * 128].rearrange("(j p) w -> p j w", p=128),
                          dbg_out[nm].ap()[j0 * 128:min(nj, j0 + jc) * 128].rearrange("(j p) w -> p j w", p=128),
                          128, (min(nj, j0 + jc) - j0) * shp[1]) for j0 in range(0, nj, jc)]
            for (sv, dv, rows, n) in views:
                st = dst_[di % 2][0:rows, 0:n * fac]
                if dt == F32:
                    st = st.bitcast(F32)[:, 0:n]
                if len(shp) == 2:
                    st = st.rearrange("p (j w) -> p j w", w=shp[1])
                sc.dma('sp', 'd_dbgi%d' % (di % 2), st, sv, reads=[('scr', nm)], writes=[('dbgst', di % 2)])
                sc.dma('sp', 'd_dbgo%d' % (di % 2), dv, st, reads=[('dbgst', di % 2)], writes=[('dbg', nm)])
                di += 1
    sc.barrier()
    sc.emit()
    return nc


def prep_shared(inp):
    f = lambda a: np.ascontiguousarray(np.asarray(a, dtype=np.float32))
    w_in = f(inp['w_in'])
    d = {}
    d['wext'] = _gather_cols(w_in, _wext_cols())
    uq = f(inp['d_w_uq'])
    swc = np.array([h * 96 + (dd if dd < 64 else 64 + ((dd - 64) ^ 16)) for h in range(4) for dd in range(96)])
    d['uq'] = np.ascontiguousarray(np.concatenate([uq, uq[:, :, swc]], axis=-1))
    ukv = f(inp['d_w_ukv'])
    kc_ = np.array([h * 128 + dd for h in range(4) for dd in range(64)])
    vc_ = np.array([h * 128 + 64 + dd for h in range(4) for dd in range(64)])
    d['ukv'] = np.ascontiguousarray(np.concatenate([ukv[:, :, kc_], ukv[:, :, vc_]], axis=-1))
    d['wout'] = f(inp['w_out'])
    d['wg'] = f(inp['w_gate'])
    d['wu'] = f(inp['w_up'])
    d['wd'] = f(inp['w_down'])
    d['rowg'] = np.ascontiguousarray(np.concatenate([f(inp['norm_mix']), f(inp['out_gain']), f(inp['norm_ffn']),
                                                     f(inp['final_norm'])[None, :]], axis=0))
    sw = np.array([dd ^ 16 for dd in range(64)])
    colc = np.zeros((2, 128, 8), np.float32)
    bq, bk = f(inp['b_q_gain']), f(inp['b_k_gain'])
    dqg, dkvg = f(inp['d_q_gain']), f(inp['d_kv_gain'])
    for l in range(2):
        colc[l, :, 0] = np.tile(bq[l], 2)
        colc[l, :, 1] = np.tile(bq[l][sw], 2)
        colc[l, :, 2] = np.tile(bk[l], 2)
        colc[l, :, 3] = np.tile(bk[l][sw], 2)
        colc[l, :, 4] = dqg[l][0:128]
        colc[l, :, 5] = dqg[l][128:256]
        colc[l, :, 6] = dkvg[l]
    d['colc'] = colc
    d['t5'] = f(inp['t5_bias'])
    d['rpbf'] = np.ascontiguousarray(f(inp['c_rpb'])[:, :, :, ::-1].reshape(2, 60, 31))
    d.update(_consts())
    return d


_NC_CACHE = {}


def kernel(**inputs):
    sh = prep_shared(inputs)
    xp = np.asarray(inputs['x_prompt'], dtype=np.float32)
    xs = np.asarray(inputs['x_sample'], dtype=np.float32)
    xall = np.concatenate([xp, xs], axis=0)
    nseq = xall.shape[0] // NCORES
    if nseq not in _NC_CACHE:
        _NC_CACHE[nseq] = build(nseq)
    nc = _NC_CACHE[nseq]
    in_maps = []
    for c in range(NCORES):
        m = dict(sh)
        m['x'] = np.ascontiguousarray(xall[c * nseq:(c + 1) * nseq])
        in_maps.append(m)
    res = run_bass_kernel_spmd(nc, in_maps, core_ids=list(range(NCORES)))
    y = np.concatenate([np.asarray(r['y'], dtype=np.float32) for r in res.results], axis=0)
    return (np.ascontiguousarray(y[:xp.shape[0]]), np.ascontiguousarray(y[xp.shape[0]:]))
```

```python
import os
from contextlib import ExitStack
import numpy as np
import ml_dtypes
import concourse.bass as bass
import concourse.mybir as mybir
from concourse.bass_utils import run_bass_kernel_spmd

F32, BF16 = mybir.dt.float32, mybir.dt.bfloat16
AF = mybir.ActivationFunctionType
ALU = mybir.AluOpType
AX = mybir.AxisListType

S = 4096
DM = 1024
DFF = 2816
NCOL = 3008
EPS = 1e-6
STOP = int(os.environ.get('KSTOP', '99'))
KV = int(os.environ.get('KV', '15'))
KDUP = int(os.environ.get('KDUP', '0'))
DEFER = int(os.environ.get('KDEFER', '1'))
NCORES = 8
ENG = ('pe', 'act', 'dve', 'pool', 'sp')


class Sched:
    def __init__(self, nc):
        self.nc = nc
        self.streams = {e: [] for e in ENG}
        self.cnt = {e: 0 for e in ENG}
        self.dcnt = {}
        self.lastw = {}
        self.readers = {}
        self.known = {e: {} for e in ENG}
        self.vc = {e: {} for e in ENG}

    def _need(self, eng, tok):
        kind, key, val = tok
        if kind == 'eng':
            if key == 'pe' and eng == 'pe':
                return
            sem = 'E_' + key
        else:
            sem = key
            val = self.dcnt[key]
        if self.known[eng].get(sem, 0) >= val:
            return
        self.known[eng][sem] = val
        self.streams[eng].append(('wait', sem, val))
        if kind == 'eng':
            snap = self.vc[key].get(val)
            if snap is not None:
                kn = self.known[eng]
                for x, c in zip(ENG, snap):
                    if c > kn.get('E_' + x, 0):
                        kn['E_' + x] = c

    def _deps(self, eng, reads, writes):
        for r in reads:
            t = self.lastw.get(r)
            if t is not None:
                self._need(eng, t)
        for w in writes:
            t = self.lastw.get(w)
            if t is not None:
                self._need(eng, t)
            for k, v in self.readers.get(w, {}).items():
                self._need(eng, (k[0], k[1], v))

    def _commit(self, tok, reads, writes):
        for r in reads:
            d = self.readers.setdefault(r, {})
            k = (tok[0], tok[1])
            if d.get(k, 0) < tok[2]:
                d[k] = tok[2]
        for w in writes:
            self.lastw[w] = tok
            self.readers[w] = {}

    def op(self, eng, fn, reads=(), writes=()):
        self._deps(eng, reads, writes)
        self.cnt[eng] += 1
        tok = ('eng', eng, self.cnt[eng])
        self.streams[eng].append(('op', fn))
        kn = self.known[eng]
        self.vc[eng][self.cnt[eng]] = tuple(kn.get('E_' + x, 0) for x in ENG)
        self._commit(tok, reads, writes)

    def dma(self, q, sem, out, in_, reads=(), writes=()):
        self._deps(q, reads, writes)
        self.dcnt[sem] = self.dcnt.get(sem, 0) + 16
        tok = ('dma', sem, self.dcnt[sem])
        self.streams[q].append(('dma', out, in_, sem))
        self._commit(tok, reads, writes)

    def barrier(self):
        for e in ENG:
            for x in ENG:
                if self.cnt[x] > 0 and not (x == 'pe' and e == 'pe'):
                    self._need(e, ('eng', x, self.cnt[x]))
            for s in list(self.dcnt.keys()):
                self._need(e, ('dma', s, 0))

    def emit(self):
        nc = self.nc
        names = ['E_' + e for e in ENG] + list(self.dcnt.keys())
        with ExitStack() as es:
            semh = {n: es.enter_context(nc.semaphore(n)) for n in names}
            block = es.enter_context(nc.Block())

            def run(e, name):
                esem = semh['E_' + name]
                for it in self.streams[name]:
                    if it[0] == 'wait':
                        e.wait_ge(semh[it[1]], it[2])
                    elif it[0] == 'op':
                        it[1](e).then_inc(esem, 1)
                    else:
                        e.dma_start(out=it[1], in_=it[2]).then_inc(semh[it[3]], 16)

            @block.tensor
            def _(e):
                run(e, 'pe')

            @block.scalar
            def _(e):
                run(e, 'act')

            @block.vector
            def _(e):
                run(e, 'dve')

            @block.gpsimd
            def _(e):
                run(e, 'pool')

            @block.sync
            def _(e):
                run(e, 'sp')


OFF = dict(aq=0, ak=256, av=512, bq=768, bk=1024, bv=1152, cq=1280, ck=1536, cv=1792, dq=2048, dkv=2304,
           dkr=2432)


def _wext_cols():
    cols = []
    for nm in ('aq', 'ak', 'cq', 'ck'):
        cols += list(range(OFF[nm], OFF[nm] + 256))
    sw = [d ^ 16 for d in range(64)]
    for pair in ((0, 2), (1, 3)):
        for h in pair:
            cols += [OFF['bq'] + h * 64 + d for d in range(64)]
    for pair in ((0, 2), (1, 3)):
        for h in pair:
            cols += [OFF['bq'] + h * 64 + sw[d] for d in range(64)]
    for h in (0, 1):
        cols += [OFF['bk'] + h * 64 + d for d in range(64)]
    for h in (0, 1):
        cols += [OFF['bk'] + h * 64 + sw[d] for d in range(64)]
    cols += list(range(OFF['dq'], OFF['dq'] + 256))
    cols += list(range(OFF['dkv'], OFF['dkv'] + 128))
    cols += [-1] * 64 + [OFF['dkr'] + d for d in range(32)]
    cols += [-1] * 64 + [OFF['dkr'] + (d ^ 16) for d in range(32)]
    cols += list(range(OFF['av'], OFF['av'] + 256))
    cols += list(range(OFF['bv'], OFF['bv'] + 128))
    cols += list(range(OFF['cv'], OFF['cv'] + 256))
    assert len(cols) == NCOL
    return np.array(cols)


def _gather_cols(w, cols):
    out = np.zeros(w.shape[:-1] + (len(cols),), dtype=w.dtype)
    m = cols >= 0
    out[..., m] = w[..., cols[m]]
    return out


def _t5_bucket(rel):
    nb, max_exact = 16, 8
    n = np.abs(rel)
    n_f = np.maximum(n, max_exact).astype(np.float32)
    large = max_exact + (np.log(n_f / np.float32(max_exact)) / np.float32(np.log(1024 / max_exact))
                         * np.float32(nb - max_exact)).astype(np.int32)
    large = np.minimum(large, nb - 1)
    return np.where(rel > 0, nb, 0) + np.where(n < max_exact, n, large)


LG = 3072
UA = 2944


def _consts():
    c = {}
    c['ident'] = np.eye(128, dtype=np.float32).astype(ml_dtypes.bfloat16)
    bo = np.zeros((128, 128), np.float32)
    bo[:64, :64] = 1
    bo[64:, 64:] = 1
    c['blockones'] = bo
    c['allones'] = np.ones((128, 128), np.float32)
    inv = (1.0 / (10000.0 ** (np.arange(0, 32, 2, dtype=np.float32) / np.float32(32)))).astype(np.float32)
    t = np.arange(S)
    cb = np.zeros((128, S), np.float32)
    sb = np.zeros((128, S), np.float32)
    for p in range(128):
        d = p % 64
        i = d % 16
        pos = (t // 64) if d < 32 else (t % 64)
        ang = (pos.astype(np.float32) * inv[i]).astype(np.float32).astype(np.float64)
        sgn = -1.0 if (d % 32) < 16 else 1.0
        cb[p] = np.cos(ang)
        sb[p] = sgn * np.sin(ang)
    cd = np.zeros((128, S), np.float32)
    sd = np.zeros((128, S), np.float32)
    for p in range(64, 96):
        d = p - 64
        i = d % 16
        ang = (t.astype(np.float32) * inv[i]).astype(np.float32).astype(np.float64)
        sgn = -1.0 if d < 16 else 1.0
        cd[p] = np.cos(ang)
        sd[p] = sgn * np.sin(ang)
    c['rope'] = np.stack([cb, sb, cd, sd], 0)
    kk = np.arange(LG)
    off = np.where(kk <= 2943, 1408 - kk, 4480 - kk)
    bk = _t5_bucket(off)
    mult = ((np.abs(off) <= 64).astype(np.float32)
            + ((np.abs(off) <= 256) & (off % 4 == 0)).astype(np.float32)
            + ((np.abs(off) <= 1024) & (off % 16 == 0)).astype(np.float32))
    mult[2944] = 0.0
    oh = np.zeros((32, LG), np.float32)
    oh[bk, np.arange(LG)] = 1.0
    c['a_onehot'] = oh
    c['a_mult'] = np.tile(mult[None, :], (4, 1)).astype(np.float32)
    cols = np.arange(64)
    cs = np.clip(cols - 8, 0, 48)
    kc = np.arange(64)[:, None]
    cv = ((kc >= cs[None, :]) & (kc < cs[None, :] + 16)).astype(np.float32)
    c['c_cv'] = np.concatenate([cv, cv], 0)
    return c


def build(nseq, dbg=None):
    nc = bass.Bass("TRN2", target_bir_lowering=False)
    sc = Sched(nc)

    def din(name, shape, dt=F32):
        return nc.dram_tensor(name, list(shape), dt, kind="ExternalInput")

    def dscr(name, shape, dt=BF16):
        return nc.dram_tensor(name, list(shape), dt, kind="Internal")

    x_in = din("x", [nseq, S, DM]).ap()
    y_out = nc.dram_tensor("y", [nseq, S, DM], F32, kind="ExternalOutput").ap()
    wext_in = din("wext", [2, DM, NCOL]).ap()
    uq_in = din("uq", [2, 256, 768]).ap()
    ukv_in = din("ukv", [2, 128, 512]).ap()
    wout_in = din("wout", [2, DM, DM]).ap()
    wg_in = din("wg", [2, DM, DFF]).ap()
    wu_in = din("wu", [2, DM, DFF]).ap()
    wd_in = din("wd", [2, DFF, DM]).ap()
    rowg_in = din("rowg", [7, DM]).ap()
    colc_in = din("colc", [2, 128, 8]).ap()
    t5_in = din("t5", [32, 4]).ap()
    rpb_in = din("rpbf", [2, 60, 31]).ap()
    ident_in = din("ident", [128, 128], BF16).ap()
    bones_in = din("blockones", [128, 128]).ap()
    aones_in = din("allones", [128, 128]).ap()
    rope_in = din("rope", [4, 128, S]).ap()
    aoh_in = din("a_onehot", [32, LG]).ap()
    amult_in = din("a_mult", [4, LG]).ap()
    ccv_in = din("c_cv", [128, 64]).ap()

    wext_b = dscr("wext_b", [2, DM, NCOL])
    uq_b = dscr("uq_b", [2, 256, 768])
    ukv_b = dscr("ukv_b", [2, 128, 512])
    wout_b = dscr("wout_b", [2, DM, DM])
    wg_b = dscr("wg_b", [2, DM, DFF])
    wu_b = dscr("wu_b", [2, DM, DFF])
    wd_b = dscr("wd_b", [2, DFF, DM])
    fmAC = dscr("fmAC", [8, 128, S])
    fmB = dscr("fmB", [3, 128, S])
    fmDq = dscr("fmDq", [4, 96, S])
    fmDk = dscr("fmDk", [4, 96, S])
    vA = dscr("vA", [S, 260])
    vB = dscr("vB", [S, 130])
    vC = dscr("vC", [S, 260])
    vD = dscr("vD", [S, 260])
    oN = dscr("oN", [S, DM])
    toeA = dscr("toeA", [4, 128, LG], F32)
    toeC = dscr("toeC", [60, 64, 128], F32)
    growA = dscr("growA", [4, LG], F32)
    growC = dscr("growC", [60, 128], F32)

    dbg_out = {}
    if dbg:
        for nm, shp, dt in dbg:
            dbg_out[nm] = nc.dram_tensor("dbg_" + nm, list(shp), dt, kind="ExternalOutput")

    def sb(name, shape, dt):
        return nc.alloc_sbuf_tensor(name, list(shape), dt)

    ident = sb("ident_s", [128, 128], BF16)
    bones = sb("bones_s", [128, 128], F32)
    aones = sb("aones_s", [128, 128], F32)
    colc = sb("colc_s", [128, 2, 8], F32)
    epsb = sb("epsb_s", [128, 1], F32)
    ARENA_ELEMS = 105000
    arena = sb("arena", [128, ARENA_ELEMS // 2], F32)
    NBANK = 6
    psA = nc.alloc_psum_tensor("psA", [128, NBANK * 512], F32)
    psT = nc.alloc_psum_tensor("psT", [128, 1024], F32)

    class Carver:
        def __init__(self):
            self.off = 0

        def take(self, shape, dt):
            n = int(np.prod(shape[1:]))
            ne = n * (2 if dt == F32 else 1)
            if self.off % 2:
                self.off += 1
            if ne % 2:
                ne += 1
            v = arena[0:shape[0], self.off // 2:(self.off + ne) // 2]
            self.last_off32 = self.off // 2
            self.off += ne
            assert self.off <= ARENA_ELEMS, (self.off, ARENA_ELEMS)
            if dt == BF16:
                v = v.bitcast(BF16)[:, 0:n]
            if len(shape) == 3:
                v = v.rearrange("p (a b) -> p a b", a=shape[1])
            elif len(shape) == 4:
                v = v.rearrange("p (a b c) -> p a b c", a=shape[1], b=shape[2])
            return v

    bank_rr = [0]

    def bank():
        b = bank_rr[0]
        bank_rr[0] = (b + 1) % 8
        return b

    def PSfull(b):
        return psA[:, b * 512:(b + 1) * 512] if b < NBANK else psT[:, (b - NBANK) * 512:(b - NBANK + 1) * 512]

    def PS(b, rows=128, cols=512, r0=0):
        return PSfull(b)[r0:r0 + rows, 0:cols]

    sc.dma('sp', 'd_c0', ident[:], ident_in, writes=['ident'])
    sc.dma('sp', 'd_c0', bones[:], bones_in, writes=['bones'])
    sc.dma('sp', 'd_c0', aones[:], aones_in, writes=['aones'])
    sc.op('dve', (lambda e: e.memset(epsb[:], EPS)), writes=['epsb'])
    sc.dma('sp', 'd_c0', colc[:], colc_in.rearrange("l p c -> p l c"), writes=['colc'])
    cvw = Carver()
    wstage = [cvw.take([128, 24064], BF16) for _ in range(2)]
    wi = 0
    deferred = []
    for l in range(2):
        for (dst, src_, nm, rows, ncol) in ((wext_b, wext_in, 'wext', DM, NCOL), (uq_b, uq_in, 'uq', 256, 768),
                                            (ukv_b, ukv_in, 'ukv', 128, 512), (wout_b, wout_in, 'wout', DM, DM),
                                            (wg_b, wg_in, 'wg', DM, DFF), (wu_b, wu_in, 'wu', DM, DFF),
                                            (wd_b, wd_in, 'wd', DFF, DM)):
            k = rows // 128
            if DEFER and not (l == 0 and nm in ('wext', 'uq', 'ukv')):
                deferred.append((dst, src_, nm, l, k, ncol))
                continue
            st = wstage[wi % 2][:, 0:k * ncol].rearrange("p (k n) -> p k n", k=k)
            sc.dma('pool', 'd_wc%d' % (wi % 2), st, src_[l].rearrange("(k p) n -> p k n", p=128), writes=[('wstage', wi % 2)])
            sc.dma('sp', 'd_wo%d' % (wi % 2), dst.ap()[l].rearrange("(k p) n -> p k n", p=128), st,
                   reads=[('wstage', wi % 2)], writes=[(nm + '_b', l)])
            wi += 1
    sc.barrier()

    def phase_p1(s, l, xsrc):
        cv = Carver()
        wext = cv.take([128, 8, NCOL], BF16)
        uq = cv.take([128, 2, 768], BF16)
        ukv = cv.take([128, 512], BF16)
        xt = [cv.take([128, 4, DM], F32) for _ in range(2)]
        ropet = [cv.take([128, 4, 512], F32) for _ in range(2)]
        hbs = [cv.take([128, 4, DM], BF16) for _ in range(2)]
        hT = cv.take([128, 8, 512], BF16)
        junk = cv.take([128, DM], BF16)
        ss = cv.take([128, 8], F32)
        sq = [cv.take([128, 512], F32) for _ in range(2)]
        rs = [cv.take([128, 512], F32) for _ in range(2)]
        t1 = [cv.take([128, 512], F32) for _ in range(2)]
        t2 = [cv.take([128, 512], F32) for _ in range(2)]
        dqn = cv.take([128, 2, 512], BF16)
        dkvn = cv.take([128, 512], BF16)
        stAC = [cv.take([128, 8, 512], BF16) for _ in range(2)]
        stB = [cv.take([128, 3, 512], BF16) for _ in range(2)]
        stDq = [cv.take([128, 4, 512], BF16) for _ in range(2)]
        stDk = [cv.take([128, 4, 512], BF16) for _ in range(2)]
        stVA = [cv.take([128, 4, 260], BF16) for _ in range(2)]
        stVB = [cv.take([128, 4, 130], BF16) for _ in range(2)]
        stVC = [cv.take([128, 4, 260], BF16) for _ in range(2)]
        stVD = [cv.take([128, 4, 260], BF16) for _ in range(2)]
        gmix = cv.take([128, DM], F32)
        sc.dma('sp', 'd_wgn', gmix, rowg_in[l:l + 1, :].partition_broadcast(128), writes=[('rowg', l)])

        sc.dma('sp', 'd_w1', wext, wext_b.ap()[l].rearrange("(k p) n -> p k n", p=128),
               reads=[('wext_b', l)] + [('wext_b', l, k_) for k_ in range(8)], writes=['wext'])
        sc.dma('sp', 'd_w1', uq, uq_b.ap()[l].rearrange("(k p) n -> p k n", p=128), reads=[('uq_b', l)] + [('uq_b', l, k_) for k_ in range(2)], writes=['uq'])
        sc.dma('sp', 'd_w1', ukv, ukv_b.ap()[l], reads=[('ukv_b', l)] + [('ukv_b', l, k_) for k_ in range(1)], writes=['ukv'])
        for i in range(2):
            for (st, H, nm) in ((stVA, 4, 'stVA'), (stVB, 2, 'stVB'), (stVC, 4, 'stVC'), (stVD, 4, 'stVD')):
                v = st[i].rearrange("p j (h e) -> p (j h) e", e=65)
                sc.op('pool', (lambda e, v=v: e.memset(v[:, :, 64:65], 1.0)), writes=[(nm, i, 'ones')])

        def load(tt):
            sl = tt % 2
            t0 = tt * 512
            sc.dma('sp', 'd_x%d' % sl, xt[sl], xsrc[t0:t0 + 512, :].rearrange("(j p) d -> p j d", p=128),
                   reads=[('xres', s, 4 * tt + j) for j in range(4)], writes=[('xt', sl)])
            sc.dma('sp', 'd_x%d' % sl, ropet[sl], rope_in[:, :, t0:t0 + 512].rearrange("a p t -> p a t"),
                   writes=[('ropet', sl)])

        load(0)

        def prep(tt):
            sl = tt % 2
            X = xt[sl]
            hb = hbs[sl]
            for j in range(4):
                sc.op('act', (lambda e, j=j: e.activation(junk, X[:, j, :], AF.Square, accum_out=ss[:, j:j + 1])),
                      reads=[('xt', sl)], writes=['junk', ('ssj', j)])
            sc.op('act', (lambda e: e.activation(ss[:, 4:8], ss[:, 0:4], AF.Sqrt, bias=epsb[:, 0:1], scale=1.0 / DM)),
                  reads=[('ssj', j) for j in range(4)] + ['epsb'], writes=['rstd0'])
            sc.op('dve', (lambda e: e.reciprocal(ss[:, 4:8], ss[:, 4:8])), reads=['rstd0'], writes=['rstd'])
            for j in range(4):
                sc.op('dve', (lambda e, j=j: e.scalar_tensor_tensor(hb[:, j, :], X[:, j, :], ss[:, 4 + j:5 + j], gmix,
                                                                   ALU.mult, ALU.mult)),
                      reads=[('xt', sl), 'rstd', ('rowg', l)], writes=[('hb', sl, j)])

        def do_tile(tt):
            sl = tt % 2
            t0 = tt * 512
            if tt + 1 < 8:
                load(tt + 1)
            R = ropet[sl]
            hb = hbs[sl]
            for kc in range(8):
                bT = bank()
                pT = PS(bT)

                def tr(e, kc=kc, pT=pT):
                    for j in range(4):
                        ins = e.matmul(pT[:, j * 128:(j + 1) * 128], hb[:, j, kc * 128:(kc + 1) * 128], ident[:], start=True,
                                       stop=True)
                    return ins
                sc.op('pe', tr, reads=[('hb', sl, j) for j in range(4)] + ['ident'], writes=[('ps', bT)])
                eng = 'act' if kc % 2 == 0 else 'dve'
                if eng == 'act':
                    sc.op('act', (lambda e, kc=kc, pT=pT: e.copy(hT[:, kc, :], pT)), reads=[('ps', bT)], writes=[('hT', kc)])
                else:
                    sc.op('dve', (lambda e, kc=kc, pT=pT: e.tensor_copy(hT[:, kc, :], pT)), reads=[('ps', bT)],
                          writes=[('hT', kc)])
            if tt + 1 < 8:
                prep(tt + 1)
            hT_r = [('hT', kc) for kc in range(8)]

            def fm_chunk(c0, width, b):
                def f(e):
                    for kc in range(8):
                        ins = e.matmul(PS(b, rows=width), wext[:, kc, c0:c0 + width], hT[:, kc, :], start=(kc == 0),
                                       stop=(kc == 7))
                    return ins
                sc.op('pe', f, reads=hT_r + ['wext'], writes=[('ps', b)])

            if STOP <= 2:
                return
            for c in range(8):
                b = bank()
                fm_chunk(c * 128, 128, b)
                sc.op('act', (lambda e, c=c, b=b: e.copy(stAC[sl][:, c, :], PS(b))), reads=[('ps', b)],
                      writes=[('stAC', sl, c)])
            sc.dma('sp', 'd_oAC%d' % sl, fmAC.ap()[:, :, t0:t0 + 512].rearrange("c p t -> p c t"), stAC[sl],
                   reads=[('stAC', sl, c) for c in range(8)], writes=[('fmAC', s)])

            if STOP <= 3:
                return
            for ci, (c_raw, c_sw, gcol) in enumerate(((8, 10, 0), (9, 11, 0), (12, 13, 2))):
                b1, b2, b3 = bank(), bank(), bank()
                u = ci % 2
                fm_chunk(1024 + (c_raw - 8) * 128, 128, b1)
                fm_chunk(1024 + (c_sw - 8) * 128, 128, b2)
                sc.op('act', (lambda e, b1=b1, u=u: e.activation(sq[u], PS(b1), AF.Square)), reads=[('ps', b1)],
                      writes=[('sq', u)])
                sc.op('pe', (lambda e, b3=b3, u=u: e.matmul(PS(b3), bones[:], sq[u], start=True, stop=True)),
                      reads=[('sq', u), 'bones'], writes=[('ps', b3)])
                sc.op('act', (lambda e, b3=b3, u=u: e.activation(rs[u], PS(b3), AF.Sqrt, bias=epsb[:, 0:1], scale=1.0 / 64)),
                      reads=[('ps', b3), 'epsb'], writes=[('rs', u)])
                sc.op('dve', (lambda e, u=u: e.reciprocal(rs[u], rs[u])), reads=[('rs', u)], writes=[('rs', u)])
                sc.op('dve', (lambda e, b1=b1, u=u, gcol=gcol: e.scalar_tensor_tensor(
                    t1[u], PS(b1), colc[:, l, gcol:gcol + 1], R[:, 0, :], ALU.mult, ALU.mult)),
                    reads=[('ps', b1), 'colc', ('ropet', sl)], writes=[('t1', u)])
                sc.op('dve', (lambda e, b2=b2, u=u, gcol=gcol: e.scalar_tensor_tensor(
                    t2[u], PS(b2), colc[:, l, gcol + 1:gcol + 2], R[:, 1, :], ALU.mult, ALU.mult)),
                    reads=[('ps', b2), 'colc', ('ropet', sl)], writes=[('t2', u)])
                sc.op('pool', (lambda e, u=u: e.tensor_tensor(t1[u], t1[u], t2[u], ALU.add)), reads=[('t1', u), ('t2', u)],
                      writes=[('t1', u)])
                sc.op('dve', (lambda e, u=u, ci=ci: e.tensor_tensor(stB[sl][:, ci, :], t1[u], rs[u], ALU.mult)),
                      reads=[('t1', u), ('rs', u)], writes=[('stB', sl, ci)])
            sc.dma('sp', 'd_oB%d' % sl, fmB.ap()[:, :, t0:t0 + 512].rearrange("c p t -> p c t"), stB[sl],
                   reads=[('stB', sl, c) for c in range(3)], writes=[('fmB', s)])

            if STOP <= 4:
                return
            bq0, bq1, bkv, bsq, bskv = bank(), bank(), bank(), bank(), bank()
            fm_chunk(1792, 128, bq0)
            fm_chunk(1920, 128, bq1)
            fm_chunk(2048, 128, bkv)
            sc.op('act', (lambda e: e.activation(sq[0], PS(bq0), AF.Square)), reads=[('ps', bq0)], writes=[('sq', 0)])
            sc.op('act', (lambda e: e.activation(sq[1], PS(bq1), AF.Square)), reads=[('ps', bq1)], writes=[('sq', 1)])

            def ssd(e):
                e.matmul(PS(bsq), aones[:], sq[0], start=True, stop=False)
                return e.matmul(PS(bsq), aones[:], sq[1], start=False, stop=True)
            sc.op('pe', ssd, reads=[('sq', 0), ('sq', 1), 'aones'], writes=[('ps', bsq)])
            sc.op('act', (lambda e: e.activation(rs[0], PS(bsq), AF.Sqrt, bias=epsb[:, 0:1], scale=1.0 / 256)),
                  reads=[('ps', bsq), 'epsb'], writes=[('rs', 0)])
            sc.op('dve', (lambda e: e.reciprocal(rs[0], rs[0])), reads=[('rs', 0)], writes=[('rs', 0)])
            sc.op('dve', (lambda e: e.scalar_tensor_tensor(dqn[:, 0, :], PS(bq0), colc[:, l, 4:5], rs[0], ALU.mult, ALU.mult)),
                  reads=[('ps', bq0), ('rs', 0), 'colc'], writes=[('dqn', 0)])
            sc.op('dve', (lambda e: e.scalar_tensor_tensor(dqn[:, 1, :], PS(bq1), colc[:, l, 5:6], rs[0], ALU.mult, ALU.mult)),
                  reads=[('ps', bq1), ('rs', 0), 'colc'], writes=[('dqn', 1)])
            sc.op('act', (lambda e: e.activation(t2[0], PS(bkv), AF.Square)), reads=[('ps', bkv)], writes=[('t2', 0)])
            sc.op('pe', (lambda e: e.matmul(PS(bskv), aones[:], t2[0], start=True, stop=True)), reads=[('t2', 0), 'aones'],
                  writes=[('ps', bskv)])
            sc.op('act', (lambda e: e.activation(rs[1], PS(bskv), AF.Sqrt, bias=epsb[:, 0:1], scale=1.0 / 128)),
                  reads=[('ps', bskv), 'epsb'], writes=[('rs', 1)])
            sc.op('dve', (lambda e: e.reciprocal(rs[1], rs[1])), reads=[('rs', 1)], writes=[('rs', 1)])
            sc.op('dve', (lambda e: e.scalar_tensor_tensor(dkvn, PS(bkv), colc[:, l, 6:7], rs[1], ALU.mult, ALU.mult)),
                  reads=[('ps', bkv), ('rs', 1), 'colc'], writes=['dkvn'])

            for h in range(4):
                b1, b2 = bank(), bank()

                def fq(e, h=h, b1=b1, o=0):
                    for kc in range(2):
                        ins = e.matmul(PS(b1, rows=96), uq[:, kc, o + h * 96:o + (h + 1) * 96], dqn[:, kc, :], start=(kc == 0),
                                       stop=(kc == 1))
                    return ins

                def fqs(e, h=h, b2=b2, o=384):
                    for kc in range(2):
                        ins = e.matmul(PS(b2, rows=96), uq[:, kc, o + h * 96:o + (h + 1) * 96], dqn[:, kc, :], start=(kc == 0),
                                       stop=(kc == 1))
                    return ins
                sc.op('pe', fq, reads=[('dqn', 0), ('dqn', 1), 'uq'], writes=[('ps', b1)])
                sc.op('pe', fqs, reads=[('dqn', 0), ('dqn', 1), 'uq'], writes=[('ps', b2)])
                sc.op('act', (lambda e, h=h, b1=b1: e.copy(stDq[sl][0:64, h, :], PS(b1, rows=64))), reads=[('ps', b1)],
                      writes=[('stDq', sl, h, 'n')])
                sc.op('dve', (lambda e, b1=b1: e.tensor_tensor(t1[0][64:96, :], PS(b1, rows=32, r0=64), R[64:96, 2, :], ALU.mult)),
                      reads=[('ps', b1), ('ropet', sl)], writes=[('t1', 0)])
                sc.op('dve', (lambda e, b2=b2: e.tensor_tensor(t2[0][64:96, :], PS(b2, rows=32, r0=64), R[64:96, 3, :], ALU.mult)),
                      reads=[('ps', b2), ('ropet', sl)], writes=[('t2', 0)])
                sc.op('dve', (lambda e, h=h: e.tensor_tensor(stDq[sl][64:96, h, :], t1[0][64:96, :], t2[0][64:96, :], ALU.add)),
                      reads=[('t1', 0), ('t2', 0)], writes=[('stDq', sl, h, 'r')])
            sc.dma('sp', 'd_oDq%d' % sl, fmDq.ap()[:, :, t0:t0 + 512].rearrange("h p t -> p h t"), stDq[sl][0:96],
                   reads=[('stDq', sl, h, x) for h in range(4) for x in 'nr'], writes=[('fmDq', s)])
            for h in range(4):
                b1 = bank()
                sc.op('pe', (lambda e, h=h, b1=b1: e.matmul(PS(b1, rows=64), ukv[:, h * 64:(h + 1) * 64], dkvn, start=True,
                                                           stop=True)), reads=['dkvn', 'ukv'], writes=[('ps', b1)])
                sc.op('act', (lambda e, h=h, b1=b1: e.copy(stDk[sl][0:64, h, :], PS(b1, rows=64))), reads=[('ps', b1)],
                      writes=[('stDk', sl, h, 'n')])
            b1, b2 = bank(), bank()
            fm_chunk(2176, 96, b1)
            fm_chunk(2272, 96, b2)
            sc.op('dve', (lambda e, b1=b1: e.tensor_tensor(t1[1][64:96, :], PS(b1, rows=32, r0=64), R[64:96, 2, :], ALU.mult)),
                  reads=[('ps', b1), ('ropet', sl)], writes=[('t1', 1)])
            sc.op('dve', (lambda e, b2=b2: e.tensor_tensor(t2[1][64:96, :], PS(b2, rows=32, r0=64), R[64:96, 3, :], ALU.mult)),
                  reads=[('ps', b2), ('ropet', sl)], writes=[('t2', 1)])
            for h in range(4):
                sc.op('dve', (lambda e, h=h: e.tensor_tensor(stDk[sl][64:96, h, :], t1[1][64:96, :], t2[1][64:96, :], ALU.add)),
                      reads=[('t1', 1), ('t2', 1)], writes=[('stDk', sl, h, 'r')])
            sc.dma('sp', 'd_oDk%d' % sl, fmDk.ap()[:, :, t0:t0 + 512].rearrange("h p t -> p h t"), stDk[sl][0:96],
                   reads=[('stDk', sl, h, x) for h in range(4) for x in 'nr'], writes=[('fmDk', s)])

            if STOP <= 5:
                return
            for j in range(4):
                b1, b2, b3 = bank(), bank(), bank()

                def fv(e, j=j, b=b1, c0=2368, w=384):
                    for kc in range(8):
                        ins = e.matmul(PS(b, cols=w), hT[:, kc, j * 128:(j + 1) * 128], wext[:, kc, c0:c0 + w], start=(kc == 0),
                                       stop=(kc == 7))
                    return ins

                def fv2(e, j=j, b=b2, c0=2752, w=256):
                    for kc in range(8):
                        ins = e.matmul(PS(b, cols=w), hT[:, kc, j * 128:(j + 1) * 128], wext[:, kc, c0:c0 + w], start=(kc == 0),
                                       stop=(kc == 7))
                    return ins
                sc.op('pe', fv, reads=hT_r + ['wext'], writes=[('ps', b1)])
                sc.op('pe', fv2, reads=hT_r + ['wext'], writes=[('ps', b2)])
                sc.op('pe', (lambda e, j=j, b3=b3: e.matmul(PS(b3, cols=256), dkvn[:, j * 128:(j + 1) * 128], ukv[:, 256:512],
                                                           start=True, stop=True)), reads=['dkvn', 'ukv'], writes=[('ps', b3)])

                def v65(st, j, H):
                    return st[sl][:, j, :].rearrange("p (h e) -> p h e", e=65)[:, :, 0:64]

                def p64(b, c0, H):
                    return PSfull(b)[:, c0:c0 + H * 64].rearrange("p (h d) -> p h d", d=64)
                if KV & 4:
                  sc.op('act', (lambda e, j=j, b1=b1: e.copy(v65(stVA, j, 4), p64(b1, 0, 4))), reads=[('ps', b1)],
                      writes=[('stVA', sl, j)])
                if KV & 8:
                  sc.op('act', (lambda e, j=j, b1=b1: e.copy(v65(stVB, j, 2), p64(b1, 256, 2))), reads=[('ps', b1)],
                      writes=[('stVB', sl, j)])
                if KV & 4:
                  sc.op('act', (lambda e, j=j, b2=b2: e.copy(v65(stVC, j, 4), p64(b2, 0, 4))), reads=[('ps', b2)],
                      writes=[('stVC', sl, j)])
                if KV & 8:
                  sc.op('act', (lambda e, j=j, b3=b3: e.copy(v65(stVD, j, 4), p64(b3, 0, 4))), reads=[('ps', b3)],
                      writes=[('stVD', sl, j)])
            if STOP <= 6:
                return
            for (st, dst, nm, w) in ((stVA, vA, 'stVA', 260), (stVB, vB, 'stVB', 130), (stVC, vC, 'stVC', 260),
                                     (stVD, vD, 'stVD', 260)):
                sc.dma('sp', 'd_o%s%d' % (nm, sl), dst.ap()[t0:t0 + 512, :].rearrange("(j p) w -> p j w", p=128), st[sl],
                       reads=[(nm, sl, j) for j in range(4)] + [(nm, sl, 'ones')], writes=[(nm[2:], s)])
        prep(0)
        for tt in range(8):
            do_tile(tt)
        sc.barrier()

    def setup_amask():
        cv = Carver()
        t5s = cv.take([128, 128], F32)
        aoh = cv.take([128, LG], F32)
        amu = cv.take([4, LG], F32)
        gr = cv.take([4, LG], F32)
        repa = cv.take([128, 4, LG], F32)
        sc.op('dve', (lambda e: e.memset(t5s, 0.0)), writes=['t5s'])
        sc.op('pool', (lambda e: e.memset(aoh, 0.0)), writes=['aoh'])
        sc.dma('sp', 'd_m0', t5s[0:32, 0:4], t5_in, writes=['t5s'])
        sc.dma('sp', 'd_m0', aoh[0:32, :], aoh_in, writes=['aoh'])
        sc.dma('sp', 'd_m0', amu, amult_in, writes=['amu'])
        for c0 in range(0, LG, 512):
            w = 512
            b = bank()
            sc.op('pe', (lambda e, b=b, c0=c0, w=w: e.matmul(PS(b, cols=w), t5s, aoh[:, c0:c0 + w], start=True, stop=True)),
                  reads=['t5s', 'aoh'], writes=[('ps', b)])
            sc.op('act', (lambda e, b=b, c0=c0, w=w: e.activation(gr[:, c0:c0 + w], PS(b, rows=4, cols=w), AF.Exp)),
                  reads=[('ps', b)], writes=[('gr', c0)])
            sc.op('dve', (lambda e, c0=c0, w=w: e.tensor_tensor(gr[:, c0:c0 + w], gr[:, c0:c0 + w], amu[:, c0:c0 + w], ALU.mult)),
                  reads=[('gr', c0), 'amu'], writes=[('gr', c0)])
        sc.dma('sp', 'd_m1', growA.ap(), gr, reads=[('gr', c0) for c0 in range(0, LG, 512)], writes=['growA'])
        sc.dma('sp', 'd_m1', repa, growA.ap().rearrange("(o h) n -> o h n", o=1).partition_broadcast(128) if False else
               bass.AP(growA, 0, [[0, 128], [LG, 4], [1, LG]]), reads=['growA'], writes=['repa'])
        sc.dma('sp', 'd_m1', toeA.ap().rearrange("h p n -> p h n"), repa, reads=['repa'], writes=['toeA'])
        sc.barrier()

    def setup_cmask(l):
        cv = Carver()
        e60 = cv.take([60, 32], F32)
        rpad = cv.take([60, 128], F32)
        repc = cv.take([64, 60, 128], F32)
        sc.dma('sp', 'd_m0', e60[:, 0:31], rpb_in[l], writes=['e60'])
        sc.op('dve', (lambda e: e.memset(rpad, 0.0)), writes=['rpad'])
        sc.op('act', (lambda e: e.activation(rpad[:, 0:16], e60[:, 15:31], AF.Exp)), reads=['e60', 'rpad'], writes=['rpad'])
        sc.op('act', (lambda e: e.activation(rpad[:, 113:128], e60[:, 0:15], AF.Exp)), reads=['e60', 'rpad'], writes=['rpad'])
        sc.dma('sp', 'd_m1', growC.ap(), rpad, reads=['rpad'], writes=['growC'])
        sc.dma('sp', 'd_m1', repc, bass.AP(growC, 0, [[0, 64], [128, 60], [1, 128]]), reads=['growC'], writes=['repc'])
        sc.dma('sp', 'd_m1', toeC.ap().rearrange("r p n -> p r n"), repc, reads=['repc'], writes=['toeC'])
        sc.barrier()

    def phase_p2(s, l):
        cv = Carver()
        KTs = [cv.take([128, 4, S], BF16) for _ in range(2)]
        Ves = [cv.take([128, 32, 260], BF16) for _ in range(2)]
        Vo = cv.take([128, 32, 260], BF16)
        Qt = [cv.take([128, 4, 512], BF16) for _ in range(2)]
        PT = [cv.take([128, 2, 512], BF16) for _ in range(3)]
        strip = cv.take([128, 4, UA], BF16)
        cm32 = cv.take([128, 14, 4, 64], F32)
        cmask = cv.take([128, 14, 4, 64], BF16)
        ccv = cv.take([128, 64], F32)
        gout = cv.take([128, DM], F32)
        og = [cv.take([128, 4, 256], F32) for _ in range(2)]
        onb = [cv.take([128, 4, 256], BF16) for _ in range(2)]
        junk = cv.take([128, 256], BF16)
        rc = cv.take([128, 16], F32)
        ssn = cv.take([128, 8], F32)
        sc.dma('sp', 'd_w1', gout, rowg_in[2 + l:3 + l, :].partition_broadcast(128), writes=['gout'])
        if deferred and s == 0 and l == 0:
            cst = [cv.take([128, NCOL], BF16) for _ in range(2)]
            ji = 0
            for (dst, src_, nm, l_, k, ncol) in deferred:
                for k_ in range(k):
                    slot = ji % 2
                    sc.dma('pool', 'd_cj%d' % slot, cst[slot][:, 0:ncol], src_[l_][k_ * 128:(k_ + 1) * 128, :], writes=[('cst', slot)])
                    sc.dma('pool', 'd_cjo%d' % slot, dst.ap()[l_][k_ * 128:(k_ + 1) * 128, :], cst[slot][:, 0:ncol],
                           reads=[('cst', slot)], writes=[(nm + '_b', l_, k_)])
                    ji += 1
            del deferred[:]
        psT_b = [psT[:, 0:512], psT[:, 512:1024]]
        obanks = [(('ps', 4), PS(4)), (('ps', 5), PS(5)), (('psT', 0), psT_b[0]), (('psT', 1), psT_b[1])]
        pt_rr = [0]
        ep_rr = [0]

        def epilogue_head(oreg, oap, ogt, h, nj, u):
            o3 = oap[:, 0:nj * 65].rearrange("p (j e) -> p j e", e=65)
            sc.op('dve', (lambda e: e.reciprocal(rc[:, u * 4:u * 4 + nj], o3[:, :, 64])), reads=[oreg], writes=[('rc', u)])
            for j in range(nj):
                sc.op('dve', (lambda e, j=j: e.tensor_scalar_mul(ogt[:, j, h * 64:(h + 1) * 64], o3[:, j, 0:64],
                                                                 rc[:, u * 4 + j:u * 4 + j + 1])),
                      reads=[oreg, ('rc', u)], writes=[('og', h, j)])

        def group_norm_store(g, t0, ogt, slot):
            for j in range(4):
                sc.op('act', (lambda e, j=j: e.activation(junk, ogt[:, j, :], AF.Square, accum_out=ssn[:, j:j + 1])),
                      reads=[('og', h, j) for h in range(4)], writes=['junk2', ('ssn', j)])
            sc.op('act', (lambda e: e.activation(ssn[:, 4:8], ssn[:, 0:4], AF.Ln, bias=epsb[:, 0:1], scale=1.0 / 256)),
                  reads=[('ssn', j) for j in range(4)] + ['epsb'], writes=['ssr0'])
            sc.op('act', (lambda e: e.activation(ssn[:, 4:8], ssn[:, 4:8], AF.Exp, scale=-0.5)), reads=['ssr0'], writes=['ssr'])
            for j in range(4):
                sc.op('dve', (lambda e, j=j: e.scalar_tensor_tensor(onb[slot][:, j, :], ogt[:, j, :], ssn[:, 4 + j:5 + j],
                                                                    gout[:, g * 256:(g + 1) * 256], ALU.mult, ALU.mult)),
                      reads=[('og', h, j) for h in range(4)] + ['ssr', 'gout'], writes=[('onb', slot, j)])
            sc.dma('sp', 'd_on%d' % slot, oN.ap()[t0:t0 + 512, g * 256:(g + 1) * 256].rearrange("(j p) w -> p j w", p=128),
                   onb[slot], reads=[('onb', slot, j) for j in range(4)], writes=[('oN', s)])

        def dense_group(g, name, bi, mode):
            KT, Ve = KTs[bi], Ves[bi]
            kreg, vreg_ = ('KT', bi), ('Ve', bi)
            nchunk = {'A': 2, 'B': 1, 'D': 4}[name]
            scale = (96.0 if name == 'D' else 64.0) ** -0.5
            if mode == 'load':
                if name == 'A':
                    sc.dma('sp', 'd_k%d' % bi, KT[:, 0:2, :], fmAC.ap()[2:4].rearrange("c p t -> p c t"), reads=[('fmAC', s)], writes=[kreg])
                    vsrc, vw, nq = vA, 260, 2
                    for h in range(4):
                        sc.dma('pool', 'd_strip', strip[:, h, :], bass.AP(toeA, h * 128 * LG, [[LG - 1, 128], [1, UA]]), reads=['toeA'],
                               writes=['strip'])
                elif name == 'B':
                    sc.dma('sp', 'd_k%d' % bi, KT[:, 0:1, :], fmB.ap()[2:3].rearrange("c p t -> p c t"), reads=[('fmB', s)], writes=[kreg])
                    vsrc, vw, nq = vB, 130, 2
                else:
                    sc.dma('sp', 'd_k%d' % bi, KT[0:96, 0:4, :], fmDk.ap().rearrange("h p t -> p h t"), reads=[('fmDk', s)], writes=[kreg])
                    vsrc, vw, nq = vD, 260, 4
                sc.dma('sp', 'd_k%d' % bi, Ve[:, :, 0:vw], vsrc.ap().rearrange("(j p) w -> p j w", p=128), reads=[(vsrc.name if False else name + 'v', s)] if False else [({'A': 'VA', 'B': 'VB', 'D': 'VD'}[name], s)], writes=[vreg_])

                return

            def loadq(qt):
                sl = qt % 2
                t0 = qt * 512
                if name == 'A':
                    sc.dma('sp', 'd_q%d' % sl, Qt[sl][:, 0:2, :], fmAC.ap()[0:2, :, t0:t0 + 512].rearrange("c p t -> p c t"),
                           reads=[('fmAC', s)], writes=[('Qt', sl)])
                elif name == 'B':
                    sc.dma('sp', 'd_q%d' % sl, Qt[sl][:, 0:2, :], fmB.ap()[0:2, :, t0:t0 + 512].rearrange("c p t -> p c t"),
                           reads=[('fmB', s)], writes=[('Qt', sl)])
                else:
                    sc.dma('sp', 'd_q%d' % sl, Qt[sl][0:96, 0:4, :], fmDq.ap()[:, :, t0:t0 + 512].rearrange("h p t -> p h t"),
                           reads=[('fmDq', s)], writes=[('Qt', sl)])

            def kbs_of(qt):
                t0 = qt * 512
                if name == 'A':
                    return [kb for kb in range(32) if (kb * 128 + 127 >= t0 - 1024) and (kb * 128 <= t0 + 511 + 1024)]
                return list(range(32))

            def views(qt, pr):
                sl = qt % 2
                if name == 'A':
                    heads = (2 * pr, 2 * pr + 1)
                    qv = [Qt[sl][0:64, pr, :], Qt[sl][64:128, pr, :]]
                    kv = [KT[0:64, pr, :], KT[64:128, pr, :]]
                    vh = heads
                elif name == 'B':
                    heads = (pr, pr + 2)
                    qv = [Qt[sl][0:64, pr, :], Qt[sl][64:128, pr, :]]
                    kv = [KT[0:64, 0, :], KT[64:128, 0, :]]
                    vh = (0, 1)
                else:
                    heads = (2 * pr, 2 * pr + 1)
                    qv = [Qt[sl][0:96, heads[0], :], Qt[sl][0:96, heads[1], :]]
                    kv = [KT[0:96, heads[0], :], KT[0:96, heads[1], :]]
                    vh = heads
                return heads, qv, kv, vh

            units = []
            for qt in range(8):
                ks = kbs_of(qt)
                for pr in range(2):
                    for ki, kb in enumerate(ks):
                        units.append(dict(qt=qt, pr=pr, ki=ki, kb=kb, nk=len(ks), idx=len(units)))

            def emit_qk(u):
                qt, pr, kb = u['qt'], u['pr'], u['kb']
                sl = qt % 2
                if pr == 0 and u['ki'] == 0 and qt + 1 < 8:
                    loadq(qt + 1)
                heads, qv, kv, vh = views(qt, pr)
                sb_ = (u['idx'] % 3) * 2

                def qk(e, kb=kb, sb_=sb_, qv=qv, kv=kv):
                    for _ in range(1 + KDUP):
                        e.matmul(PS(sb_), kv[0][:, kb * 128:(kb + 1) * 128], qv[0], start=True, stop=True)
                        ins = e.matmul(PS(sb_ + 1), kv[1][:, kb * 128:(kb + 1) * 128], qv[1], start=True, stop=True)
                    return ins
                sc.op('pe', qk, reads=[kreg, ('Qt', sl)], writes=[('ps', sb_), ('ps', sb_ + 1)])

            def emit_rest(u):
                qt, pr, kb, ki, nk = u['qt'], u['pr'], u['kb'], u['ki'], u['nk']
                t0 = qt * 512
                heads, qv, kv, vh = views(qt, pr)
                sb_ = (u['idx'] % 3) * 2
                pslot = u['idx'] % 3
                P_ = PT[pslot]
                ob = [obanks[2], obanks[3]]
                ogt = og[qt % 2]
                sc.op('act', (lambda e: e.activation(P_.rearrange("p a n -> p (a n)"), psA[:, sb_ * 512:(sb_ + 2) * 512], AF.Exp,
                                                     scale=scale)),
                      reads=[('ps', sb_), ('ps', sb_ + 1)], writes=[('PT', pslot, 0), ('PT', pslot, 1)])
                if name == 'A':
                    off = 1408 - (kb * 128 - t0)
                    h0 = heads[0]
                    sc.op('dve', (lambda e: e.tensor_tensor(P_, P_, strip[:, h0:h0 + 2, off:off + 512], ALU.mult)),
                          reads=[('PT', pslot, 0), ('PT', pslot, 1), 'strip'], writes=[('PT', pslot, 0), ('PT', pslot, 1)])
                first, last = (ki == 0), (ki == nk - 1)

                def pv(e):
                    for i in range(2):
                        for j in range(4):
                            ins = e.matmul(ob[i][1][:, j * 65:(j + 1) * 65], P_[:, i, j * 128:(j + 1) * 128],
                                           Ve[:, kb, vh[i] * 65:(vh[i] + 1) * 65], start=(first and j == 0), stop=last,
                                           skip_group_check=True)
                    return ins
                sc.op('pe', pv, reads=[('PT', pslot, 0), ('PT', pslot, 1), vreg_], writes=[ob[0][0], ob[1][0]])
                if last:
                    for i in range(2):
                        epilogue_head(ob[i][0], ob[i][1], ogt, heads[i], 4, i)
                    if pr == 1:
                        group_norm_store(g, t0, ogt, qt % 2)

            loadq(0)
            emit_qk(units[0])
            emit_qk(units[1])
            for n, u in enumerate(units):
                if n + 2 < len(units):
                    emit_qk(units[n + 2])
                emit_rest(u)

        def group_c(g, bi, mode):
            KT, Ve = KTs[bi], Ves[bi]
            kreg, vreg_ = ('KT', bi), ('Ve', bi)
            if mode == 'load':
                sc.dma('sp', 'd_k%d' % bi, KT[:, 0:2, :], fmAC.ap()[6:8].rearrange("c p t -> p c t"), reads=[('fmAC', s)], writes=[kreg])
                sc.dma('sp', 'd_k%d' % bi, Ve, vC.ap().rearrange("(j p) w -> p j w", p=128), reads=[('VC', s)], writes=[vreg_])
                sc.dma('sp', 'd_kc', Vo[:, 0:31, :], vC.ap()[64:64 + 31 * 128, :].rearrange("(j p) w -> p j w", p=128), reads=[('VC', s)],
                       writes=['Vo'])
                sc.dma('sp', 'd_kc', ccv, ccv_in, writes=['ccv'])
                for pos, h in enumerate((0, 2, 1, 3)):
                    for a_ in range(2):
                        sc.dma('sp', 'd_kc', cm32[a_ * 64:(a_ + 1) * 64, :, pos, :],
                               bass.AP(toeC, (h * 15 + a_) * 8192, [[127, 64], [8192, 14], [1, 64]]), reads=['toeC'], writes=['cm32'])
                for m in range(14):
                    for pos in range(4):
                        sc.op('dve', (lambda e, m=m, pos=pos: e.tensor_tensor(cmask[:, m, pos, :], cm32[:, m, pos, :], ccv, ALU.mult)),
                              reads=['cm32', 'ccv'], writes=['cmask'])

                return

            def loadq(qt):
                sl = qt % 2
                t0 = qt * 512
                sc.dma('sp', 'd_q%d' % sl, Qt[sl][:, 0:2, :], fmAC.ap()[4:6, :, t0:t0 + 512].rearrange("c p t -> p c t"),
                       reads=[('fmAC', s)], writes=[('Qt', sl)])

            units = []
            for r in range(64):
                for jw in range(4):
                    units.append(dict(r=r, jw=jw, idx=len(units)))

            def emit_qk(u):
                r, jw = u['r'], u['jw']
                qt = r // 8
                sl = qt % 2
                if r % 8 == 0 and jw == 0 and qt + 1 < 8:
                    loadq(qt + 1)
                rs_ = min(max(r - 4, 0), 56)
                kw = rs_ * 64 + 128 * jw
                sb_ = (u['idx'] % 3) * 2
                qc = (r % 8) * 64

                def qk(e):
                    for h in range(4):
                        c, lo = h // 2, 64 * (h % 2)
                        bnk = sb_ + (h % 2)
                        ins = e.matmul(psA[:, bnk * 512 + (h // 2) * 64: bnk * 512 + (h // 2) * 64 + 64],
                                       KT[lo:lo + 64, c, kw:kw + 128], Qt[sl][lo:lo + 64, c, qc:qc + 64], start=True, stop=True)
                    return ins
                sc.op('pe', qk, reads=[kreg, ('Qt', sl)], writes=[('ps', sb_), ('ps', sb_ + 1)])

            def emit_rest(u):
                r, jw = u['r'], u['jw']
                qt = r // 8
                ogt = og[qt % 2]
                i4 = (r % 8) // 2
                half = r % 2
                rs_ = min(max(r - 4, 0), 56)
                oreg, oap = obanks[2 + (r // 2) % 2]
                kw = rs_ * 64 + 128 * jw
                m = rs_ + 2 * jw - r + 7
                sb_ = (u['idx'] % 3) * 2
                pslot = u['idx'] % 3
                P_ = PT[pslot]
                pout = P_[:, :, 0:128]
                pin = psA[:, sb_ * 512:(sb_ + 2) * 512].rearrange("p (b n) -> p b n", b=2)[:, :, 0:128]
                sc.op('act', (lambda e: e.activation(pout, pin, AF.Exp, scale=0.125)),
                      reads=[('ps', sb_), ('ps', sb_ + 1)], writes=[('PT', pslot, 0), ('PT', pslot, 1)])
                mk = cmask[:, m, :, :].rearrange("p (b u) c -> p b (u c)", b=2)
                sc.op('dve', (lambda e: e.tensor_tensor(pout, pout, mk, ALU.mult)),
                      reads=[('PT', pslot, 0), ('PT', pslot, 1), 'cmask'], writes=[('PT', pslot, 0), ('PT', pslot, 1)])
                if kw % 128 == 0:
                    vt, vi, vreg = Ve, kw // 128, vreg_
                else:
                    vt, vi, vreg = Vo, (kw - 64) // 128, 'Vo'
                first, last = (jw == 0), (jw == 3)

                def pv(e):
                    for h in range(4):
                        b_, u_ = h % 2, h // 2
                        ins = e.matmul(oap[half * 64:half * 64 + 64, h * 65:(h + 1) * 65],
                                       P_[:, b_, u_ * 64:u_ * 64 + 64], vt[:, vi, h * 65:(h + 1) * 65], start=(first and h == 0),
                                       stop=last, skip_group_check=True)
                    return ins
                sc.op('pe', pv, reads=[('PT', pslot, 0), ('PT', pslot, 1), vreg], writes=[oreg])
                if half == 1 and last:
                    o3 = oap[:, 0:260].rearrange("p (h e) -> p h e", e=65)
                    sc.op('dve', (lambda e: e.reciprocal(rc[:, 0:4], o3[:, :, 64])), reads=[oreg], writes=[('rc', 0)])
                    for h in range(4):
                        sc.op('dve', (lambda e, h=h: e.tensor_scalar_mul(ogt[:, i4, h * 64:(h + 1) * 64], o3[:, h, 0:64],
                                                                         rc[:, h:h + 1])),
                              reads=[oreg, ('rc', 0)], writes=[('og', h, i4)])
                    if r % 8 == 7:
                        group_norm_store(g, qt * 512, ogt, qt % 2)

            loadq(0)
            emit_qk(units[0])
            emit_qk(units[1])
            for n, u in enumerate(units):
                if n + 2 < len(units):
                    emit_qk(units[n + 2])
                emit_rest(u)

        grp = os.environ.get('KGROUPS', 'ABCD')
        order = [x for x in (('B', 1), ('D', 3), ('A', 0), ('C', 2)) if x[0] in grp]

        def run_group(k, mode):
            nm, g = order[k]
            if nm == 'C':
                group_c(g, k % 2, mode)
            else:
                dense_group(g, nm, k % 2, mode)
        run_group(0, 'load')
        for k in range(len(order)):
            if k + 1 < len(order):
                run_group(k + 1, 'load')
            run_group(k, 'compute')
        sc.barrier()

    def phase_p3(s, l, xsrc, last):
        cv = Carver()
        wout = cv.take([128, 8, DM], BF16)
        wg = cv.take([128, 8, DFF], BF16)
        wu = cv.take([128, 8, DFF], BF16)
        wd = cv.take([128, 22, DM], BF16)
        gffn = cv.take([128, DM], F32)
        gfin = cv.take([128, DM], F32)
        xt = [cv.take([128, DM], F32) for _ in range(2)]
        ont = [cv.take([128, DM], BF16) for _ in range(2)]
        onT = cv.take([128, 8, 128], BF16)
        x1s = [cv.take([128, DM], F32) for _ in range(2)]
        hb2 = cv.take([128, DM], BF16)
        h2T = cv.take([128, 8, 128], BF16)
        sil = [cv.take([128, 512], BF16) for _ in range(2)]
        act = cv.take([128, DFF], BF16)
        actT = cv.take([128, 22, 128], BF16)
        yout = [cv.take([128, DM], F32) for _ in range(2)]
        sss = [cv.take([128, 8], F32) for _ in range(2)]
        sc.dma('sp', 'd_wo_', wout, wout_b.ap()[l].rearrange("(k p) n -> p k n", p=128), reads=[('wout_b', l)] + [('wout_b', l, k_) for k_ in range(8)], writes=['wout'])
        sc.dma('sp', 'd_wg_', wg, wg_b.ap()[l].rearrange("(k p) n -> p k n", p=128), reads=[('wg_b', l)] + [('wg_b', l, k_) for k_ in range(8)], writes=['wg'])
        sc.dma('sp', 'd_wu_', wu, wu_b.ap()[l].rearrange("(k p) n -> p k n", p=128), reads=[('wu_b', l)] + [('wu_b', l, k_) for k_ in range(8)], writes=['wu'])
        sc.dma('sp', 'd_wd_', wd, wd_b.ap()[l].rearrange("(k p) n -> p k n", p=128), reads=[('wd_b', l)] + [('wd_b', l, k_) for k_ in range(22)], writes=['wd'])
        sc.dma('sp', 'd_wgn', gffn, rowg_in[4 + l:5 + l, :].partition_broadcast(128), writes=['gffn'])
        sc.dma('sp', 'd_wgn', gfin, rowg_in[6:7, :].partition_broadcast(128), writes=['gfin'])
        psT_b = [psT[:, 0:512], psT[:, 512:1024]]
        tb = [0]

        def tbank():
            k = tb[0] % 2
            tb[0] += 1
            return ('psT', k), psT_b[k]

        def load(i):
            sl = i % 2
            sc.dma('sp', 'd_x%d' % sl, xt[sl], xsrc[i * 128:(i + 1) * 128, :], reads=[('xres', s, i)], writes=[('xt', sl)])
            sc.dma('sp', 'd_x%d' % sl, ont[sl], oN.ap()[i * 128:(i + 1) * 128, :], reads=[('oN', s)], writes=[('ont', sl)])

        def transposes(srcfn, n, dst, dreg, sreads):
            for k0 in range(0, n, 4):
                kk = min(4, n - k0)
                bT = bank()
                treg, tap = ('ps', bT), PS(bT)

                def tr(e, k0=k0, kk=kk, tap=tap):
                    for q in range(kk):
                        ins = e.matmul(tap[:, q * 128:(q + 1) * 128], srcfn(k0 + q), ident[:], start=True, stop=True)
                    return ins
                sc.op('pe', tr, reads=sreads + ['ident'], writes=[treg])
                dv = dst[:, k0:k0 + kk, :].rearrange("p k t -> p (k t)")
                if (k0 // 4) % 2 == 0:
                    sc.op('dve', (lambda e, dv=dv, tap=tap, kk=kk: e.tensor_copy(dv, tap[:, 0:kk * 128])), reads=[treg],
                          writes=[(dreg, k0)])
                else:
                    sc.op('act', (lambda e, dv=dv, tap=tap, kk=kk: e.copy(dv, tap[:, 0:kk * 128])), reads=[treg],
                          writes=[(dreg, k0)])

        def front1(i, part):
            sl = i % 2
            X = xt[sl]
            ON = ont[sl]
            x1 = x1s[sl]
            ss = sss[sl]
            if part == 'a':
                if i + 1 < 32:
                    load(i + 1)
                transposes(lambda k: ON[:, k * 128:(k + 1) * 128], 8, onT, 'onT', [('ont', sl)])
                return
            for half in range(2):
                b = bank()

                def fo(e, half=half, b=b):
                    for c in range(8):
                        ins = e.matmul(PS(b), onT[:, c, :], wout[:, c, half * 512:(half + 1) * 512], start=(c == 0), stop=(c == 7))
                    return ins
                sc.op('pe', fo, reads=[('onT', 0), ('onT', 4), 'wout'], writes=[('ps', b)])
                sc.op('dve', (lambda e, half=half, b=b: e.tensor_tensor(x1[:, half * 512:(half + 1) * 512], PS(b),
                                                                        X[:, half * 512:(half + 1) * 512], ALU.add)),
                      reads=[('ps', b), ('xt', sl)], writes=[('x1', sl, half)])
            sc.op('act', (lambda e: e.activation(hb2, x1, AF.Square, accum_out=ss[:, 0:1])), reads=[('x1', sl, 0), ('x1', sl, 1)],
                  writes=['hb2', ('ss0', sl)])
            sc.op('act', (lambda e: e.activation(ss[:, 1:2], ss[:, 0:1], AF.Sqrt, bias=epsb[:, 0:1], scale=1.0 / DM)),
                  reads=[('ss0', sl), 'epsb'], writes=[('ss1', sl)])
            sc.op('dve', (lambda e: e.reciprocal(ss[:, 1:2], ss[:, 1:2])), reads=[('ss1', sl)], writes=[('ss1r', sl)])
            sc.op('dve', (lambda e: e.scalar_tensor_tensor(hb2, x1, ss[:, 1:2], gffn, ALU.mult, ALU.mult)),
                  reads=[('x1', sl, 0), ('x1', sl, 1), ('ss1r', sl), 'gffn'], writes=['hb2'])

        def front2(i, part):
            sl = i % 2
            if part == 'a':
                transposes(lambda k: hb2[:, k * 128:(k + 1) * 128], 8, h2T, 'h2T', ['hb2'])
                return
            for fi, f0 in enumerate(range(0, DFF, 512)):
                w = min(512, DFF - f0)
                bg, bu = bank(), bank()

                def fg(e, f0=f0, w=w, bg=bg):
                    for kc in range(8):
                        ins = e.matmul(PS(bg, cols=w), h2T[:, kc, :], wg[:, kc, f0:f0 + w], start=(kc == 0), stop=(kc == 7))
                    return ins

                def fu(e, f0=f0, w=w, bu=bu):
                    for kc in range(8):
                        ins = e.matmul(PS(bu, cols=w), h2T[:, kc, :], wu[:, kc, f0:f0 + w], start=(kc == 0), stop=(kc == 7))
                    return ins
                sc.op('pe', fg, reads=[('h2T', 0), ('h2T', 4), 'wg'], writes=[('ps', bg)])
                sc.op('pe', fu, reads=[('h2T', 0), ('h2T', 4), 'wu'], writes=[('ps', bu)])
                u = fi % 2
                sc.op('act', (lambda e, w=w, bg=bg, u=u: e.activation(sil[u][:, 0:w], PS(bg, cols=w), AF.Silu)),
                      reads=[('ps', bg)], writes=[('sil', u)])
                sc.op('dve', (lambda e, f0=f0, w=w, bu=bu, u=u: e.tensor_tensor(act[:, f0:f0 + w], PS(bu, cols=w), sil[u][:, 0:w],
                                                                               ALU.mult)),
                      reads=[('ps', bu), ('sil', u)], writes=[('act', fi)])

        bdmap = {}

        def back(i, part):
            sl = i % 2
            x1 = x1s[sl]
            ss = sss[sl]
            Y = yout[sl]
            if part == 'a':
                transposes(lambda k: act[:, k * 128:(k + 1) * 128], 22, actT, 'actT', [('act', fi) for fi in range(6)])
                bdmap[i] = [bank(), bank()]
                return
            bd = bdmap[i]
            for k0 in (range(0, 16, 4) if part == 'b1' else range(16, 22, 4)):
                def fd(e, k0=k0):
                    for f in range(k0, min(22, k0 + 4)):
                        for half in range(2):
                            ins = e.matmul(PS(bd[half]), actT[:, f, :], wd[:, f, half * 512:(half + 1) * 512], start=(f == 0),
                                           stop=(f == 21))
                    return ins
                sc.op('pe', fd, reads=[('actT', k0), 'wd'], writes=[('ps', bd[0]), ('ps', bd[1])])
            if part == 'b1':
                return
            for half in range(2):
                b = bd[half]
                sc.op('dve', (lambda e, half=half, b=b: e.tensor_tensor(Y[:, half * 512:(half + 1) * 512], PS(b),
                                                                        x1[:, half * 512:(half + 1) * 512], ALU.add)),
                      reads=[('ps', b), ('x1', sl, half)], writes=[('yout', sl, half)])
            if last:
                sc.op('act', (lambda e: e.activation(act[:, 0:DM], Y, AF.Square, accum_out=ss[:, 2:3])),
                      reads=[('yout', sl, 0), ('yout', sl, 1)], writes=[('act', fi) for fi in range(2)] + [('ss2', sl)])
                sc.op('act', (lambda e: e.activation(ss[:, 3:4], ss[:, 2:3], AF.Sqrt, bias=epsb[:, 0:1], scale=1.0 / DM)),
                      reads=[('ss2', sl), 'epsb'], writes=[('ss3', sl)])
                sc.op('dve', (lambda e: e.reciprocal(ss[:, 3:4], ss[:, 3:4])), reads=[('ss3', sl)], writes=[('ss3r', sl)])
                sc.op('dve', (lambda e: e.scalar_tensor_tensor(Y, Y, ss[:, 3:4], gfin, ALU.mult, ALU.mult)),
                      reads=[('yout', sl, 0), ('yout', sl, 1), ('ss3r', sl), 'gfin'], writes=[('yout', sl, 0), ('yout', sl, 1)])
            sc.dma('sp', 'd_y%d' % sl, y_out[s][i * 128:(i + 1) * 128, :], Y, reads=[('yout', sl, 0), ('yout', sl, 1)],
                   writes=[('xres', s, i)])

        load(0)
        front1(0, 'a')
        front1(0, 'b')
        front2(0, 'a')
        front2(0, 'b')
        for i in range(32):
            nxt = i + 1 < 32
            if nxt:
                front1(i + 1, 'a')
            back(i, 'a')
            if nxt:
                front1(i + 1, 'b')
            back(i, 'b1')
            if nxt:
                front2(i + 1, 'a')
            back(i, 'b2')
            if nxt:
                front2(i + 1, 'b')
        sc.barrier()

    phases = os.environ.get('KPH', 'ac123')
    nlayers = int(os.environ.get('KLAYERS', '2'))
    if 'a' in phases:
        setup_amask()
    for l in range(nlayers):
        if 'c' in phases:
            setup_cmask(l)
        for s in range(nseq):
            xsrc = x_in[s] if l == 0 else y_out[s]
            phase_p1(s, l, xsrc)
            if '2' in phases:
                phase_p2(s, l)
            if '3' in phases:
                phase_p3(s, l, xsrc, last=(l == 1))
    if dbg:
        cvd = Carver()
        dst_ = [cvd.take([128, 16384], BF16) for _ in range(2)]
        di = 0
        srcs = dict(fmAC=fmAC, fmB=fmB, fmDq=fmDq, fmDk=fmDk, vA=vA, vB=vB, vC=vC, vD=vD, oN=oN, toeA=toeA, toeC=toeC)
        for nm, shp, dt in dbg:
            src_ = srcs[nm]
            fac = 2 if dt == F32 else 1
            if len(shp) == 3:
                views = [(src_.ap()[c], dbg_out[nm].ap()[c], shp[1], shp[2]) for c in range(shp[0])]
            else:
                nj = shp[0] // 128
                jc = max(1, 16384 // shp[1])
                views = [(src_.ap()[j0 * 128:min(nj, j0 + jc) * 128].rearrange("(j p) w -> p j w", p=128),
                          dbg_out[nm].ap()[j0 * 128:min(nj, j0 + jc) * 128].rearrange("(j p) w -> p j w", p=128),
                          128, (min(nj, j0 + jc) - j0) * shp[1]) for j0 in range(0, nj, jc)]
            for (sv, dv, rows, n) in views:
                st = dst_[di % 2][0:rows, 0:n * fac]
                if dt == F32:
                    st = st.bitcast(F32)[:, 0:n]
                if len(shp) == 2:
                    st = st.rearrange("p (j w) -> p j w", w=shp[1])
                sc.dma('sp', 'd_dbgi%d' % (di % 2), st, sv, reads=[('scr', nm)], writes=[('dbgst', di % 2)])
                sc.dma('sp', 'd_dbgo%d' % (di % 2), dv, st, reads=[('dbgst', di % 2)], writes=[('dbg', nm)])
                di += 1
    sc.barrier()
    sc.emit()
    return nc


def prep_shared(inp):
    f = lambda a: np.ascontiguousarray(np.asarray(a, dtype=np.float32))
    w_in = f(inp['w_in'])
    d = {}
    d['wext'] = _gather_cols(w_in, _wext_cols())
    uq = f(inp['d_w_uq'])
    swc = np.array([h * 96 + (dd if dd < 64 else 64 + ((dd - 64) ^ 16)) for h in range(4) for dd in range(96)])
    d['uq'] = np.ascontiguousarray(np.concatenate([uq, uq[:, :, swc]], axis=-1))
    ukv = f(inp['d_w_ukv'])
    kc_ = np.array([h * 128 + dd for h in range(4) for dd in range(64)])
    vc_ = np.array([h * 128 + 64 + dd for h in range(4) for dd in range(64)])
    d['ukv'] = np.ascontiguousarray(np.concatenate([ukv[:, :, kc_], ukv[:, :, vc_]], axis=-1))
    d['wout'] = f(inp['w_out'])
    d['wg'] = f(inp['w_gate'])
    d['wu'] = f(inp['w_up'])
    d['wd'] = f(inp['w_down'])
    d['rowg'] = np.ascontiguousarray(np.concatenate([f(inp['norm_mix']), f(inp['out_gain']), f(inp['norm_ffn']),
                                                     f(inp['final_norm'])[None, :]], axis=0))
    sw = np.array([dd ^ 16 for dd in range(64)])
    colc = np.zeros((2, 128, 8), np.float32)
    bq, bk = f(inp['b_q_gain']), f(inp['b_k_gain'])
    dqg, dkvg = f(inp['d_q_gain']), f(inp['d_kv_gain'])
    for l in range(2):
        colc[l, :, 0] = np.tile(bq[l], 2)
        colc[l, :, 1] = np.tile(bq[l][sw], 2)
        colc[l, :, 2] = np.tile(bk[l], 2)
        colc[l, :, 3] = np.tile(bk[l][sw], 2)
        colc[l, :, 4] = dqg[l][0:128]
        colc[l, :, 5] = dqg[l][128:256]
        colc[l, :, 6] = dkvg[l]
    d['colc'] = colc
    d['t5'] = f(inp['t5_bias'])
    d['rpbf'] = np.ascontiguousarray(f(inp['c_rpb'])[:, :, :, ::-1].reshape(2, 60, 31))
    d.update(_consts())
    return d


_NC_CACHE = {}


def kernel(**inputs):
    sh = prep_shared(inputs)
    xp = np.asarray(inputs['x_prompt'], dtype=np.float32)
    xs = np.asarray(inputs['x_sample'], dtype=np.float32)
    xall = np.concatenate([xp, xs], axis=0)
    nseq = xall.shape[0] // NCORES
    if nseq not in _NC_CACHE:
        _NC_CACHE[nseq] = build(nseq)
    nc = _NC_CACHE[nseq]
    in_maps = []
    for c in range(NCORES):
        m = dict(sh)
        m['x'] = np.ascontiguousarray(xall[c * nseq:(c + 1) * nseq])
        in_maps.append(m)
    res = run_bass_kernel_spmd(nc, in_maps, core_ids=list(range(NCORES)))
    y = np.concatenate([np.asarray(r['y'], dtype=np.float32) for r in res.results], axis=0)
    return (np.ascontiguousarray(y[:xp.shape[0]]), np.ascontiguousarray(y[xp.shape[0]:]))
```
